# Optimizing a Trainium2 kernel written in Bass

```python
import math
import jax, jax.numpy as jnp
from jax import lax
import numpy as np

D_MODEL = 4096
BATCH = 4
SEQ = 2048
DEPTH = 2

PLE_DIM = 256
N_BRANCHES = 3
BRANCH_WIDTH = D_MODEL // 2
GMLP_CHUNK = 128
GMLP_GROUPS = 8
SWA_HEAD_DIM = 64
SWA_Q_HEADS = BRANCH_WIDTH // SWA_HEAD_DIM
SWA_KV_HEADS = 4
SWA_WINDOW = 128
SWA_BLOCK = 128
ROT_DIM = SWA_HEAD_DIM // 4
ROPE_THETA = 500000.0
MLSTM_HEADS = 4
MLSTM_V_DIM = BRANCH_WIDTH // MLSTM_HEADS
MLSTM_QK_DIM = MLSTM_V_DIM // 2
MLSTM_CHUNK = 64
GATE_SOFTCAP = 15.0
NORM_EPS = 1e-6
NEG_INF = -1e30

SPLIT_SIZES = (
    BRANCH_WIDTH, BRANCH_WIDTH, BRANCH_WIDTH,
    SWA_Q_HEADS * SWA_HEAD_DIM, SWA_KV_HEADS * SWA_HEAD_DIM,
    SWA_KV_HEADS * SWA_HEAD_DIM, BRANCH_WIDTH,
    MLSTM_HEADS * MLSTM_QK_DIM, MLSTM_HEADS * MLSTM_QK_DIM,
    MLSTM_HEADS * MLSTM_V_DIM, MLSTM_HEADS, MLSTM_HEADS,
    BRANCH_WIDTH, BRANCH_WIDTH,
    N_BRANCHES * D_MODEL,
)
N_IN = sum(SPLIT_SIZES)

kernel_name = 'hybrid_gmlp_swa_mlstm_block'


def rms_norm(x, g):
    xf = x.astype(jnp.float32)
    y = xf * lax.rsqrt(jnp.mean(xf * xf, axis=-1, keepdims=True) + NORM_EPS)
    return (y * g.astype(jnp.float32)).astype(x.dtype)


def layer_norm(x, g, b):
    xf = x.astype(jnp.float32)
    mu = jnp.mean(xf, axis=-1, keepdims=True)
    var = jnp.mean(jnp.square(xf - mu), axis=-1, keepdims=True)
    y = (xf - mu) * lax.rsqrt(var + NORM_EPS)
    return (y * g.astype(jnp.float32) + b.astype(jnp.float32)).astype(x.dtype)


def softcap(z):
    return GATE_SOFTCAP * jnp.tanh(z / GATE_SOFTCAP)


def apply_partial_rope(t, cos, sin):
    half = ROT_DIM // 2
    t1 = t[..., :half].astype(jnp.float32)
    t2 = t[..., half:ROT_DIM].astype(jnp.float32)
    rot = jnp.concatenate([t1 * cos - t2 * sin, t2 * cos + t1 * sin], axis=-1).astype(t.dtype)
    return jnp.concatenate([rot, t[..., ROT_DIM:]], axis=-1)


def gmlp_branch(u, v, ln_g, ln_b, ws, bs):
    B, S, W = v.shape
    nc = S // GMLP_CHUNK
    u = jax.nn.gelu(u, approximate=False)
    v = layer_norm(jax.nn.gelu(v, approximate=False), ln_g, ln_b)
    vg = v.reshape(B, nc, GMLP_CHUNK, GMLP_GROUPS, W // GMLP_GROUPS)
    w_causal = jnp.tril(ws)
    mixed = jnp.einsum('gts,bnsgc->bntgc', w_causal, vg) + bs.T[None, None, :, :, None]
    return u * mixed.reshape(B, S, W)


def swa_branch(q, k, v, sinks, cos, sin):
    B, S, _ = q.shape
    G = SWA_Q_HEADS // SWA_KV_HEADS
    nb = S // SWA_BLOCK
    q = apply_partial_rope(q.reshape(B, S, SWA_Q_HEADS, SWA_HEAD_DIM), cos, sin)
    k = apply_partial_rope(k.reshape(B, S, SWA_KV_HEADS, SWA_HEAD_DIM), cos, sin)
    v = v.reshape(B, S, SWA_KV_HEADS, SWA_HEAD_DIM)
    qb = q.reshape(B, nb, SWA_BLOCK, SWA_KV_HEADS, G, SWA_HEAD_DIM)

    def band(t):
        tp = jnp.pad(t, ((0, 0), (SWA_BLOCK, 0), (0, 0), (0, 0)))
        tp = tp.reshape(B, nb + 1, SWA_BLOCK, SWA_KV_HEADS, SWA_HEAD_DIM)
        return jnp.concatenate([tp[:, :-1], tp[:, 1:]], axis=2)

    kb, vb = band(k), band(v)
    s = jnp.einsum('bnqhgd,bnkhd->bnhgqk', qb, kb).astype(jnp.float32) * (SWA_HEAD_DIM ** -0.5)
    qi = jnp.arange(SWA_BLOCK)[:, None]
    kj = jnp.arange(2 * SWA_BLOCK)[None, :]
    diff = qi + SWA_BLOCK - kj
    local = (diff >= 0) & (diff < SWA_WINDOW)
    not_pad = (jnp.arange(nb)[:, None, None] > 0) | (kj >= SWA_BLOCK)[None]
    mask = local[None] & not_pad
    s = jnp.where(mask[None, :, None, None], s, NEG_INF)
    sink = sinks.astype(jnp.float32).reshape(SWA_KV_HEADS, G)[None, None, :, :, None, None]
    sink = jnp.broadcast_to(sink, s.shape[:-1] + (1,))
    probs = jax.nn.softmax(jnp.concatenate([s, sink], axis=-1), axis=-1)[..., :-1]
    o = jnp.einsum('bnhgqk,bnkhd->bnqhgd', probs.astype(v.dtype), vb)
    return o.reshape(B, S, SWA_Q_HEADS * SWA_HEAD_DIM)


def mlstm_chunkwise(q, k, v, ig, lf):
    B, S, H, dk = q.shape
    dv = v.shape[-1]
    L = MLSTM_CHUNK
    nc = S // L

    def to_chunks(t):
        if t.ndim == 4:
            return t.reshape(B, nc, L, H, t.shape[-1]).transpose(1, 0, 3, 2, 4)
        return t.reshape(B, nc, L, H).transpose(1, 0, 3, 2)

    causal = jnp.tril(jnp.ones((L, L), dtype=bool))

    def step(carry, xs):
        C, n, m = carry
        qc, kc, vc, ic, fc = xs
        b = jnp.cumsum(fc, axis=-1)
        log_d = b[..., :, None] - b[..., None, :] + ic[..., None, :]
        log_d = jnp.where(causal, log_d, NEG_INF)
        m_inter = b + m[..., None]
        m_t = jnp.maximum(m_inter, jnp.max(log_d, axis=-1))
        s = jnp.einsum('bhqd,bhkd->bhqk', qc, kc) * jnp.exp(log_d - m_t[..., None])
        a = jnp.exp(m_inter - m_t)
        num = jnp.einsum('bhqk,bhkv->bhqv', s, vc) + a[..., None] * jnp.einsum('bhqd,bhdv->bhqv', qc, C)
        den = jnp.sum(s, axis=-1) + a * jnp.einsum('bhqd,bhd->bhq', qc, n)
        h = num / jnp.maximum(jnp.abs(den), jnp.exp(-m_t))[..., None]
        g = b[..., -1]
        w_log = g[..., None] - b + ic
        m_new = jnp.maximum(g + m, jnp.max(w_log, axis=-1))
        w = jnp.exp(w_log - m_new[..., None])
        decay = jnp.exp(g + m - m_new)
        C = decay[..., None, None] * C + jnp.einsum('bhl,bhld,bhlv->bhdv', w, kc, vc)
        n = decay[..., None] * n + jnp.einsum('bhl,bhld->bhd', w, kc)
        return (C, n, m_new), h

    init = (jnp.zeros((B, H, dk, dv), jnp.float32), jnp.zeros((B, H, dk), jnp.float32),
            jnp.zeros((B, H), jnp.float32))
    _, hs = lax.scan(step, init, (to_chunks(q), to_chunks(k), to_chunks(v), to_chunks(ig), to_chunks(lf)))
    return hs.transpose(1, 0, 3, 2, 4).reshape(B, S, H, dv)


def mlstm_branch(q, k, v, i_pre, f_pre, o_pre, ib, fb, norm_g):
    B, S, _ = q.shape
    qf = q.reshape(B, S, MLSTM_HEADS, MLSTM_QK_DIM).astype(jnp.float32) * (MLSTM_QK_DIM ** -0.5)
    kf = k.reshape(B, S, MLSTM_HEADS, MLSTM_QK_DIM).astype(jnp.float32)
    vf = v.reshape(B, S, MLSTM_HEADS, MLSTM_V_DIM).astype(jnp.float32)
    ig = softcap(i_pre.astype(jnp.float32) + ib.astype(jnp.float32))
    lf = jax.nn.log_sigmoid(softcap(f_pre.astype(jnp.float32) + fb.astype(jnp.float32)))
    h = mlstm_chunkwise(qf, kf, vf, ig, lf)
    h = h * lax.rsqrt(jnp.mean(h * h, axis=-1, keepdims=True) + NORM_EPS)
    h = h.reshape(B, S, MLSTM_HEADS * MLSTM_V_DIM) * norm_g.astype(jnp.float32)
    return h.astype(o_pre.dtype) * jax.nn.sigmoid(o_pre)


def split_columns(proj):
    idx, acc = [], 0
    for sz in SPLIT_SIZES[:-1]:
        acc += sz
        idx.append(acc)
    return jnp.split(proj, idx, axis=-1)


def hybrid_layer(x, p_l, cos, sin, norm_pre, w_in, gmlp_ln_g, gmlp_ln_b, gmlp_ws, gmlp_bs,
                 attn_sinks, mlstm_ib, mlstm_fb, mlstm_norm_g, w_branch, w_out, norm_post,
                 ple_proj, ple_norm, ple_gate):
    B, S, D = x.shape
    h = rms_norm(x, norm_pre)
    proj = jnp.einsum('bsd,dn->bsn', h, w_in)
    (a_u, a_v, a_z, b_q, b_k, b_v, b_z,
     c_q, c_k, c_v, c_i, c_f, c_o, c_z, gates) = split_columns(proj)
    y_a = gmlp_branch(a_u, a_v, gmlp_ln_g, gmlp_ln_b, gmlp_ws, gmlp_bs) * jax.nn.silu(a_z)
    y_b = swa_branch(b_q, b_k, b_v, attn_sinks, cos, sin) * jax.nn.silu(b_z)
    y_c = mlstm_branch(c_q, c_k, c_v, c_i, c_f, c_o, mlstm_ib, mlstm_fb, mlstm_norm_g) * jax.nn.silu(c_z)
    ys = jnp.stack([y_a, y_b, y_c], axis=2)
    br = jnp.einsum('bsjc,jcd->bsjd', ys, w_branch)
    g = jax.nn.sigmoid(gates.reshape(B, S, N_BRANCHES, D))
    mixed = jnp.sum(g * br, axis=2)
    x = x + rms_norm(jnp.einsum('bsd,de->bse', mixed, w_out), norm_post)
    e = rms_norm(jnp.einsum('bsp,pd->bsd', p_l, ple_proj), ple_norm)
    x = x + jax.nn.sigmoid(jnp.einsum('bsd,de->bse', x, ple_gate)) * e
    return x


def setup_inputs(seed: int = 0) -> dict:
    key = jax.random.key(seed)
    ks = jax.random.split(key, 20)

    def nrm(k, shape, scale):
        return jax.random.normal(k, shape, jnp.float32) * scale

    x = nrm(ks[0], (BATCH, SEQ, D_MODEL), 1.0)
    p = nrm(ks[1], (DEPTH, BATCH, SEQ, PLE_DIM), 1.0)
    start = jax.random.randint(ks[2], (BATCH, 1), 0, 4096, dtype=jnp.int32)
    positions = start + jnp.arange(SEQ, dtype=jnp.int32)[None, :]
    norm_pre = 1.0 + nrm(ks[3], (DEPTH, D_MODEL), 0.05)
    w_in = nrm(ks[4], (DEPTH, D_MODEL, N_IN), D_MODEL ** -0.5)
    gmlp_ln_g = 1.0 + nrm(ks[5], (DEPTH, BRANCH_WIDTH), 0.05)
    gmlp_ln_b = nrm(ks[6], (DEPTH, BRANCH_WIDTH), 0.02)
    gmlp_ws = nrm(ks[7], (DEPTH, GMLP_GROUPS, GMLP_CHUNK, GMLP_CHUNK), GMLP_CHUNK ** -0.5)
    gmlp_bs = 1.0 + nrm(ks[8], (DEPTH, GMLP_GROUPS, GMLP_CHUNK), 0.05)
    attn_sinks = nrm(ks[9], (DEPTH, SWA_Q_HEADS), 0.5)
    mlstm_ib = nrm(ks[10], (DEPTH, MLSTM_HEADS), 0.1)
    mlstm_fb = jnp.linspace(3.0, 6.0, MLSTM_HEADS, dtype=jnp.float32)[None, :] + nrm(ks[11], (DEPTH, MLSTM_HEADS), 0.1)
    mlstm_norm_g = 1.0 + nrm(ks[12], (DEPTH, BRANCH_WIDTH), 0.05)
    w_branch = nrm(ks[13], (DEPTH, N_BRANCHES, BRANCH_WIDTH, D_MODEL), BRANCH_WIDTH ** -0.5)
    w_out = nrm(ks[14], (DEPTH, D_MODEL, D_MODEL), D_MODEL ** -0.5)
    norm_post = 1.0 + nrm(ks[15], (DEPTH, D_MODEL), 0.05)
    ple_proj = nrm(ks[16], (DEPTH, PLE_DIM, D_MODEL), PLE_DIM ** -0.5)
    ple_norm = 1.0 + nrm(ks[17], (DEPTH, D_MODEL), 0.05)
    ple_gate = nrm(ks[18], (DEPTH, D_MODEL, D_MODEL), D_MODEL ** -0.5)
    return {'x': x, 'p': p, 'positions': positions, 'norm_pre': norm_pre, 'w_in': w_in,
            'gmlp_ln_g': gmlp_ln_g, 'gmlp_ln_b': gmlp_ln_b, 'gmlp_ws': gmlp_ws, 'gmlp_bs': gmlp_bs,
            'attn_sinks': attn_sinks, 'mlstm_ib': mlstm_ib, 'mlstm_fb': mlstm_fb,
            'mlstm_norm_g': mlstm_norm_g, 'w_branch': w_branch, 'w_out': w_out,
            'norm_post': norm_post, 'ple_proj': ple_proj, 'ple_norm': ple_norm, 'ple_gate': ple_gate}


def reference(x, p, positions, norm_pre, w_in, gmlp_ln_g, gmlp_ln_b, gmlp_ws, gmlp_bs,
              attn_sinks, mlstm_ib, mlstm_fb, mlstm_norm_g, w_branch, w_out, norm_post,
              ple_proj, ple_norm, ple_gate):
    inv_freq = ROPE_THETA ** (-jnp.arange(0, ROT_DIM, 2, dtype=jnp.float32) / ROT_DIM)
    ang = positions.astype(jnp.float32)[..., None] * inv_freq
    cos = jnp.cos(ang)[:, :, None, :]
    sin = jnp.sin(ang)[:, :, None, :]
    for i in range(DEPTH):
        x = hybrid_layer(x, p[i], cos, sin, norm_pre[i], w_in[i], gmlp_ln_g[i], gmlp_ln_b[i],
                         gmlp_ws[i], gmlp_bs[i], attn_sinks[i], mlstm_ib[i], mlstm_fb[i],
                         mlstm_norm_g[i], w_branch[i], w_out[i], norm_post[i],
                         ple_proj[i], ple_norm[i], ple_gate[i])
    return x
```

```python
import math
from contextlib import ExitStack
import numpy as np
import ml_dtypes
import concourse.bass as bass
import concourse.mybir as mybir
from concourse.bass_utils import run_bass_kernel_spmd

F32 = mybir.dt.float32
BF16 = mybir.dt.bfloat16
I32 = mybir.dt.int32
AF = mybir.ActivationFunctionType
ALU = mybir.AluOpType
AX = mybir.AxisListType
ENGS = ("pe", "act", "dve", "pool", "sp")

D = 4096
T = 512
NIN = 31240
OFF = dict(a_u=0, a_v=2048, a_z=4096, b_q=6144, b_k=8192, b_v=8448, b_z=8704, c_q=10752,
           c_k=11776, c_v=12800, c_i=14848, c_f=14852, c_o=14856, c_z=16904, g=18952)
EPS = 1e-6
NCB = 1152
NCF = 900
ARENA = 29000


class Buf:
    __slots__ = ("name", "w", "r", "dsem", "dcnt")

    def __init__(self, name):
        self.name = name
        self.w = None
        self.r = []
        self.dsem = None
        self.dcnt = 0


class Prog:
    def __init__(self, nc):
        self.nc = nc
        self.ops = {e: [] for e in ENGS}
        self.cnt = {e: 0 for e in ENGS}
        self.sems = {}
        self.sem_keys = list(ENGS)
        self.ndsem = 0
        self.seen = {e: {} for e in ENGS}
        self.dcounts = {}
        self.name2sem = {}

    def _dsem(self, buf):
        if buf.dsem is None:
            if buf.name not in self.name2sem:
                self.name2sem[buf.name] = "d%d" % self.ndsem
                self.ndsem += 1
                self.sem_keys.append(self.name2sem[buf.name])
            buf.dsem = self.name2sem[buf.name]
        return buf.dsem

    def _waits(self, eng, reads, writes):
        need = {}

        def add(tok):
            if tok is None:
                return
            k, v = tok
            if k == eng and eng == "pe":
                return
            if need.get(k, 0) < v:
                need[k] = v
        for b in reads:
            add(b.w)
        for b in writes:
            add(b.w)
            for t in b.r:
                add(t)
        out = []
        seen = self.seen[eng]
        for k, v in need.items():
            if seen.get(k, 0) >= v:
                continue
            seen[k] = v
            out.append((k, v))
        return out

    def op(self, eng, fn, reads=(), writes=(), sig=True):
        waits = self._waits(eng, reads, writes)
        if sig:
            self.cnt[eng] += 1
            tok = (eng, self.cnt[eng])
        else:
            tok = (eng, self.cnt[eng] + 1)
        self.ops[eng].append((waits, fn, (eng, 1) if sig else None))
        for b in reads:
            if len(b.r) > 48:
                mx = {}
                for k_, v_ in b.r:
                    if mx.get(k_, 0) < v_:
                        mx[k_] = v_
                b.r = list(mx.items())
            b.r.append(tok)
        for b in writes:
            b.w = tok
            b.r = []
        return tok

    def dma(self, queue, fn, reads=(), writes=(), dbuf=None):
        if dbuf is None:
            dbuf = writes[0]
        waits = self._waits(queue, reads, writes)
        key = self._dsem(dbuf)
        self.dcounts[key] = self.dcounts.get(key, 0) + 16
        tok = (key, self.dcounts[key])
        self.ops[queue].append((waits, fn, (key, 16)))
        for b in reads:
            b.r.append(tok)
        for b in writes:
            b.w = tok
            b.r = []
        return tok

    def wait_all(self, eng, bufs):
        waits = self._waits(eng, bufs, ())
        self.ops[eng].append((waits, None, None))

    def barrier(self):
        toks = {e: self.cnt[e] for e in ENGS if self.cnt[e] > 0}
        toks.update(self.dcounts)
        for e in ENGS:
            waits = []
            for k, v in toks.items():
                if k == e or self.seen[e].get(k, 0) >= v:
                    continue
                self.seen[e][k] = v
                waits.append((k, v))
            if waits:
                self.ops[e].append((waits, None, None))

    def build(self, stack):
        nc = self.nc
        for k in self.sem_keys:
            self.sems[k] = stack.enter_context(nc.semaphore("s_" + k))
        block = stack.enter_context(nc.Block())
        sems = self.sems

        def run(engname):
            def body(e):
                for waits, fn, inc in self.ops[engname]:
                    for k, v in waits:
                        e.wait_ge(sems[k], v)
                    if fn is not None:
                        ins = fn(e)
                        if inc is not None:
                            ins.then_inc(sems[inc[0]], inc[1])
            return body
        block.tensor(run("pe"))
        block.scalar(run("act"))
        block.vector(run("dve"))
        block.gpsimd(run("pool"))
        block.sync(run("sp"))


def make_consts():
    cb = np.zeros((128, NCB), np.float32)
    cf = np.zeros((128, NCF), np.float32)
    idx = np.arange(128)
    cb[:, 0:128] = np.eye(128)
    cb[:, 128:256] = 1.0
    cb[:, 256:384] = (idx[:, None] <= idx[None, :])
    cb[:, 384:512] = (idx[:, None] > idx[None, :])
    R = np.zeros((128, 128), np.float32)
    for p_ in range(128):
        d = p_ % 64
        if d < 8:
            R[p_ + 8, p_] = -1.0
        elif d < 16:
            R[p_ - 8, p_] = 1.0
    cb[:, 512:640] = R
    for half in range(2):
        Dm = np.zeros((128, 128), np.float32)
        for p_ in range(128):
            Dm[half * 64 + p_ % 64, p_] = 1.0
        cb[:, 640 + half * 128:768 + half * 128] = Dm
        cb[:, 896 + half * 128:1024 + half * 128] = R @ Dm
    cf[:, 0:128] = np.eye(128)
    cf[:, 128:256] = 1.0
    cf[:, 256:384] = np.where(idx[:, None] <= idx[None, :], 0.0, -30000.0)
    inv = (500000.0 ** (-(np.arange(0, 16, 2, dtype=np.float32)) / 16.0)).astype(np.float32)
    for p_ in range(128):
        d = p_ % 64
        cf[p_, 384] = inv[d % 8] if d < 16 else 0.0
    for h in range(4):
        cf[h, 385 + h * 128:385 + (h + 1) * 128] = 1.0
    return cb.astype(ml_dtypes.bfloat16), cf


class _Stop(Exception):
    pass


def build_program(S_core=2048, n_layers=2, n_tiles=None, debug=False, stop=None):
    if n_tiles is None:
        n_tiles = S_core // T
    nc = bass.Bass("TRN2", target_bir_lowering=False)

    def din(name, shape, dt=F32):
        return nc.dram_tensor(name, list(shape), dt, kind="ExternalInput").ap()
    x_d = din("x", [S_core, D])
    p_d = din("p", [2, S_core, 256])
    pos_d = din("pos", [1, S_core], I32)
    norm_pre = din("norm_pre", [2, D])
    w_in = din("w_in", [2, D, NIN])
    ln_g = din("gmlp_ln_g", [2, 2048])
    ln_b = din("gmlp_ln_b", [2, 2048])
    wsT_d = din("gmlp_wsT", [2, 8, 128, 128])
    bs_d = din("gmlp_bs", [2, 1024])
    sinks_d = din("attn_sinks", [2, 32])
    ib_d = din("mlstm_ib", [2, 4])
    fb_d = din("mlstm_fb", [2, 4])
    mng_d = din("mlstm_norm_g", [2, 2048])
    w_branch = din("w_branch", [2, 3, 2048, D])
    w_out = din("w_out", [2, D, D])
    norm_post = din("norm_post", [2, D])
    ple_proj = din("ple_proj", [2, 256, D])
    ple_norm = din("ple_norm", [2, D])
    ple_gate = din("ple_gate", [2, D, D])
    cb_d = din("cst_bf", [128, NCB], BF16)
    cf_d = din("cst_f", [128, NCF])
    out_d = nc.dram_tensor("out", [S_core, D], F32, kind="ExternalOutput").ap()
    x1_d = nc.dram_tensor("x1_scr", [S_core, D], F32).ap()
    mT_d = nc.dram_tensor("mT_scr", [4096, T], BF16).ap()
    o_d = nc.dram_tensor("o_scr", [T, D], F32).ap()
    xm_d = nc.dram_tensor("xm_scr", [T, D], F32).ap()
    if debug:
        dbg_d = nc.dram_tensor("dbg", [3, 2048, T], F32, kind="ExternalOutput").ap()

    P = Prog(nc)
    st = ExitStack()
    with st:
        def _full(t, shape):
            return t[:, :] if len(shape) == 2 else t[:, :, :]

        def sb(name, shape, dt=F32):
            return _full(st.enter_context(nc.sbuf_tensor(name, list(shape), dt)), shape)

        def psum(name, shape, dt=F32):
            return _full(st.enter_context(nc.psum_tensor(name, list(shape), dt)), shape)

        cb = sb("cb", [128, NCB], BF16)
        cf = sb("cf", [128, NCF])
        identb, onesb = cb[:, 0:128], cb[:, 128:256]
        mask2 = cb[:, 256:512]
        RrotT = cb[:, 512:640]
        DupT = [cb[:, 640:768], cb[:, 768:896]]
        DupRT = [cb[:, 896:1024], cb[:, 1024:1152]]
        identf, onesf, mb128 = cf[:, 0:128], cf[:, 128:256], cf[:, 256:384]
        invf = cf[:, 384:385]
        b_cst = Buf("cst")
        wt = [sb("wt%d" % i, [128, 8192], BF16) for i in range(3)]
        b_wt = [Buf("wt%d" % i) for i in range(3)]
        Cst = sb("Cst", [128, 8, 512])
        Cbf = sb("Cbf", [128, 8, 512], BF16)
        nst = sb("nst", [128, 8])
        nbf = sb("nbf", [128, 8], BF16)
        mst = sb("mst", [4, 1])
        b_C = [Buf("C%d" % i) for i in range(4)]
        b_m = Buf("mst")
        kTd = [sb("kTd%d" % g, [128, 640], BF16) for g in range(4)]
        b_kTd = [Buf("kTd%d" % g) for g in range(4)]
        vdup = [sb("vdup%d" % i, [128, 4, 128], BF16) for i in range(5)]
        b_vdup = [Buf("vdup%d" % i) for i in range(5)]
        cosF = sb("cosF", [128, T])
        sinF = sb("sinF", [128, T])
        b_cs = Buf("cossin")
        small = sb("small", [128, 256])
        b_small = {}
        arena = sb("arena", [128, ARENA])

        def smallbuf(name):
            if name not in b_small:
                b_small[name] = Buf("sm_" + name)
            return b_small[name]

        psG = [psum("psG%d" % i, [128, 512]) for i in range(2)]
        b_psG = [Buf("psG%d" % i) for i in range(2)]
        psT = [psum("psT%d" % i, [128, 1024], BF16) for i in range(2)]
        b_psT = [Buf("psT0"), Buf("psT1")]
        pg = [psum("pg%d" % i, [128, 512]) for i in range(4)]
        b_pg = [Buf("pg%d" % i) for i in range(4)]

        top = [0]

        def alloc(n32, dt=F32):
            a = top[0]
            top[0] += n32
            assert top[0] <= ARENA, ("arena overflow", top[0])
            v = arena[:, a:a + n32]
            return v if dt == F32 else v.bitcast(dt)

        def ACT(out, in_, func, reads, writes, **kw):
            P.op("act", lambda e: e.activation(out, in_, func, **kw), reads, writes)

        def DV(name, *args, reads, writes, **kw):
            P.op("dve", lambda e: getattr(e, name)(*args, **kw), reads, writes)

        def PL(name, *args, reads, writes, **kw):
            P.op("pool", lambda e: getattr(e, name)(*args, **kw), reads, writes)

        def MM(out, lhsT, rhs, start, stop, reads, writes, sig=None):
            if sig is None:
                sig = stop
            P.op("pe", lambda e: e.matmul(out, lhsT, rhs, start=start, stop=stop), reads, writes, sig)

        def TR(out, in_, ident, reads, writes, sig=True):
            P.op("pe", lambda e: e.transpose(out, in_, ident), reads, writes, sig)

        def DMA(q, out, in_, reads, writes, dbuf=None, slow=False):
            if slow:
                P.dma(q, lambda e: e.dma_start(out=out, in_=in_, allow_slow_non_contiguous=True), reads, writes, dbuf)
            else:
                P.dma(q, lambda e: e.dma_start(out=out, in_=in_), reads, writes, dbuf)

        ring = {"i": 0, "pinned": set()}

        def wload(srcs, K, ncols, pin=False):
            while ring["i"] % 3 in ring["pinned"]:
                ring["i"] += 1
            s = ring["i"] % 3
            ring["i"] += 1
            if pin:
                ring["pinned"].add(s)
            kc = K // 128
            view = wt[s][:, 0:kc * ncols].rearrange("p (k n) -> p k n", n=ncols)
            col = 0
            for ap in srcs:
                n_i = ap.shape[1]
                DMA("pool", view[:, :, col:col + n_i], ap.rearrange("(k p) n -> p k n", p=128),
                    reads=[], writes=[b_wt[s]])
                col += n_i
            return view, b_wt[s], s

        gcount = [0]

        def gemm_F(view, wb, kc, ncols, rhs_fn, rhs_bufs, consume, cw=128):
            for c in range((ncols + cw - 1) // cw):
                m = min(cw, ncols - c * cw)
                pi = gcount[0] % 2
                gcount[0] += 1
                for k in range(kc):
                    MM(psG[pi][0:m, :], view[:, k, c * cw:c * cw + m], rhs_fn(k), k == 0, k == kc - 1,
                       reads=[wb] + rhs_bufs, writes=[b_psG[pi]])
                consume(c, psG[pi], b_psG[pi])

        def gemm_T(view, wb, kc, ncols, lhs_fn, lhs_bufs, consume):
            for rt in range(4):
                pi = gcount[0] % 2
                gcount[0] += 1
                for k in range(kc):
                    MM(psG[pi][:, 0:ncols], lhs_fn(k, rt), view[:, k, 0:ncols], k == 0, k == kc - 1,
                       reads=[wb] + lhs_bufs, writes=[b_psG[pi]])
                consume(rt, psG[pi], b_psG[pi])

        tcount = [0]

        def transposes(src_fn, n, dst_fn, src_bufs, dst_bufs, scale_fn=None):
            i = 0
            while i < n:
                cnt = min(4, n - i)
                hb = tcount[0] % 2
                tcount[0] += 1
                base = 0
                psTb = psT[hb]
                for j in range(cnt):
                    TR(psTb[:, base + j * 128:base + (j + 1) * 128], src_fn(i + j), identb,
                       reads=src_bufs + [b_cst], writes=[b_psT[hb]], sig=(j == cnt - 1))
                src = psTb[:, base:base + cnt * 128].rearrange("p (c t) -> p c t", t=128)
                if tcount[0] % 2 == 0:
                    P.op("act", lambda e, o=dst_fn(i, cnt), s_=src: e.copy(o, s_), [b_psT[hb]], dst_bufs)
                else:
                    DV("tensor_copy", dst_fn(i, cnt), src, reads=[b_psT[hb]], writes=dst_bufs)
                i += cnt

        DMA("sp", cb[:, :], cb_d[:, :], [], [b_cst])
        DMA("sp", cf[:, :], cf_d[:, :], [], [b_cst])

        b_x = [Buf("x_l0"), Buf("x_l1"), Buf("x_l2")]
        b_mTd, b_od, b_xmd = Buf("mTd"), Buf("od"), Buf("xmd")
        b_dbg = Buf("dbg")
        xs = [x_d, x1_d, out_d] if n_layers == 2 else [x_d, out_d]

        def dump_dbg(yT, b_yT, nj, mark):
            top[0] = mark
            dtmp = alloc(512)
            b_dtmp = Buf("dtmp")
            for j in range(nj):
                for cc in range(16):
                    DV("tensor_copy", dtmp, yT[j][:, cc, :], reads=[b_yT[j][cc]], writes=[b_dtmp])
                    DMA("sp", dbg_d[j, cc * 128:(cc + 1) * 128, :], dtmp, [b_dtmp], [b_dbg])
            P.barrier()

        def _layers():
          for l in range(n_layers):
            x_in, x_out = xs[l], xs[l + 1]
            b_xin, b_xout = b_x[l], b_x[l + 1]
            for ti in range(n_tiles):
                tok0 = ti * T
                first_tile = (ti == 0)
                P.barrier()
                top[0] = 0
                if first_tile:
                    DV("memset", Cst[:, :, :], 0.0, reads=[], writes=b_C)
                    DV("memset", Cbf[:, :, :], 0.0, reads=[], writes=b_C)
                    DV("memset", nst[:, :], 0.0, reads=[], writes=b_C)
                    DV("memset", nbf[:, :], 0.0, reads=[], writes=b_C)
                    DV("memset", mst[:, :], 0.0, reads=[], writes=[b_m])
                hT = alloc(8192, BF16).rearrange("p (k t) -> p k t", t=T)
                b_hT = Buf("hT")
                mark_p0 = top[0]
                gbuf = alloc(4096)
                b_gbuf = Buf("gbuf")
                xrow = alloc(4096)
                b_xrow = Buf("xrow")
                xsb = alloc(2048, BF16)
                b_xsb = Buf("xsb")
                sm_ss = small[:, 0:1]
                sm_rstd = small[:, 1:2]
                b_ss = smallbuf("ss")
                DMA("sp", gbuf, norm_pre[l:l + 1, :].broadcast_to([128, D]), [], [b_gbuf])
                for rt in range(4):
                    r0 = tok0 + rt * 128
                    DMA("sp", xrow, x_in[r0:r0 + 128, :], [b_xin], [b_xrow])
                    ACT(xsb, xrow, AF.Square, [b_xrow], [b_xsb, b_ss], accum_out=sm_ss)
                    DV("tensor_scalar", small[:, 2:3], sm_ss, 1.0 / D, EPS, ALU.mult, ALU.add, reads=[b_ss], writes=[b_ss])
                    ACT(sm_rstd, small[:, 2:3], AF.Sqrt, [b_ss], [b_ss]); DV("reciprocal", sm_rstd, sm_rstd, reads=[b_ss], writes=[b_ss])
                    DV("scalar_tensor_tensor", xsb, xrow, sm_rstd, gbuf, ALU.mult, ALU.mult,
                       reads=[b_xrow, b_ss, b_gbuf], writes=[b_xsb])
                    transposes(lambda i: xsb[:, i * 128:(i + 1) * 128], 32,
                               lambda i0, cnt, rt=rt: hT[:, i0:i0 + cnt, rt * 128:(rt + 1) * 128],
                               [b_xsb], [b_hT])
                P.barrier()
                if stop == "p0":
                    raise _Stop()
                top[0] = mark_p0
                hT_fn = lambda k: hT[:, k, :]
                hT_lhs = lambda k, rt: hT[:, k, rt * 128:(rt + 1) * 128]
                W = w_in[l]

                def wcols(c0, n):
                    return W[:, c0:c0 + n]

                yT = [None, None, None]
                b_yT = [None, None, None]
                yT[0] = alloc(4096, BF16).rearrange("p (k t) -> p k t", t=T)
                b_yT[0] = [Buf("yTa%d" % i) for i in range(16)]
                mark_a = top[0]
                GB = alloc(4096).rearrange("p (a n) -> p a n", a=2)
                b_GB = Buf("GB")
                gv = [alloc(1024, BF16) for _ in range(4)]
                b_gv = [Buf("gv%d" % i) for i in range(4)]
                wsraw = alloc(1024).rearrange("p (g t) -> p g t", t=128)
                wsTm = alloc(512, BF16).rearrange("p (g t) -> p g t", t=128)
                b_ws = Buf("ws")
                bsrow = alloc(1024)
                b_bs = Buf("bsrow")
                gu = [alloc(256, BF16) for _ in range(2)]
                sz = [alloc(256, BF16) for _ in range(2)]
                b_gu = [Buf("gu0"), Buf("gu1")]
                b_sz = [Buf("sz0"), Buf("sz1")]
                atmp = alloc(512)
                b_atmp = Buf("atmp")
                junkA = alloc(128, BF16)
                b_junk = Buf("junk")
                s1 = small[:, 8:40]
                s2 = small[:, 40:72]
                b_s12 = smallbuf("s12")
                stA = small[:, 72:96]
                b_stA = smallbuf("stA")
                DMA("sp", GB[:, 0, :], ln_g[l:l + 1, :].broadcast_to([128, 2048]), [], [b_GB])
                DMA("sp", GB[:, 1, :], ln_b[l:l + 1, :].broadcast_to([128, 2048]), [], [b_GB])
                DMA("sp", wsraw, wsT_d[l].rearrange("g s t -> s g t"), [], [b_ws])
                DMA("sp", bsrow[0:1, :], bs_d[l:l + 1, :], [], [b_bs])
                for g in range(8):
                    DV("tensor_tensor", wsTm[:, g, :], wsraw[:, g, :], mask2[:, 0:128], ALU.mult,
                       reads=[b_ws, b_cst], writes=[b_ws])
                for blk in range(8):
                    view, wb, _ = wload([wcols(OFF["a_v"] + blk * 256, 256)], D, 256)

                    def cons_v(rt, ps, bps, blk=blk):
                        sl = gv[rt][:, blk * 256:(blk + 1) * 256]
                        ACT(sl, ps[:, 0:256], AF.Gelu, [bps], [b_gv[rt]])
                        DV("tensor_reduce", s1[:, rt * 8 + blk:rt * 8 + blk + 1], sl, AX.X, ALU.add,
                           reads=[b_gv[rt]], writes=[b_s12])
                        ACT(junkA[:, 0:256], sl, AF.Square, [b_gv[rt]], [b_junk, b_s12],
                            accum_out=s2[:, rt * 8 + blk:rt * 8 + blk + 1])
                    gemm_T(view, wb, 32, 256, hT_lhs, [b_hT], cons_v)
                DV("tensor_reduce", stA[:, 0:4], s1.rearrange("p (r b) -> p r b", b=8), AX.X, ALU.add,
                   reads=[b_s12], writes=[b_stA])
                DV("tensor_reduce", stA[:, 4:8], s2.rearrange("p (r b) -> p r b", b=8), AX.X, ALU.add,
                   reads=[b_s12], writes=[b_stA])
                DV("tensor_scalar", stA[:, 8:12], stA[:, 0:4], 1.0 / 2048, None, ALU.mult, reads=[b_stA], writes=[b_stA])
                DV("tensor_tensor", stA[:, 12:16], stA[:, 8:12], stA[:, 8:12], ALU.mult, reads=[b_stA], writes=[b_stA])
                DV("scalar_tensor_tensor", stA[:, 12:16], stA[:, 4:8], 1.0 / 2048, stA[:, 12:16], ALU.mult, ALU.subtract,
                   reads=[b_stA], writes=[b_stA])
                DV("tensor_scalar", stA[:, 12:16], stA[:, 12:16], 0.0, EPS, ALU.max, ALU.add, reads=[b_stA], writes=[b_stA]); ACT(stA[:, 16:20], stA[:, 12:16], AF.Sqrt, [b_stA], [b_stA]); DV("reciprocal", stA[:, 16:20], stA[:, 16:20], reads=[b_stA], writes=[b_stA])
                DV("scalar_tensor_tensor", stA[:, 20:24], stA[:, 8:12], -1.0, stA[:, 16:20], ALU.mult, ALU.mult,
                   reads=[b_stA], writes=[b_stA])
                for rt in range(4):
                    DV("tensor_scalar", gv[rt], gv[rt], stA[:, 16 + rt:17 + rt], stA[:, 20 + rt:21 + rt],
                       ALU.mult, ALU.add, reads=[b_gv[rt], b_stA], writes=[b_gv[rt]])
                    DV("tensor_tensor", gv[rt], gv[rt], GB[:, 0, :], ALU.mult, reads=[b_gv[rt], b_GB], writes=[b_gv[rt]])
                    DV("tensor_tensor", gv[rt], gv[rt], GB[:, 1, :], ALU.add, reads=[b_gv[rt], b_GB], writes=[b_gv[rt]])
                for blk in range(8):
                    view, wb, _ = wload([wcols(OFF["a_u"] + blk * 256, 256)], D, 256)

                    def cons_u(c, ps, bps):
                        ACT(gu[c], ps, AF.Gelu, [bps], [b_gu[c]])
                    gemm_F(view, wb, 32, 256, hT_fn, [b_hT], cons_u)
                    view, wb, _ = wload([wcols(OFF["a_z"] + blk * 256, 256)], D, 256)

                    def cons_z(c, ps, bps):
                        ACT(sz[c], ps, AF.Silu, [bps], [b_sz[c]])
                    gemm_F(view, wb, 32, 256, hT_fn, [b_hT], cons_z)
                    for c in range(2):
                        cc = blk * 2 + c
                        for rt in range(4):
                            MM(pg[0][:, rt * 128:(rt + 1) * 128], gv[rt][:, cc * 128:(cc + 1) * 128], wsTm[:, blk, :],
                               True, False, reads=[b_gv[rt], b_ws], writes=[b_pg[0]], sig=False)
                            MM(pg[0][:, rt * 128:(rt + 1) * 128], onesf[0:1, 0:128], bsrow[0:1, blk * 128:(blk + 1) * 128],
                               False, True, reads=[b_bs, b_cst], writes=[b_pg[0]], sig=(rt == 3))
                        DV("tensor_tensor", atmp, pg[0], gu[c], ALU.mult, reads=[b_pg[0], b_gu[c]], writes=[b_atmp])
                        DV("tensor_tensor", yT[0][:, cc, :], atmp, sz[c], ALU.mult, reads=[b_atmp, b_sz[c]],
                           writes=[b_yT[0][cc]])
                P.barrier()
                if stop == "a":
                    dump_dbg(yT, b_yT, 1, mark_a)
                    raise _Stop()
                top[0] = mark_a

                yT[1] = alloc(4096, BF16).rearrange("p (k t) -> p k t", t=T)
                b_yT[1] = [Buf("yTb%d" % i) for i in range(16)]
                mark_b = top[0]
                posb = alloc(512).bitcast(I32)
                angf = alloc(512)
                b_pos = Buf("pos")
                esink = small[:, 96:128]
                b_esink = smallbuf("esink")
                kraw = [alloc(256, BF16) for _ in range(2)]
                b_kraw = [Buf("kraw0"), Buf("kraw1")]
                qraw = [alloc(256, BF16) for _ in range(2)]
                b_qraw = [Buf("qraw0"), Buf("qraw1")]
                qT = [alloc(256, BF16) for _ in range(2)]
                b_qT = [Buf("qT0"), Buf("qT1")]
                t1 = alloc(512)
                t2 = alloc(512)
                b_t1, b_t2 = Buf("t1"), Buf("t2")
                vtmp = alloc(128, BF16)
                b_vtmp = Buf("vtmp")
                PT = [alloc(128, BF16) for _ in range(2)]
                b_PT = [Buf("PT0"), Buf("PT1")]
                rtmp = alloc(512)
                b_rtmp = Buf("rtmp")
                szb = [alloc(256, BF16) for _ in range(2)]
                b_szb = [Buf("szb0"), Buf("szb1")]
                DMA("sp", posb, pos_d[0:1, tok0:tok0 + T].broadcast_to([128, T]), [], [b_pos])
                DV("tensor_copy", angf, posb, reads=[b_pos], writes=[b_pos])
                DV("tensor_scalar", angf, angf, invf, None, ALU.mult, reads=[b_pos, b_cst], writes=[b_pos])
                C1 = 6.28125
                C2 = 2 * math.pi - 6.28125
                kint = posb
                DV("tensor_scalar", t1, angf, 1.0 / (2 * math.pi), None, ALU.mult, reads=[b_pos], writes=[b_t1])
                DV("tensor_copy", kint, t1, reads=[b_t1], writes=[b_pos])
                DV("tensor_copy", t2, kint, reads=[b_pos], writes=[b_t2])
                DV("scalar_tensor_tensor", t1, t2, -C1, angf, ALU.mult, ALU.add, reads=[b_t2, b_pos], writes=[b_t1])
                DV("scalar_tensor_tensor", t1, t2, -C2, t1, ALU.mult, ALU.add, reads=[b_t2, b_t1], writes=[b_t1])
                DV("tensor_scalar", t2, t1, math.pi, 2 * math.pi, ALU.is_gt, ALU.mult, reads=[b_t1], writes=[b_t2])
                DV("tensor_tensor", t1, t1, t2, ALU.subtract, reads=[b_t1, b_t2], writes=[b_t1])
                DV("tensor_scalar", t2, t1, -math.pi, 2 * math.pi, ALU.is_lt, ALU.mult, reads=[b_t1], writes=[b_t2])
                DV("tensor_tensor", t1, t1, t2, ALU.add, reads=[b_t1, b_t2], writes=[b_t1])
                DV("tensor_scalar", angf, t1, 0.5 * math.pi, None, ALU.add, reads=[b_t1], writes=[b_pos])
                DV("tensor_scalar", t2, angf, math.pi, 2 * math.pi, ALU.is_gt, ALU.mult, reads=[b_pos], writes=[b_t2])
                DV("tensor_tensor", t2, angf, t2, ALU.subtract, reads=[b_pos, b_t2], writes=[b_t2])
                ACT(sinF, t1, AF.Sin, [b_t1], [b_cs])
                ACT(cosF, t2, AF.Sin, [b_t2], [b_cs])
                DMA("sp", esink, sinks_d[l:l + 1, :].broadcast_to([128, 32]), [], [b_esink])
                ACT(esink, esink, AF.Exp, [b_esink], [b_esink])
                view, wb, _ = wload([wcols(OFF["b_k"], 256)], D, 256)

                def cons_k(c, ps, bps):
                    ACT(kraw[c], ps, AF.Copy, [bps], [b_kraw[c]])
                gemm_F(view, wb, 32, 256, hT_fn, [b_hT], cons_k)
                for g in range(4):
                    MM(pg[0], DupT[g % 2], kraw[g // 2], True, True, reads=[b_kraw[g // 2], b_cst], writes=[b_pg[0]])
                    MM(pg[1], DupRT[g % 2], kraw[g // 2], True, True, reads=[b_kraw[g // 2], b_cst], writes=[b_pg[1]])
                    DV("tensor_tensor", t1, pg[0], cosF, ALU.mult, reads=[b_pg[0], b_cs], writes=[b_t1])
                    DV("tensor_tensor", t2, pg[1], sinF, ALU.mult, reads=[b_pg[1], b_cs], writes=[b_t2])
                    DV("tensor_tensor", kTd[g][:, 128:640], t1, t2, ALU.add, reads=[b_t1, b_t2], writes=[b_kTd[g]])
                view, wb, _ = wload([wcols(OFF["b_v"], 256)], D, 256)

                def cons_vb(rt, ps, bps):
                    ACT(vtmp, ps[:, 0:256], AF.Copy, [bps], [b_vtmp])
                    v3 = vtmp.rearrange("p (g d) -> p g d", d=64)
                    DV("tensor_copy", vdup[rt + 1][:, :, 0:64], v3, reads=[b_vtmp], writes=[b_vdup[rt + 1]])
                    DV("tensor_copy", vdup[rt + 1][:, :, 64:128], v3, reads=[b_vtmp], writes=[b_vdup[rt + 1]])
                gemm_T(view, wb, 32, 256, hT_lhs, [b_hT], cons_vb)
                sc_cnt = [0]
                for qb in range(8):
                    view, wb, _ = wload([wcols(OFF["b_q"] + qb * 256, 256)], D, 256)

                    def cons_q(c, ps, bps, qb=qb):
                        qc = qb * 2 + c
                        g = qc // 4
                        ACT(qraw[c], ps, AF.Copy, [bps], [b_qraw[c]], scale=0.125)
                        MM(pg[0], RrotT, qraw[c], True, True, reads=[b_qraw[c], b_cst], writes=[b_pg[0]])
                        DV("tensor_tensor", t1, qraw[c], cosF, ALU.mult, reads=[b_qraw[c], b_cs], writes=[b_t1])
                        DV("tensor_tensor", t2, pg[0], sinF, ALU.mult, reads=[b_pg[0], b_cs], writes=[b_t2])
                        DV("tensor_tensor", qT[c], t1, t2, ALU.add, reads=[b_t1, b_t2], writes=[b_qT[c]])
                        for half in range(2):
                            h = 2 * qc + half
                            rows = slice(half * 64, half * 64 + 64)
                            for rt in range(4):
                                first = first_tile and rt == 0
                                nk = 128 if first else 256
                                si = sc_cnt[0] % 2
                                sc_cnt[0] += 1
                                S2 = pg[1][:, 0:256]
                                bS2 = b_pg[1]
                                qsl = qT[c][rows, rt * 128:(rt + 1) * 128]
                                MM(S2[:, 0:128], kTd[g][rows, 128 + rt * 128:256 + rt * 128], qsl, True, True,
                                   reads=[b_kTd[g], b_qT[c]], writes=[bS2], sig=first)
                                if not first:
                                    MM(S2[:, 128:256], kTd[g][rows, rt * 128:rt * 128 + 128], qsl, True, True,
                                       reads=[b_kTd[g], b_qT[c]], writes=[bS2], sig=True)
                                ACT(PT[si][:, 0:nk], S2[:, 0:nk], AF.Exp, [bS2], [b_PT[si]])
                                DV("tensor_tensor", PT[si][:, 0:nk], PT[si][:, 0:nk], mask2[:, 0:nk], ALU.mult,
                                   reads=[b_PT[si], b_cst], writes=[b_PT[si]])
                                osl = slice(rt * 128, (rt + 1) * 128)
                                MM(pg[2][:, osl], vdup[rt + 1][:, g, :], PT[si][:, 0:128], True, first,
                                   reads=[b_vdup[rt + 1], b_PT[si]], writes=[b_pg[2]], sig=False)
                                if not first:
                                    MM(pg[2][:, osl], vdup[rt][:, g, :], PT[si][:, 128:256], False, True,
                                       reads=[b_vdup[rt], b_PT[si]], writes=[b_pg[2]], sig=False)
                                MM(pg[3][:, osl], onesb, PT[si][:, 0:128], True, first,
                                   reads=[b_PT[si], b_cst], writes=[b_pg[3]], sig=first)
                                if not first:
                                    MM(pg[3][:, osl], onesb, PT[si][:, 128:256], False, True,
                                       reads=[b_PT[si], b_cst], writes=[b_pg[3]], sig=True)
                            DV("tensor_scalar", rtmp[rows, :], pg[3][rows, :], esink[rows, h:h + 1], None, ALU.add,
                               reads=[b_pg[3], b_esink], writes=[b_rtmp])
                            DV("reciprocal", rtmp[rows, :], rtmp[rows, :], reads=[b_rtmp], writes=[b_rtmp])
                            DV("tensor_tensor", yT[1][rows, qc, :], pg[2][rows, :], rtmp[rows, :], ALU.mult,
                               reads=[b_pg[2], b_rtmp], writes=[b_yT[1][qc]])
                    gemm_F(view, wb, 32, 256, hT_fn, [b_hT], cons_q)
                for g in range(4):
                    DV("tensor_copy", kTd[g][:, 0:128], kTd[g][:, 512:640], reads=[b_kTd[g]], writes=[b_kTd[g]])
                DV("tensor_copy", vdup[0][:, :, :], vdup[4][:, :, :], reads=[b_vdup[4]], writes=[b_vdup[0]])
                for blk in range(8):
                    view, wb, _ = wload([wcols(OFF["b_z"] + blk * 256, 256)], D, 256)

                    def cons_zb(c, ps, bps, blk=blk):
                        cc = blk * 2 + c
                        ACT(szb[c], ps, AF.Silu, [bps], [b_szb[c]])
                        DV("tensor_tensor", yT[1][:, cc, :], yT[1][:, cc, :], szb[c], ALU.mult,
                           reads=[b_yT[1][cc], b_szb[c]], writes=[b_yT[1][cc]])
                    gemm_F(view, wb, 32, 256, hT_fn, [b_hT], cons_zb)
                P.barrier()
                if stop == "b":
                    dump_dbg(yT, b_yT, 2, mark_b)
                    raise _Stop()
                top[0] = mark_b

                yT[2] = alloc(4096, BF16).rearrange("p (k t) -> p k t", t=T)
                b_yT[2] = [Buf("yTc%d" % i) for i in range(16)]
                igc = alloc(512)
                lfb = alloc(512)
                bcs = alloc(512)
                Mg = alloc(512)
                b_gates = Buf("gates")
                gsm = alloc(1024)
                b_gsm = Buf("gsm")
                tokT = alloc(64)
                b_tokT = [Buf("tokT%d" % i) for i in range(4)]
                mng = small[:, 128:144]
                b_mng = smallbuf("mng")
                ibfb = small[:, 144:146]
                b_ibfb = smallbuf("ibfb")
                QT = [alloc(256, BF16) for _ in range(2)]
                KT = [alloc(256, BF16) for _ in range(2)]
                b_QT = [Buf("QT0"), Buf("QT1")]
                b_KT = [Buf("KT0"), Buf("KT1")]
                Ktok = [alloc(128, BF16) for _ in range(4)]
                b_Ktok = [Buf("Ktok%d" % i) for i in range(4)]
                Vtok = [alloc(256, BF16) for _ in range(4)]
                b_Vtok = [Buf("Vtok%d" % i) for i in range(4)]
                Dt = alloc(128)
                b_Dt = Buf("Dt")
                St = alloc(64, BF16)
                b_St = Buf("St")
                intra = alloc(512)
                b_intra = Buf("intra")
                num = alloc(512)
                b_num = Buf("num")
                hn = alloc(256, BF16)
                b_hn = Buf("hn")
                kw = alloc(128, BF16)
                b_kw = Buf("kw")
                junkC = alloc(256, BF16)
                sgc = [alloc(256, BF16) for _ in range(2)]
                b_sgc = [Buf("sgc0"), Buf("sgc1")]
                csm = small[:, 160:176]
                b_csm = smallbuf("csm")
                DMA("sp", mng, mng_d[l].rearrange("(c p) -> p c", p=128), [], [b_mng], slow=True)
                DMA("sp", ibfb[0:4, 0:1], ib_d[l].rearrange("(h o) -> h o", o=1), [], [b_ibfb], slow=True)
                DMA("sp", ibfb[0:4, 1:2], fb_d[l].rearrange("(h o) -> h o", o=1), [], [b_ibfb], slow=True)
                DV("tensor_scalar", ibfb[0:4, 0:2], ibfb[0:4, 0:2], 1.0 / 15.0, None, ALU.mult, reads=[b_ibfb], writes=[b_ibfb])
                view, wb, _ = wload([wcols(OFF["c_i"], 8)], D, 8)

                def cons_if(c, ps, bps):
                    dst = igc if c == 0 else lfb
                    ACT(dst[0:4, :], ps[0:4, :], AF.Tanh, [bps, b_ibfb], [b_gates], scale=1.0 / 15.0,
                        bias=ibfb[0:4, c:c + 1])
                gemm_F(view, wb, 32, 8, hT_fn, [b_hT], cons_if, cw=4)
                G4 = lambda a: a[0:4, :]
                DV("tensor_scalar", G4(igc), G4(igc), 15.0, None, ALU.mult, reads=[b_gates], writes=[b_gates])
                ACT(G4(lfb), G4(lfb), AF.Exp, [b_gates], [b_gates], scale=-15.0)
                ACT(G4(lfb), G4(lfb), AF.Ln, [b_gates], [b_gates], bias=1.0)
                DV("tensor_scalar", G4(lfb), G4(lfb), -1.0, None, ALU.mult, reads=[b_gates], writes=[b_gates])
                for ch in range(4):
                    csl = slice(ch * 128, (ch + 1) * 128)
                    DV("tensor_tensor_scan", bcs[0:4, csl], onesf[0:4, 0:128], lfb[0:4, csl], 0.0, ALU.mult, ALU.add,
                       reads=[b_gates, b_cst], writes=[b_gates])
                DV("tensor_tensor", G4(igc), G4(igc), G4(bcs), ALU.subtract, reads=[b_gates], writes=[b_gates])
                selrow = lambda h: cf[0:4, 385 + h * 128:385 + (h + 1) * 128]
                for ch in range(4):
                    csl = slice(ch * 128, (ch + 1) * 128)
                    last = ch * 128 + 127
                    gA = gsm[0:4, 0:128]
                    gE = gsm[0:4, 128:256]
                    gW = gsm[0:4, 256:384]
                    gNM = gsm[0:4, 384 + ch * 128:512 + ch * 128]
                    gml = gsm[0:4, 900:901]
                    gdiag = gsm[0:4, 904:908]
                    DV("tensor_tensor_scan", Mg[0:4, csl], igc[0:4, csl], igc[0:4, csl], mst[0:4, 0:1], ALU.max, ALU.max,
                       reads=[b_gates, b_m], writes=[b_gates])
                    ACT(gA, Mg[0:4, csl], AF.Exp, [b_gates, b_m], [b_gsm], scale=-1.0, bias=mst[0:4, 0:1])
                    DV("tensor_tensor", gE, bcs[0:4, csl], Mg[0:4, csl], ALU.add, reads=[b_gates], writes=[b_gsm])
                    ACT(gE, gE, AF.Exp, [b_gsm], [b_gsm], scale=-1.0)
                    DV("tensor_scalar", gml, Mg[0:4, last:last + 1], -1.0, None, ALU.mult, reads=[b_gates], writes=[b_gsm])
                    ACT(gW, igc[0:4, csl], AF.Exp, [b_gates, b_gsm], [b_gsm], bias=gml)
                    DV("tensor_scalar", gNM, Mg[0:4, csl], -1.0, None, ALU.mult, reads=[b_gates], writes=[b_gsm])
                    DV("tensor_scalar", gdiag, identf[0:4, 0:4], gA[:, 127:128], None, ALU.mult,
                       reads=[b_gsm, b_cst], writes=[b_gsm])
                    DV("tensor_tensor", mst[0:4, 0:1], bcs[0:4, last:last + 1], Mg[0:4, last:last + 1], ALU.add,
                       reads=[b_gates, b_gsm], writes=[b_m])
                    px = pg[0][:, 384:400]
                    TR(px[:, 0:4], gA, identf[0:4, 0:4], reads=[b_gsm, b_cst], writes=[b_pg[0]], sig=False)
                    TR(px[:, 4:8], gE, identf[0:4, 0:4], reads=[b_gsm, b_cst], writes=[b_pg[0]], sig=False)
                    TR(px[:, 8:12], gW, identf[0:4, 0:4], reads=[b_gsm, b_cst], writes=[b_pg[0]], sig=False)
                    MM(px[:, 12:16], onesf[0:4, 0:128], gdiag, True, True, reads=[b_gsm, b_cst], writes=[b_pg[0]])
                    tk = tokT[:, ch * 16:(ch + 1) * 16]
                    ACT(tk, px, AF.Copy, [b_pg[0]], [b_tokT[ch]])
                for hd in range(4):
                    view, wb, _ = wload([wcols(OFF["c_q"] + hd * 256, 256)], D, 256)

                    def cons_cq(c, ps, bps):
                        ACT(QT[c], ps, AF.Copy, [bps], [b_QT[c]], scale=1.0 / 16.0)
                    gemm_F(view, wb, 32, 256, hT_fn, [b_hT], cons_cq)
                    view, wb, _ = wload([wcols(OFF["c_k"] + hd * 256, 256)], D, 256)

                    def cons_ck(c, ps, bps):
                        ACT(KT[c], ps, AF.Copy, [bps], [b_KT[c]])
                    gemm_F(view, wb, 32, 256, hT_fn, [b_hT], cons_ck)
                    for ch in range(4):
                        transposes(lambda i, ch=ch: KT[i][:, ch * 128:(ch + 1) * 128], 2,
                                   lambda i0, cnt, ch=ch: Ktok[ch].rearrange("p (c d) -> p c d", d=128)[:, i0:i0 + cnt, :],
                                   b_KT, [b_Ktok[ch]])
                    for vb in range(2):
                        view, wb, _ = wload([wcols(OFF["c_v"] + hd * 512 + vb * 256, 256)], D, 256)

                        def cons_cv(rt, ps, bps, vb=vb):
                            ACT(Vtok[rt][:, vb * 256:(vb + 1) * 256], ps[:, 0:256], AF.Copy, [bps], [b_Vtok[rt]])
                        gemm_T(view, wb, 32, 256, hT_lhs, [b_hT], cons_cv)
                    bC = b_C[hd]
                    for ch in range(4):
                        csl = slice(ch * 128, (ch + 1) * 128)
                        tk = tokT[:, ch * 16:(ch + 1) * 16]
                        a_col = tk[:, hd:hd + 1]
                        e_col = tk[:, 4 + hd:5 + hd]
                        w_col = tk[:, 8 + hd:9 + hd]
                        d_col = tk[:, 12 + hd:13 + hd]
                        gNM = gsm[0:4, 384 + ch * 128:512 + ch * 128]
                        pL = pg[0][:, 0:128]
                        pS = pg[0][:, 128:256]
                        psm = pg[0][:, 256:264]
                        MM(pL, igc[0:4, csl], selrow(hd), True, False, reads=[b_gates, b_cst], writes=[b_pg[0]], sig=False)
                        MM(pL, selrow(hd), gNM, False, False, reads=[b_gsm, b_cst], writes=[b_pg[0]], sig=False)
                        MM(pL, identf, mb128, False, True, reads=[b_cst], writes=[b_pg[0]], sig=True)
                        ACT(Dt, pL, AF.Exp, [b_pg[0]], [b_Dt])
                        for dc in range(2):
                            MM(pS, KT[dc][:, csl], QT[dc][:, csl], dc == 0, dc == 1, reads=[b_KT[dc], b_QT[dc]],
                               writes=[b_pg[0]])
                        DV("tensor_tensor", St, pS, Dt, ALU.mult, reads=[b_pg[0], b_Dt], writes=[b_St])
                        MM(pg[1], St, Vtok[ch], True, True, reads=[b_St, b_Vtok[ch]], writes=[b_pg[1]])
                        MM(psm[:, 0:1], St, onesb[:, 0:1], True, True, reads=[b_St, b_cst], writes=[b_pg[0]], sig=False)
                        for dc in range(2):
                            MM(pg[2], QT[dc][:, csl], Cbf[:, hd * 2 + dc, :], dc == 0, dc == 1,
                               reads=[b_QT[dc], bC], writes=[b_pg[2]])
                        for dc in range(2):
                            MM(psm[:, 1:2], QT[dc][:, csl], nbf[:, hd * 2 + dc:hd * 2 + dc + 1], dc == 0, dc == 1,
                               reads=[b_QT[dc], bC], writes=[b_pg[0]])
                        ACT(intra, pg[1], AF.Copy, [b_pg[1]], [b_intra])
                        DV("scalar_tensor_tensor", num, pg[2], a_col, intra, ALU.mult, ALU.add,
                           reads=[b_pg[2], b_intra, b_tokT[ch]], writes=[b_num])
                        ACT(csm[:, 0:1], psm[:, 0:1], AF.Copy, [b_pg[0]], [b_csm])
                        DV("scalar_tensor_tensor", csm[:, 1:2], psm[:, 1:2], a_col, csm[:, 0:1], ALU.mult, ALU.add,
                           reads=[b_pg[0], b_csm, b_tokT[ch]], writes=[b_csm])
                        ACT(csm[:, 1:2], csm[:, 1:2], AF.Abs, [b_csm], [b_csm])
                        DV("tensor_tensor", csm[:, 1:2], csm[:, 1:2], e_col, ALU.max, reads=[b_csm, b_tokT[ch]], writes=[b_csm])
                        DV("reciprocal", csm[:, 2:3], csm[:, 1:2], reads=[b_csm], writes=[b_csm])
                        ACT(junkC, num, AF.Square, [b_num, b_csm], [b_junk, b_csm], scale=csm[:, 2:3], accum_out=csm[:, 3:4])
                        DV("tensor_scalar", csm[:, 6:7], csm[:, 3:4], 1.0 / 512, EPS, ALU.mult, ALU.add, reads=[b_csm], writes=[b_csm])
                        ACT(csm[:, 4:5], csm[:, 6:7], AF.Sqrt, [b_csm], [b_csm]); DV("reciprocal", csm[:, 4:5], csm[:, 4:5], reads=[b_csm], writes=[b_csm])
                        DV("tensor_tensor", csm[:, 5:6], csm[:, 2:3], csm[:, 4:5], ALU.mult, reads=[b_csm], writes=[b_csm])
                        DV("tensor_scalar", hn, num, csm[:, 5:6], None, ALU.mult, reads=[b_num, b_csm], writes=[b_hn])
                        for vc in range(4):
                            hb = tcount[0] % 2
                            tcount[0] += 1
                            pt = psT[hb][:, 0:128]
                            TR(pt, hn[:, vc * 128:(vc + 1) * 128], identb, reads=[b_hn, b_cst], writes=[b_psT[hb]])
                            DV("tensor_scalar", yT[2][:, hd * 4 + vc, csl], pt, mng[:, hd * 4 + vc:hd * 4 + vc + 1], None,
                               ALU.mult, reads=[b_psT[hb], b_mng], writes=[b_yT[2][hd * 4 + vc]])
                        DV("tensor_scalar", kw, Ktok[ch], w_col, None, ALU.mult, reads=[b_Ktok[ch], b_tokT[ch]], writes=[b_kw])
                        for dc in range(2):
                            MM(pg[3], kw[:, dc * 128:(dc + 1) * 128], Vtok[ch], True, True,
                               reads=[b_kw, b_Vtok[ch]], writes=[b_pg[3]])
                            DV("scalar_tensor_tensor", Cst[:, hd * 2 + dc, :], Cst[:, hd * 2 + dc, :], d_col, pg[3],
                               ALU.mult, ALU.add, reads=[b_pg[3], b_tokT[ch], bC], writes=[bC])
                            P.op("act", lambda e, o=Cbf[:, hd * 2 + dc, :], i_=Cst[:, hd * 2 + dc, :]: e.copy(o, i_), [bC], [bC])
                        for dc in range(2):
                            MM(psm[:, 2 + dc:3 + dc], kw[:, dc * 128:(dc + 1) * 128], onesb[:, 0:1], True, True,
                               reads=[b_kw, b_cst], writes=[b_pg[0]])
                            DV("scalar_tensor_tensor", nst[:, hd * 2 + dc:hd * 2 + dc + 1], nst[:, hd * 2 + dc:hd * 2 + dc + 1],
                               d_col, psm[:, 2 + dc:3 + dc], ALU.mult, ALU.add, reads=[b_pg[0], b_tokT[ch], bC], writes=[bC])
                        DV("tensor_copy", nbf[:, hd * 2:hd * 2 + 2], nst[:, hd * 2:hd * 2 + 2], reads=[bC], writes=[bC])
                    for (off, func) in ((OFF["c_o"], AF.Sigmoid), (OFF["c_z"], AF.Silu)):
                        for vb in range(2):
                            view, wb, _ = wload([wcols(off + hd * 512 + vb * 256, 256)], D, 256)

                            def cons_oz(c, ps, bps, vb=vb, func=func):
                                cc = hd * 4 + vb * 2 + c
                                ACT(sgc[c], ps, func, [bps], [b_sgc[c]])
                                DV("tensor_tensor", yT[2][:, cc, :], yT[2][:, cc, :], sgc[c], ALU.mult,
                                   reads=[b_yT[2][cc], b_sgc[c]], writes=[b_yT[2][cc]])
                            gemm_F(view, wb, 32, 256, hT_fn, [b_hT], cons_oz)
                P.barrier()
                if debug and l == 0 and ti == 0:
                    top[0] = mark_b + 4096
                    dtmp = alloc(512)
                    b_dtmp = Buf("dtmp")
                    for j in range(3):
                        for cc in range(16):
                            DV("tensor_copy", dtmp, yT[j][:, cc, :], reads=[b_yT[j][cc]], writes=[b_dtmp])
                            DMA("sp", dbg_d[j, cc * 128:(cc + 1) * 128, :], dtmp, [b_dtmp], [b_dbg])
                    P.barrier()
                top[0] = mark_b + 4096

                sg = [[alloc(512) for _ in range(2)] for _ in range(2)]
                b_sg = [[Buf("sg%d%d" % (a, c)) for c in range(2)] for a in range(2)]
                acc = [alloc(512) for _ in range(2)]
                b_acc = [Buf("acc0"), Buf("acc1")]
                mtmp = [alloc(512) for _ in range(2)]
                b_mtmp = [Buf("mtmp0"), Buf("mtmp1")]
                mo = [alloc(256, BF16) for _ in range(2)]
                b_mo = [Buf("mo0"), Buf("mo1")]
                for db in range(16):
                    for j in range(3):
                        view, wb, _ = wload([wcols(OFF["g"] + j * D + db * 256, 256)], D, 256)

                        def cons_g(c, ps, bps, j=j):
                            ACT(sg[j % 2][c], ps, AF.Sigmoid, [bps], [b_sg[j % 2][c]])
                        gemm_F(view, wb, 32, 256, hT_fn, [b_hT], cons_g)
                        view, wb, _ = wload([w_branch[l, j][:, db * 256:(db + 1) * 256]], 2048, 256)

                        def cons_br(c, ps, bps, j=j, db=db):
                            s_ = sg[j % 2][c]
                            bs_ = b_sg[j % 2][c]
                            if j == 0:
                                DV("tensor_tensor", acc[c], ps, s_, ALU.mult, reads=[bps, bs_], writes=[b_acc[c]])
                            elif j == 1:
                                DV("tensor_tensor", mtmp[c], ps, s_, ALU.mult, reads=[bps, bs_], writes=[b_mtmp[c]])
                                DV("tensor_tensor", acc[c], acc[c], mtmp[c], ALU.add, reads=[b_acc[c], b_mtmp[c]],
                                   writes=[b_acc[c]])
                            else:
                                DV("tensor_tensor", mtmp[c], ps, s_, ALU.mult, reads=[bps, bs_], writes=[b_mtmp[c]])
                                DV("tensor_tensor", mo[c], acc[c], mtmp[c], ALU.add, reads=[b_acc[c], b_mtmp[c]],
                                   writes=[b_mo[c]])
                                r0 = (db * 2 + c) * 128
                                DMA("sp", mT_d[r0:r0 + 128, :], mo[c], [b_mo[c]], [b_mTd])
                        gemm_F(view, wb, 16, 256, lambda k, j=j: yT[j][:, k, :], b_yT[j], cons_br)
                P.barrier()
                if stop == "p2":
                    raise _Stop()
                top[0] = 0

                mT = alloc(8192, BF16).rearrange("p (k t) -> p k t", t=T)
                b_mT = Buf("mT")
                gbuf = alloc(4096)
                b_gbuf = Buf("gbuf2")
                orow = alloc(2048)
                xrow2 = alloc(2048)
                b_orow, b_xrow2 = Buf("orow"), Buf("xrow2")
                xmb = alloc(1024, BF16)
                b_xmb = Buf("xmb")
                otmp = [alloc(256) for _ in range(2)]
                b_otmp = [Buf("otmp0"), Buf("otmp1")]
                junk3 = alloc(128, BF16)
                so = small[:, 176:240]
                b_so = smallbuf("so")
                st3 = small[:, 240:248]
                b_st3 = smallbuf("st3")
                DMA("sp", mT, mT_d.rearrange("(k p) t -> p k t", p=128), [b_mTd], [b_mT])
                DMA("sp", gbuf, norm_post[l:l + 1, :].broadcast_to([128, D]), [], [b_gbuf])
                oc = [0]
                for blk in range(16):
                    view, wb, _ = wload([w_out[l][:, blk * 256:(blk + 1) * 256]], D, 256)

                    def cons_o(rt, ps, bps, blk=blk):
                        oi = oc[0] % 2
                        oc[0] += 1
                        ACT(junk3[:, 0:256], ps[:, 0:256], AF.Square, [bps], [b_junk, b_so],
                            accum_out=so[:, rt * 16 + blk:rt * 16 + blk + 1])
                        ACT(otmp[oi], ps[:, 0:256], AF.Copy, [bps], [b_otmp[oi]])
                        DMA("sp", o_d[rt * 128:(rt + 1) * 128, blk * 256:(blk + 1) * 256], otmp[oi], [b_otmp[oi]], [b_od])
                    gemm_T(view, wb, 32, 256, lambda k, rt: mT[:, k, rt * 128:(rt + 1) * 128], [b_mT], cons_o)
                DV("tensor_reduce", st3[:, 0:4], so.rearrange("p (r b) -> p r b", b=16), AX.X, ALU.add,
                   reads=[b_so], writes=[b_st3])
                DV("tensor_scalar", st3[:, 4:8], st3[:, 0:4], 1.0 / D, EPS, ALU.mult, ALU.add, reads=[b_st3], writes=[b_st3])
                ACT(st3[:, 0:4], st3[:, 4:8], AF.Sqrt, [b_st3], [b_st3]); DV("reciprocal", st3[:, 0:4], st3[:, 0:4], reads=[b_st3], writes=[b_st3])
                for rt in range(4):
                    r0 = tok0 + rt * 128
                    for half in range(2):
                        hs = slice(half * 2048, (half + 1) * 2048)
                        DMA("sp", orow, o_d[rt * 128:(rt + 1) * 128, hs], [b_od], [b_orow])
                        DMA("sp", xrow2, x_in[r0:r0 + 128, hs], [b_xin], [b_xrow2])
                        DV("scalar_tensor_tensor", orow, orow, st3[:, rt:rt + 1], gbuf[:, hs], ALU.mult, ALU.mult,
                           reads=[b_orow, b_st3, b_gbuf], writes=[b_orow])
                        DV("tensor_tensor", orow, orow, xrow2, ALU.add, reads=[b_orow, b_xrow2], writes=[b_orow])
                        DMA("sp", xm_d[rt * 128:(rt + 1) * 128, hs], orow, [b_orow], [b_xmd])
                        ACT(xmb, orow, AF.Copy, [b_orow], [b_xmb])
                        transposes(lambda i: xmb[:, i * 128:(i + 1) * 128], 16,
                                   lambda i0, cnt, rt=rt, half=half: mT[:, half * 16 + i0:half * 16 + i0 + cnt, rt * 128:(rt + 1) * 128],
                                   [b_xmb], [b_mT])
                prow = alloc(256)
                b_prow = Buf("prow")
                pbf = alloc(128, BF16)
                b_pbf = Buf("pbf")
                pT = alloc(512, BF16).rearrange("p (k t) -> p k t", t=T)
                b_pT = Buf("pT")
                sgt = [alloc(256) for _ in range(2)]
                b_sgt = [Buf("sgt0"), Buf("sgt1")]
                et = [alloc(256) for _ in range(2)]
                b_et = [Buf("et0"), Buf("et1")]
                xmt = [alloc(256) for _ in range(2)]
                b_xmt = [Buf("xmt0"), Buf("xmt1")]
                se = small[:, 176:240]
                b_se = b_so
                st4 = small[:, 248:256]
                b_st4 = smallbuf("st4")
                DMA("sp", gbuf, ple_norm[l:l + 1, :].broadcast_to([128, D]), [], [b_gbuf])
                for rt in range(4):
                    r0 = tok0 + rt * 128
                    DMA("sp", prow, p_d[l, r0:r0 + 128, :], [], [b_prow])
                    ACT(pbf, prow, AF.Copy, [b_prow], [b_pbf])
                    transposes(lambda i: pbf[:, i * 128:(i + 1) * 128], 2,
                               lambda i0, cnt, rt=rt: pT[:, i0:i0 + cnt, rt * 128:(rt + 1) * 128], [b_pbf], [b_pT])
                pview, pwb, pslot = wload([ple_proj[l][:, :]], 256, D, pin=True)
                for blk in range(16):
                    for rt in range(4):
                        pi = gcount[0] % 2
                        gcount[0] += 1
                        for k in range(2):
                            MM(psG[pi][:, 0:256], pT[:, k, rt * 128:(rt + 1) * 128], pview[:, k, blk * 256:(blk + 1) * 256],
                               k == 0, k == 1, reads=[b_pT, pwb], writes=[b_psG[pi]])
                        ACT(junk3[:, 0:256], psG[pi][:, 0:256], AF.Square, [b_psG[pi]], [b_junk, b_se],
                            accum_out=se[:, rt * 16 + blk:rt * 16 + blk + 1])
                DV("tensor_reduce", st4[:, 0:4], se.rearrange("p (r b) -> p r b", b=16), AX.X, ALU.add,
                   reads=[b_se], writes=[b_st4])
                DV("tensor_scalar", st4[:, 4:8], st4[:, 0:4], 1.0 / D, EPS, ALU.mult, ALU.add, reads=[b_st4], writes=[b_st4])
                ACT(st4[:, 0:4], st4[:, 4:8], AF.Sqrt, [b_st4], [b_st4]); DV("reciprocal", st4[:, 0:4], st4[:, 0:4], reads=[b_st4], writes=[b_st4])
                ec = [0]
                for blk in range(16):
                    view, wb, _ = wload([ple_gate[l][:, blk * 256:(blk + 1) * 256]], D, 256)

                    def cons_pg(rt, ps, bps, blk=blk):
                        ei = ec[0] % 2
                        ec[0] += 1
                        r0 = tok0 + rt * 128
                        bsl = slice(blk * 256, (blk + 1) * 256)
                        ACT(sgt[ei], ps[:, 0:256], AF.Sigmoid, [bps], [b_sgt[ei]])
                        DMA("sp", xmt[ei], xm_d[rt * 128:(rt + 1) * 128, bsl], [b_xmd], [b_xmt[ei]])
                        for k in range(2):
                            MM(pg[0][:, 0:256], pT[:, k, rt * 128:(rt + 1) * 128], pview[:, k, bsl], k == 0, k == 1,
                               reads=[b_pT, pwb], writes=[b_pg[0]])
                        DV("scalar_tensor_tensor", et[ei], pg[0][:, 0:256], st4[:, rt:rt + 1], gbuf[:, bsl], ALU.mult, ALU.mult,
                           reads=[b_pg[0], b_st4, b_gbuf], writes=[b_et[ei]])
                        DV("tensor_tensor", et[ei], et[ei], sgt[ei], ALU.mult, reads=[b_et[ei], b_sgt[ei]], writes=[b_et[ei]])
                        DV("tensor_tensor", et[ei], et[ei], xmt[ei], ALU.add, reads=[b_et[ei], b_xmt[ei]], writes=[b_et[ei]])
                        DMA("sp", x_out[r0:r0 + 128, bsl], et[ei], [b_et[ei]], [b_xout])
                    gemm_T(view, wb, 32, 256, lambda k, rt: mT[:, k, rt * 128:(rt + 1) * 128], [b_mT], cons_pg)
                ring["pinned"].discard(pslot)
        try:
            _layers()
        except _Stop:
            pass
        P.barrier()
        outs = [b_x[n_layers]]
        if debug:
            outs.append(b_dbg)
        P.wait_all("sp", outs)
        P.build(st)
    return nc


_CACHE = {}


def _prep_weights(inputs):
    cbf, cff = make_consts()
    w = {}
    for k in ("norm_pre", "w_in", "gmlp_ln_g", "gmlp_ln_b", "attn_sinks", "mlstm_ib", "mlstm_fb",
              "mlstm_norm_g", "w_branch", "w_out", "norm_post", "ple_proj", "ple_norm", "ple_gate"):
        w[k] = np.ascontiguousarray(np.asarray(inputs[k], dtype=np.float32))
    w["gmlp_wsT"] = np.ascontiguousarray(np.transpose(np.asarray(inputs["gmlp_ws"], np.float32), (0, 1, 3, 2)))
    w["gmlp_bs"] = np.ascontiguousarray(np.asarray(inputs["gmlp_bs"], np.float32).reshape(2, 1024))
    w["cst_bf"] = cbf
    w["cst_f"] = cff
    return w


def kernel(**inputs):
    x = np.asarray(inputs["x"], np.float32)
    p = np.asarray(inputs["p"], np.float32)
    pos = np.asarray(inputs["positions"], np.int32)
    B, S, _ = x.shape
    if "nc" not in _CACHE:
        _CACHE["nc"] = build_program(S_core=S)
    nc = _CACHE["nc"]
    w = _prep_weights(inputs)
    in_maps = []
    for b in range(B):
        m = dict(w)
        m["x"] = np.ascontiguousarray(x[b])
        m["p"] = np.ascontiguousarray(p[:, b])
        m["pos"] = np.ascontiguousarray(pos[b:b + 1])
        in_maps.append(m)
    res = run_bass_kernel_spmd(nc, in_maps, core_ids=list(range(B)))
    return np.stack([np.asarray(r["out"], np.float32) for r in res.results], axis=0)
```

```python
import math
from contextlib import ExitStack
import numpy as np
import ml_dtypes
import concourse.bass as bass
import concourse.mybir as mybir
from concourse.bass_utils import run_bass_kernel_spmd

F32 = mybir.dt.float32
BF16 = mybir.dt.bfloat16
I32 = mybir.dt.int32
AF = mybir.ActivationFunctionType
ALU = mybir.AluOpType
AX = mybir.AxisListType
ENGS = ("pe", "act", "dve", "pool", "sp")

D = 4096
T = 512
NIN = 31240
OFF = dict(a_u=0, a_v=2048, a_z=4096, b_q=6144, b_k=8192, b_v=8448, b_z=8704, c_q=10752,
           c_k=11776, c_v=12800, c_i=14848, c_f=14852, c_o=14856, c_z=16904, g=18952)
EPS = 1e-6
NCB = 1152
NCF = 900
ARENA = 29000


class Buf:
    __slots__ = ("name", "w", "r", "dsem", "dcnt")

    def __init__(self, name):
        self.name = name
        self.w = None
        self.r = []
        self.dsem = None
        self.dcnt = 0


class Prog:
    def __init__(self, nc):
        self.nc = nc
        self.ops = {e: [] for e in ENGS}
        self.cnt = {e: 0 for e in ENGS}
        self.sems = {}
        self.sem_keys = list(ENGS)
        self.ndsem = 0
        self.seen = {e: {} for e in ENGS}
        self.dcounts = {}
        self.name2sem = {}
        self.sem_keys.append("cc")

    def _dsem(self, buf):
        if buf.dsem is None:
            if buf.name not in self.name2sem:
                self.name2sem[buf.name] = "d%d" % self.ndsem
                self.ndsem += 1
                self.sem_keys.append(self.name2sem[buf.name])
            buf.dsem = self.name2sem[buf.name]
        return buf.dsem

    def _waits(self, eng, reads, writes):
        need = {}

        def add(tok):
            if tok is None:
                return
            k, v = tok
            if k == eng and eng == "pe":
                return
            if need.get(k, 0) < v:
                need[k] = v
        for b in reads:
            add(b.w)
        for b in writes:
            add(b.w)
            for t in b.r:
                add(t)
        out = []
        seen = self.seen[eng]
        for k, v in need.items():
            if seen.get(k, 0) >= v:
                continue
            seen[k] = v
            out.append((k, v))
        return out

    def op(self, eng, fn, reads=(), writes=(), sig=True):
        waits = self._waits(eng, reads, writes)
        if sig:
            self.cnt[eng] += 1
            tok = (eng, self.cnt[eng])
        else:
            tok = (eng, self.cnt[eng] + 1)
        self.ops[eng].append((waits, fn, (eng, 1) if sig else None))
        for b in reads:
            if len(b.r) > 48:
                mx = {}
                for k_, v_ in b.r:
                    if mx.get(k_, 0) < v_:
                        mx[k_] = v_
                b.r = list(mx.items())
            b.r.append(tok)
        for b in writes:
            b.w = tok
            b.r = []
        return tok

    def dma(self, queue, fn, reads=(), writes=(), dbuf=None):
        if dbuf is None:
            dbuf = writes[0]
        waits = self._waits(queue, reads, writes)
        key = self._dsem(dbuf)
        self.dcounts[key] = self.dcounts.get(key, 0) + 16
        tok = (key, self.dcounts[key])
        self.ops[queue].append((waits, fn, (key, 16)))
        for b in reads:
            b.r.append(tok)
        for b in writes:
            b.w = tok
            b.r = []
        return tok

    def coll(self, fn, reads, writes):
        waits = self._waits("pool", reads, writes)
        self.dcounts["cc"] = self.dcounts.get("cc", 0) + 1
        tok = ("cc", self.dcounts["cc"])
        self.ops["pool"].append((waits, fn, ("cc", 1)))
        for b in reads:
            b.r.append(tok)
        for b in writes:
            b.w = tok
            b.r = []
        return tok

    def wait_all(self, eng, bufs):
        waits = self._waits(eng, bufs, ())
        self.ops[eng].append((waits, None, None))

    def barrier(self):
        toks = {e: self.cnt[e] for e in ENGS if self.cnt[e] > 0}
        skip = set(self.name2sem.get(n) for n in ("wt0", "wt1", "wt2"))
        skip.add("cc")
        for k_, v_ in self.dcounts.items():
            if k_ not in skip:
                toks[k_] = v_
        for e in ENGS:
            if e == "pool":
                continue
            waits = []
            for k, v in toks.items():
                if k == e or self.seen[e].get(k, 0) >= v:
                    continue
                self.seen[e][k] = v
                waits.append((k, v))
            if waits:
                self.ops[e].append((waits, None, None))

    def build(self, stack):
        nc = self.nc
        for k in self.sem_keys:
            self.sems[k] = stack.enter_context(nc.semaphore("s_" + k))
        block = stack.enter_context(nc.Block())
        sems = self.sems

        def run(engname):
            def body(e):
                for waits, fn, inc in self.ops[engname]:
                    for k, v in waits:
                        e.wait_ge(sems[k], v)
                    if fn is not None:
                        ins = fn(e)
                        if inc is not None:
                            ins.then_inc(sems[inc[0]], inc[1])
            return body
        block.tensor(run("pe"))
        block.scalar(run("act"))
        block.vector(run("dve"))
        block.gpsimd(run("pool"))
        block.sync(run("sp"))


def make_consts():
    cb = np.zeros((128, NCB), np.float32)
    cf = np.zeros((128, NCF), np.float32)
    idx = np.arange(128)
    cb[:, 0:128] = np.eye(128)
    cb[:, 128:256] = 1.0
    cb[:, 256:384] = (idx[:, None] <= idx[None, :])
    cb[:, 384:512] = (idx[:, None] > idx[None, :])
    R = np.zeros((128, 128), np.float32)
    for p_ in range(128):
        d = p_ % 64
        if d < 8:
            R[p_ + 8, p_] = -1.0
        elif d < 16:
            R[p_ - 8, p_] = 1.0
    cb[:, 512:640] = R
    for half in range(2):
        Dm = np.zeros((128, 128), np.float32)
        for p_ in range(128):
            Dm[half * 64 + p_ % 64, p_] = 1.0
        cb[:, 640 + half * 128:768 + half * 128] = Dm
        cb[:, 896 + half * 128:1024 + half * 128] = R @ Dm
    cf[:, 0:128] = np.eye(128)
    cf[:, 128:256] = 1.0
    cf[:, 256:384] = np.where(idx[:, None] <= idx[None, :], 0.0, -30000.0)
    inv = (500000.0 ** (-(np.arange(0, 16, 2, dtype=np.float32)) / 16.0)).astype(np.float32)
    for p_ in range(128):
        d = p_ % 64
        cf[p_, 384] = inv[d % 8] if d < 16 else 0.0
    for h in range(4):
        cf[h, 385 + h * 128:385 + (h + 1) * 128] = 1.0
    return cb.astype(ml_dtypes.bfloat16), cf


class _Stop(Exception):
    pass


NS = 5
GROUPS = [[0, 1], [2, 3], [4, 5], [6, 7]]


def build_program(n_slots=NS, debug=False, stop=None, use_cc=True):
    nc = bass.Bass("TRN2", target_bir_lowering=False)

    def din(name, shape, dt=F32):
        return nc.dram_tensor(name, list(shape), dt, kind="ExternalInput").ap()
    x_d = din("xs", [n_slots * T, D])
    p_d = din("ps", [n_slots * T, 256])
    pos_d = din("pos", [1, n_slots * T], I32)
    flags_d = din("flags", [1, 32])
    norm_pre = din("norm_pre", [2, D])
    w_in = din("w_in", [2, D, NIN])
    ln_g = din("gmlp_ln_g", [2, 2048])
    ln_b = din("gmlp_ln_b", [2, 2048])
    wsT_d = din("gmlp_wsT", [2, 8, 128, 128])
    bs_d = din("gmlp_bs", [2, 1024])
    sinks_d = din("attn_sinks", [2, 32])
    ib_d = din("mlstm_ib", [2, 4])
    fb_d = din("mlstm_fb", [2, 4])
    mng_d = din("mlstm_norm_g", [2, 2048])
    w_branch = din("w_branch", [2, 3, 2048, D])
    w_out = din("w_out", [2, D, D])
    norm_post = din("norm_post", [2, D])
    ple_proj = din("ple_proj", [2, 256, D])
    ple_norm = din("ple_norm", [2, D])
    ple_gate = din("ple_gate", [2, D, D])
    cb_d = din("cst_bf", [128, NCB], BF16)
    cf_d = din("cst_f", [128, NCF])
    out_d = nc.dram_tensor("outs", [n_slots * T, D], F32, kind="ExternalOutput").ap()
    xc_d = nc.dram_tensor("xc_scr", [T, D], F32).ap()
    cfA_in_t = nc.dram_tensor("cfA_in", [1024, 512], F32)
    cfA_out_t = nc.dram_tensor("cfA_out", [2048, 512], F32)
    cfB_in_t = nc.dram_tensor("cfB_in", [256, 512], F32)
    cfB_out_t = nc.dram_tensor("cfB_out", [512, 512], F32)
    cfA_in, cfA_out, cfB_in, cfB_out = cfA_in_t.ap(), cfA_out_t.ap(), cfB_in_t.ap(), cfB_out_t.ap()
    mT_d = nc.dram_tensor("mT_scr", [4096, T], BF16).ap()
    o_d = nc.dram_tensor("o_scr", [T, D], F32).ap()
    xm_d = nc.dram_tensor("xm_scr", [T, D], F32).ap()
    if debug:
        dbg_d = nc.dram_tensor("dbg", [3, 2048, T], F32, kind="ExternalOutput").ap()

    P = Prog(nc)
    st = ExitStack()
    with st:
        def _full(t, shape):
            return t[:, :] if len(shape) == 2 else t[:, :, :]

        def sb(name, shape, dt=F32):
            return _full(st.enter_context(nc.sbuf_tensor(name, list(shape), dt)), shape)

        def psum(name, shape, dt=F32):
            return _full(st.enter_context(nc.psum_tensor(name, list(shape), dt)), shape)

        cb = sb("cb", [128, NCB], BF16)
        cf = sb("cf", [128, NCF])
        identb, onesb = cb[:, 0:128], cb[:, 128:256]
        mask2 = cb[:, 256:512]
        RrotT = cb[:, 512:640]
        DupT = [cb[:, 640:768], cb[:, 768:896]]
        DupRT = [cb[:, 896:1024], cb[:, 1024:1152]]
        identf, onesf, mb128 = cf[:, 0:128], cf[:, 128:256], cf[:, 256:384]
        invf = cf[:, 384:385]
        b_cst = Buf("cst")
        wt = [sb("wt%d" % i, [128, 8192], BF16) for i in range(3)]
        b_wt = [Buf("wt%d" % i) for i in range(3)]
        Cst = sb("Cst", [128, 8, 512])
        Cbf = sb("Cbf", [128, 8, 512], BF16)
        nst = sb("nst", [128, 8])
        nbf = sb("nbf", [128, 8], BF16)
        mst = sb("mst", [4, 1])
        b_C = [Buf("C%d" % i) for i in range(4)]
        b_m = Buf("mst")
        kTd = [sb("kTd%d" % g, [128, 640], BF16) for g in range(4)]
        b_kTd = [Buf("kTd%d" % g) for g in range(4)]
        vdup = [sb("vdup%d" % i, [128, 4, 128], BF16) for i in range(5)]
        b_vdup = [Buf("vdup%d" % i) for i in range(5)]
        cosF = sb("cosF", [128, T])
        sinF = sb("sinF", [128, T])
        b_cs = Buf("cossin")
        small = sb("small", [128, 256])
        b_small = {}
        arena = sb("arena", [128, ARENA])

        def smallbuf(name):
            if name not in b_small:
                b_small[name] = Buf("sm_" + name)
            return b_small[name]

        psG = [psum("psG%d" % i, [128, 512]) for i in range(2)]
        b_psG = [Buf("psG%d" % i) for i in range(2)]
        psT = [psum("psT%d" % i, [128, 1024], BF16) for i in range(2)]
        b_psT = [Buf("psT0"), Buf("psT1")]
        pg = [psum("pg%d" % i, [128, 512]) for i in range(4)]
        b_pg = [Buf("pg%d" % i) for i in range(4)]

        top = [0]

        def alloc(n32, dt=F32):
            a = top[0]
            top[0] += n32
            assert top[0] <= ARENA, ("arena overflow", top[0])
            v = arena[:, a:a + n32]
            return v if dt == F32 else v.bitcast(dt)

        def ACT(out, in_, func, reads, writes, **kw):
            P.op("act", lambda e: e.activation(out, in_, func, **kw), reads, writes)

        def DV(name, *args, reads, writes, **kw):
            P.op("dve", lambda e: getattr(e, name)(*args, **kw), reads, writes)

        def PL(name, *args, reads, writes, **kw):
            P.op("pool", lambda e: getattr(e, name)(*args, **kw), reads, writes)

        def MM(out, lhsT, rhs, start, stop, reads, writes, sig=None):
            if sig is None:
                sig = stop
            P.op("pe", lambda e: e.matmul(out, lhsT, rhs, start=start, stop=stop), reads, writes, sig)

        def TR(out, in_, ident, reads, writes, sig=True):
            P.op("pe", lambda e: e.transpose(out, in_, ident), reads, writes, sig)

        def DMA(q, out, in_, reads, writes, dbuf=None, slow=False):
            if slow:
                P.dma(q, lambda e: e.dma_start(out=out, in_=in_, allow_slow_non_contiguous=True), reads, writes, dbuf)
            else:
                P.dma(q, lambda e: e.dma_start(out=out, in_=in_), reads, writes, dbuf)

        ring = {"i": 0, "pinned": set()}

        def wload(srcs, K, ncols, pin=False):
            while ring["i"] % 3 in ring["pinned"]:
                ring["i"] += 1
            s = ring["i"] % 3
            ring["i"] += 1
            if pin:
                ring["pinned"].add(s)
            kc = K // 128
            view = wt[s][:, 0:kc * ncols].rearrange("p (k n) -> p k n", n=ncols)
            col = 0
            for ap in srcs:
                n_i = ap.shape[1]
                DMA("pool", view[:, :, col:col + n_i], ap.rearrange("(k p) n -> p k n", p=128),
                    reads=[], writes=[b_wt[s]])
                col += n_i
            return view, b_wt[s], s

        gcount = [0]

        def gemm_F(view, wb, kc, ncols, rhs_fn, rhs_bufs, consume, cw=128):
            for c in range((ncols + cw - 1) // cw):
                m = min(cw, ncols - c * cw)
                pi = gcount[0] % 2
                gcount[0] += 1
                for k in range(kc):
                    MM(psG[pi][0:m, :], view[:, k, c * cw:c * cw + m], rhs_fn(k), k == 0, k == kc - 1,
                       reads=[wb] + rhs_bufs, writes=[b_psG[pi]])
                consume(c, psG[pi], b_psG[pi])

        def gemm_T(view, wb, kc, ncols, lhs_fn, lhs_bufs, consume):
            for rt in range(4):
                pi = gcount[0] % 2
                gcount[0] += 1
                for k in range(kc):
                    MM(psG[pi][:, 0:ncols], lhs_fn(k, rt), view[:, k, 0:ncols], k == 0, k == kc - 1,
                       reads=[wb] + lhs_bufs, writes=[b_psG[pi]])
                consume(rt, psG[pi], b_psG[pi])

        tcount = [0]

        def transposes(src_fn, n, dst_fn, src_bufs, dst_bufs, scale_fn=None):
            i = 0
            while i < n:
                cnt = min(4, n - i)
                hb = tcount[0] % 2
                tcount[0] += 1
                base = 0
                psTb = psT[hb]
                for j in range(cnt):
                    TR(psTb[:, base + j * 128:base + (j + 1) * 128], src_fn(i + j), identb,
                       reads=src_bufs + [b_cst], writes=[b_psT[hb]], sig=(j == cnt - 1))
                src = psTb[:, base:base + cnt * 128].rearrange("p (c t) -> p c t", t=128)
                if tcount[0] % 2 == 0:
                    P.op("act", lambda e, o=dst_fn(i, cnt), s_=src: e.copy(o, s_), [b_psT[hb]], dst_bufs)
                else:
                    DV("tensor_copy", dst_fn(i, cnt), src, reads=[b_psT[hb]], writes=dst_bufs)
                i += cnt

        DMA("sp", cb[:, :], cb_d[:, :], [], [b_cst])
        DMA("sp", cf[:, :], cf_d[:, :], [], [b_cst])

        flg = sb("flg", [128, 32])
        b_flg = Buf("flg")
        DMA("sp", flg, flags_d[0:1, :].broadcast_to([128, 32]), [], [b_flg])
        b_outs, b_xc, b_cfin, b_cfout = Buf("outs"), Buf("xc"), Buf("cfin"), Buf("cfout")
        zt = arena[:, 0:512]
        b_zt = Buf("zt")
        DV("memset", zt, 0.0, reads=[], writes=[b_zt])
        DMA("sp", cfB_in[0:128, :], zt, [b_zt], [b_cfin])
        P.barrier()
        b_x = [Buf("x_l0"), Buf("x_l1"), Buf("x_l2")]
        b_mTd, b_od, b_xmd = Buf("mTd"), Buf("od"), Buf("xmd")
        b_dbg = Buf("dbg")

        def dump_dbg(yT, b_yT, nj, mark):
            top[0] = mark
            dtmp = alloc(512)
            b_dtmp = Buf("dtmp")
            for j in range(nj):
                for cc in range(16):
                    DV("tensor_copy", dtmp, yT[j][:, cc, :], reads=[b_yT[j][cc]], writes=[b_dtmp])
                    DMA("sp", dbg_d[j, cc * 128:(cc + 1) * 128, :], dtmp, [b_dtmp], [b_dbg])
            P.barrier()

        def _layers():
          for s in range(n_slots):
            for l in [s % 2]:
                tok0 = s * T
                x_out = out_d
                b_xout = b_outs
                f_x = flg[:, s * 4:s * 4 + 1]
                f_0 = flg[:, s * 4 + 1:s * 4 + 2]
                f_1 = flg[:, s * 4 + 2:s * 4 + 3]
                f_c = flg[:, s * 4 + 3:s * 4 + 4]
                P.barrier()
                top[0] = 0
                if s == 0:
                    DV("memset", Cst[:, :, :], 0.0, reads=[], writes=b_C)
                    DV("memset", Cbf[:, :, :], 0.0, reads=[], writes=b_C)
                    DV("memset", nst[:, :], 0.0, reads=[], writes=b_C)
                    DV("memset", nbf[:, :], 0.0, reads=[], writes=b_C)
                    DV("memset", mst[:, :], 0.0, reads=[], writes=[b_m])
                    for g in range(4):
                        DV("memset", kTd[g][:, 0:128], 0.0, reads=[], writes=[b_kTd[g]])
                    DV("memset", vdup[0][:, :, :], 0.0, reads=[], writes=[b_vdup[0]])
                hT = alloc(8192, BF16).rearrange("p (k t) -> p k t", t=T)
                b_hT = Buf("hT")
                mark_p0 = top[0]
                if s > 0:
                    rC = alloc(4096)
                    b_rC = Buf("rC")
                    rM = alloc(32).rearrange("p (a n) -> p a n", a=2)
                    b_rM = Buf("rM")
                    rB = [alloc(512, BF16) for _ in range(2)]
                    b_rB = Buf("rB")
                    C2 = Cst.rearrange("p j n -> p (j n)")
                    DMA("sp", Cst, cfA_out[0:1024, :].rearrange("(p j) n -> p j n", j=8), [b_cfout], b_C)
                    DMA("sp", rC.rearrange("p (j n) -> p j n", j=8), cfA_out[1024:2048, :].rearrange("(p j) n -> p j n", j=8),
                        [b_cfout], [b_rC])
                    DV("tensor_scalar", C2, C2, f_0, None, ALU.mult, reads=b_C + [b_flg], writes=b_C)
                    DV("scalar_tensor_tensor", C2, rC, f_1, C2, ALU.mult, ALU.add, reads=[b_rC, b_flg] + b_C, writes=b_C)
                    P.op("act", lambda e, o=Cbf.rearrange("p j n -> p (j n)"), i_=C2: e.copy(o, i_), b_C, b_C)
                    DMA("sp", rM[:, 0, :], cfB_out[0:128, 0:16], [b_cfout], [b_rM])
                    DMA("sp", rM[:, 1, :], cfB_out[256:384, 0:16], [b_cfout], [b_rM])
                    DV("tensor_scalar", rM[:, 0, :], rM[:, 0, :], f_0, None, ALU.mult, reads=[b_rM, b_flg], writes=[b_rM])
                    DV("scalar_tensor_tensor", rM[:, 0, :], rM[:, 1, :], f_1, rM[:, 0, :], ALU.mult, ALU.add,
                       reads=[b_rM, b_flg], writes=[b_rM])
                    DV("tensor_copy", nst[:, :], rM[:, 0, 0:8], reads=[b_rM], writes=b_C)
                    DV("tensor_copy", nbf[:, :], rM[:, 0, 0:8], reads=[b_rM], writes=b_C)
                    DV("tensor_copy", mst[0:4, 0:1], rM[0:4, 0, 8:9], reads=[b_rM], writes=[b_m])
                    DMA("sp", rB[0], cfB_out[128:256, :].bitcast(BF16), [b_cfout], [b_rB])
                    DMA("sp", rB[1], cfB_out[384:512, :].bitcast(BF16), [b_cfout], [b_rB])
                    DV("tensor_scalar", rB[0], rB[0], f_0, None, ALU.mult, reads=[b_rB, b_flg], writes=[b_rB])
                    DV("scalar_tensor_tensor", rB[0], rB[1], f_1, rB[0], ALU.mult, ALU.add, reads=[b_rB, b_flg], writes=[b_rB])
                    for g in range(4):
                        DV("tensor_copy", kTd[g][:, 0:128], rB[0][:, g * 128:(g + 1) * 128], reads=[b_rB], writes=[b_kTd[g]])
                    DV("tensor_copy", vdup[0][:, :, :], rB[0][:, 512:1024].rearrange("p (g d) -> p g d", d=128),
                       reads=[b_rB], writes=[b_vdup[0]])
                gbuf = alloc(4096)
                b_gbuf = Buf("gbuf")
                xrow = alloc(4096)
                b_xrow = Buf("xrow")
                xsb = alloc(2048, BF16)
                b_xsb = Buf("xsb")
                xch = alloc(4096)
                b_xch = Buf("xch")
                sm_ss = small[:, 0:1]
                sm_rstd = small[:, 1:2]
                b_ss = smallbuf("ss")
                DMA("sp", gbuf, norm_pre[l:l + 1, :].broadcast_to([128, D]), [], [b_gbuf])
                for rt in range(4):
                    r0 = tok0 + rt * 128
                    DMA("sp", xrow, x_d[r0:r0 + 128, :], [], [b_xrow])
                    if s > 0:
                        DMA("sp", xch, out_d[r0 - T:r0 - T + 128, :], [b_outs], [b_xch])
                        DV("scalar_tensor_tensor", xrow, xch, f_x, xrow, ALU.mult, ALU.add,
                           reads=[b_xch, b_flg, b_xrow], writes=[b_xrow])
                    DMA("sp", xc_d[rt * 128:(rt + 1) * 128, :], xrow, [b_xrow], [b_xc])
                    ACT(xsb, xrow, AF.Square, [b_xrow], [b_xsb, b_ss], accum_out=sm_ss)
                    DV("tensor_scalar", small[:, 2:3], sm_ss, 1.0 / D, EPS, ALU.mult, ALU.add, reads=[b_ss], writes=[b_ss])
                    ACT(sm_rstd, small[:, 2:3], AF.Sqrt, [b_ss], [b_ss]); DV("reciprocal", sm_rstd, sm_rstd, reads=[b_ss], writes=[b_ss])
                    DV("scalar_tensor_tensor", xsb, xrow, sm_rstd, gbuf, ALU.mult, ALU.mult,
                       reads=[b_xrow, b_ss, b_gbuf], writes=[b_xsb])
                    transposes(lambda i: xsb[:, i * 128:(i + 1) * 128], 32,
                               lambda i0, cnt, rt=rt: hT[:, i0:i0 + cnt, rt * 128:(rt + 1) * 128],
                               [b_xsb], [b_hT])
                P.barrier()
                if stop == "p0":
                    raise _Stop()
                top[0] = mark_p0
                hT_fn = lambda k: hT[:, k, :]
                hT_lhs = lambda k, rt: hT[:, k, rt * 128:(rt + 1) * 128]
                W = w_in[l]

                def wcols(c0, n):
                    return W[:, c0:c0 + n]

                yT = [None, None, None]
                b_yT = [None, None, None]
                yT[0] = alloc(4096, BF16).rearrange("p (k t) -> p k t", t=T)
                b_yT[0] = [Buf("yTa%d" % i) for i in range(16)]
                mark_a = top[0]
                GB = alloc(4096).rearrange("p (a n) -> p a n", a=2)
                b_GB = Buf("GB")
                gv = [alloc(1024, BF16) for _ in range(4)]
                b_gv = [Buf("gv%d" % i) for i in range(4)]
                wsraw = alloc(1024).rearrange("p (g t) -> p g t", t=128)
                wsTm = alloc(512, BF16).rearrange("p (g t) -> p g t", t=128)
                b_ws = Buf("ws")
                bsrow = alloc(1024)
                b_bs = Buf("bsrow")
                gu = [alloc(256, BF16) for _ in range(2)]
                sz = [alloc(256, BF16) for _ in range(2)]
                b_gu = [Buf("gu0"), Buf("gu1")]
                b_sz = [Buf("sz0"), Buf("sz1")]
                atmp = alloc(512)
                b_atmp = Buf("atmp")
                junkA = alloc(128, BF16)
                b_junk = Buf("junk")
                s1 = small[:, 8:40]
                s2 = small[:, 40:72]
                b_s12 = smallbuf("s12")
                stA = small[:, 72:96]
                b_stA = smallbuf("stA")
                DMA("sp", GB[:, 0, :], ln_g[l:l + 1, :].broadcast_to([128, 2048]), [], [b_GB])
                DMA("sp", GB[:, 1, :], ln_b[l:l + 1, :].broadcast_to([128, 2048]), [], [b_GB])
                DMA("sp", wsraw, wsT_d[l].rearrange("g s t -> s g t"), [], [b_ws])
                DMA("sp", bsrow[0:1, :], bs_d[l:l + 1, :], [], [b_bs])
                for g in range(8):
                    DV("tensor_tensor", wsTm[:, g, :], wsraw[:, g, :], mask2[:, 0:128], ALU.mult,
                       reads=[b_ws, b_cst], writes=[b_ws])
                for blk in range(8):
                    view, wb, _ = wload([wcols(OFF["a_v"] + blk * 256, 256)], D, 256)

                    def cons_v(rt, ps, bps, blk=blk):
                        sl = gv[rt][:, blk * 256:(blk + 1) * 256]
                        ACT(sl, ps[:, 0:256], AF.Gelu, [bps], [b_gv[rt]])
                        DV("tensor_reduce", s1[:, rt * 8 + blk:rt * 8 + blk + 1], sl, AX.X, ALU.add,
                           reads=[b_gv[rt]], writes=[b_s12])
                        ACT(junkA[:, 0:256], sl, AF.Square, [b_gv[rt]], [b_junk, b_s12],
                            accum_out=s2[:, rt * 8 + blk:rt * 8 + blk + 1])
                    gemm_T(view, wb, 32, 256, hT_lhs, [b_hT], cons_v)
                DV("tensor_reduce", stA[:, 0:4], s1.rearrange("p (r b) -> p r b", b=8), AX.X, ALU.add,
                   reads=[b_s12], writes=[b_stA])
                DV("tensor_reduce", stA[:, 4:8], s2.rearrange("p (r b) -> p r b", b=8), AX.X, ALU.add,
                   reads=[b_s12], writes=[b_stA])
                DV("tensor_scalar", stA[:, 8:12], stA[:, 0:4], 1.0 / 2048, None, ALU.mult, reads=[b_stA], writes=[b_stA])
                DV("tensor_tensor", stA[:, 12:16], stA[:, 8:12], stA[:, 8:12], ALU.mult, reads=[b_stA], writes=[b_stA])
                DV("scalar_tensor_tensor", stA[:, 12:16], stA[:, 4:8], 1.0 / 2048, stA[:, 12:16], ALU.mult, ALU.subtract,
                   reads=[b_stA], writes=[b_stA])
                DV("tensor_scalar", stA[:, 12:16], stA[:, 12:16], 0.0, EPS, ALU.max, ALU.add, reads=[b_stA], writes=[b_stA]); ACT(stA[:, 16:20], stA[:, 12:16], AF.Sqrt, [b_stA], [b_stA]); DV("reciprocal", stA[:, 16:20], stA[:, 16:20], reads=[b_stA], writes=[b_stA])
                DV("scalar_tensor_tensor", stA[:, 20:24], stA[:, 8:12], -1.0, stA[:, 16:20], ALU.mult, ALU.mult,
                   reads=[b_stA], writes=[b_stA])
                for rt in range(4):
                    DV("tensor_scalar", gv[rt], gv[rt], stA[:, 16 + rt:17 + rt], stA[:, 20 + rt:21 + rt],
                       ALU.mult, ALU.add, reads=[b_gv[rt], b_stA], writes=[b_gv[rt]])
                    DV("tensor_tensor", gv[rt], gv[rt], GB[:, 0, :], ALU.mult, reads=[b_gv[rt], b_GB], writes=[b_gv[rt]])
                    DV("tensor_tensor", gv[rt], gv[rt], GB[:, 1, :], ALU.add, reads=[b_gv[rt], b_GB], writes=[b_gv[rt]])
                for blk in range(8):
                    view, wb, _ = wload([wcols(OFF["a_u"] + blk * 256, 256)], D, 256)

                    def cons_u(c, ps, bps):
                        ACT(gu[c], ps, AF.Gelu, [bps], [b_gu[c]])
                    gemm_F(view, wb, 32, 256, hT_fn, [b_hT], cons_u)
                    view, wb, _ = wload([wcols(OFF["a_z"] + blk * 256, 256)], D, 256)

                    def cons_z(c, ps, bps):
                        ACT(sz[c], ps, AF.Silu, [bps], [b_sz[c]])
                    gemm_F(view, wb, 32, 256, hT_fn, [b_hT], cons_z)
                    for c in range(2):
                        cc = blk * 2 + c
                        for rt in range(4):
                            MM(pg[0][:, rt * 128:(rt + 1) * 128], gv[rt][:, cc * 128:(cc + 1) * 128], wsTm[:, blk, :],
                               True, False, reads=[b_gv[rt], b_ws], writes=[b_pg[0]], sig=False)
                            MM(pg[0][:, rt * 128:(rt + 1) * 128], onesf[0:1, 0:128], bsrow[0:1, blk * 128:(blk + 1) * 128],
                               False, True, reads=[b_bs, b_cst], writes=[b_pg[0]], sig=(rt == 3))
                        DV("tensor_tensor", atmp, pg[0], gu[c], ALU.mult, reads=[b_pg[0], b_gu[c]], writes=[b_atmp])
                        DV("tensor_tensor", yT[0][:, cc, :], atmp, sz[c], ALU.mult, reads=[b_atmp, b_sz[c]],
                           writes=[b_yT[0][cc]])
                P.barrier()
                if stop == "a":
                    dump_dbg(yT, b_yT, 1, mark_a)
                    raise _Stop()
                top[0] = mark_a

                yT[1] = alloc(4096, BF16).rearrange("p (k t) -> p k t", t=T)
                b_yT[1] = [Buf("yTb%d" % i) for i in range(16)]
                mark_b = top[0]
                posb = alloc(512).bitcast(I32)
                angf = alloc(512)
                b_pos = Buf("pos")
                esink = small[:, 96:128]
                b_esink = smallbuf("esink")
                kraw = [alloc(256, BF16) for _ in range(2)]
                b_kraw = [Buf("kraw0"), Buf("kraw1")]
                qraw = [alloc(256, BF16) for _ in range(2)]
                b_qraw = [Buf("qraw0"), Buf("qraw1")]
                qT = [alloc(256, BF16) for _ in range(2)]
                b_qT = [Buf("qT0"), Buf("qT1")]
                t1 = alloc(512)
                t2 = alloc(512)
                b_t1, b_t2 = Buf("t1"), Buf("t2")
                vtmp = alloc(128, BF16)
                b_vtmp = Buf("vtmp")
                PT = [alloc(128, BF16) for _ in range(2)]
                b_PT = [Buf("PT0"), Buf("PT1")]
                rtmp = alloc(512)
                b_rtmp = Buf("rtmp")
                szb = [alloc(256, BF16) for _ in range(2)]
                b_szb = [Buf("szb0"), Buf("szb1")]
                DMA("sp", posb, pos_d[0:1, tok0:tok0 + T].broadcast_to([128, T]), [], [b_pos])
                DV("tensor_copy", angf, posb, reads=[b_pos], writes=[b_pos])
                DV("tensor_scalar", angf, angf, invf, None, ALU.mult, reads=[b_pos, b_cst], writes=[b_pos])
                C1 = 6.28125
                C2 = 2 * math.pi - 6.28125
                kint = posb
                DV("tensor_scalar", t1, angf, 1.0 / (2 * math.pi), None, ALU.mult, reads=[b_pos], writes=[b_t1])
                DV("tensor_copy", kint, t1, reads=[b_t1], writes=[b_pos])
                DV("tensor_copy", t2, kint, reads=[b_pos], writes=[b_t2])
                DV("scalar_tensor_tensor", t1, t2, -C1, angf, ALU.mult, ALU.add, reads=[b_t2, b_pos], writes=[b_t1])
                DV("scalar_tensor_tensor", t1, t2, -C2, t1, ALU.mult, ALU.add, reads=[b_t2, b_t1], writes=[b_t1])
                DV("tensor_scalar", t2, t1, math.pi, 2 * math.pi, ALU.is_gt, ALU.mult, reads=[b_t1], writes=[b_t2])
                DV("tensor_tensor", t1, t1, t2, ALU.subtract, reads=[b_t1, b_t2], writes=[b_t1])
                DV("tensor_scalar", t2, t1, -math.pi, 2 * math.pi, ALU.is_lt, ALU.mult, reads=[b_t1], writes=[b_t2])
                DV("tensor_tensor", t1, t1, t2, ALU.add, reads=[b_t1, b_t2], writes=[b_t1])
                DV("tensor_scalar", angf, t1, 0.5 * math.pi, None, ALU.add, reads=[b_t1], writes=[b_pos])
                DV("tensor_scalar", t2, angf, math.pi, 2 * math.pi, ALU.is_gt, ALU.mult, reads=[b_pos], writes=[b_t2])
                DV("tensor_tensor", t2, angf, t2, ALU.subtract, reads=[b_pos, b_t2], writes=[b_t2])
                ACT(sinF, t1, AF.Sin, [b_t1], [b_cs])
                ACT(cosF, t2, AF.Sin, [b_t2], [b_cs])
                DMA("sp", esink, sinks_d[l:l + 1, :].broadcast_to([128, 32]), [], [b_esink])
                ACT(esink, esink, AF.Exp, [b_esink], [b_esink])
                view, wb, _ = wload([wcols(OFF["b_k"], 256)], D, 256)

                def cons_k(c, ps, bps):
                    ACT(kraw[c], ps, AF.Copy, [bps], [b_kraw[c]])
                gemm_F(view, wb, 32, 256, hT_fn, [b_hT], cons_k)
                for g in range(4):
                    MM(pg[0], DupT[g % 2], kraw[g // 2], True, True, reads=[b_kraw[g // 2], b_cst], writes=[b_pg[0]])
                    MM(pg[1], DupRT[g % 2], kraw[g // 2], True, True, reads=[b_kraw[g // 2], b_cst], writes=[b_pg[1]])
                    DV("tensor_tensor", t1, pg[0], cosF, ALU.mult, reads=[b_pg[0], b_cs], writes=[b_t1])
                    DV("tensor_tensor", t2, pg[1], sinF, ALU.mult, reads=[b_pg[1], b_cs], writes=[b_t2])
                    DV("tensor_tensor", kTd[g][:, 128:640], t1, t2, ALU.add, reads=[b_t1, b_t2], writes=[b_kTd[g]])
                view, wb, _ = wload([wcols(OFF["b_v"], 256)], D, 256)

                def cons_vb(rt, ps, bps):
                    ACT(vtmp, ps[:, 0:256], AF.Copy, [bps], [b_vtmp])
                    v3 = vtmp.rearrange("p (g d) -> p g d", d=64)
                    DV("tensor_copy", vdup[rt + 1][:, :, 0:64], v3, reads=[b_vtmp], writes=[b_vdup[rt + 1]])
                    DV("tensor_copy", vdup[rt + 1][:, :, 64:128], v3, reads=[b_vtmp], writes=[b_vdup[rt + 1]])
                gemm_T(view, wb, 32, 256, hT_lhs, [b_hT], cons_vb)
                sc_cnt = [0]
                for qb in range(8):
                    view, wb, _ = wload([wcols(OFF["b_q"] + qb * 256, 256)], D, 256)

                    def cons_q(c, ps, bps, qb=qb):
                        qc = qb * 2 + c
                        g = qc // 4
                        ACT(qraw[c], ps, AF.Copy, [bps], [b_qraw[c]], scale=0.125)
                        MM(pg[0], RrotT, qraw[c], True, True, reads=[b_qraw[c], b_cst], writes=[b_pg[0]])
                        DV("tensor_tensor", t1, qraw[c], cosF, ALU.mult, reads=[b_qraw[c], b_cs], writes=[b_t1])
                        DV("tensor_tensor", t2, pg[0], sinF, ALU.mult, reads=[b_pg[0], b_cs], writes=[b_t2])
                        DV("tensor_tensor", qT[c], t1, t2, ALU.add, reads=[b_t1, b_t2], writes=[b_qT[c]])
                        steps = [(half, rt) for half in range(2) for rt in range(4)]
                        S2bufs = [(pg[1][:, 0:256], b_pg[1]), (psT[0].bitcast(F32)[:, 0:256], b_psT[0])]

                        def stepA(j):
                            half, rt = steps[j]
                            rows = slice(half * 64, half * 64 + 64)
                            S2, bS2 = S2bufs[j % 2]
                            qsl = qT[c][rows, rt * 128:(rt + 1) * 128]
                            MM(S2[:, 0:128], kTd[g][rows, 128 + rt * 128:256 + rt * 128], qsl, True, True,
                               reads=[b_kTd[g], b_qT[c]], writes=[bS2], sig=False)
                            MM(S2[:, 128:256], kTd[g][rows, rt * 128:rt * 128 + 128], qsl, True, True,
                               reads=[b_kTd[g], b_qT[c]], writes=[bS2], sig=True)

                        def stepB(j):
                            half, rt = steps[j]
                            h = 2 * qc + half
                            rows = slice(half * 64, half * 64 + 64)
                            S2, bS2 = S2bufs[j % 2]
                            si = j % 2
                            ACT(PT[si][:, 0:256], S2[:, 0:256], AF.Exp, [bS2], [b_PT[si]])
                            DV("tensor_tensor", PT[si][:, 0:256], PT[si][:, 0:256], mask2[:, 0:256], ALU.mult,
                               reads=[b_PT[si], b_cst], writes=[b_PT[si]])
                            if rt == 0:
                                DV("tensor_scalar", PT[si][:, 128:256], PT[si][:, 128:256], f_c, None, ALU.mult,
                                   reads=[b_PT[si], b_flg], writes=[b_PT[si]])
                            osl = slice(rt * 128, (rt + 1) * 128)
                            MM(pg[2][:, osl], vdup[rt + 1][:, g, :], PT[si][:, 0:128], True, False,
                               reads=[b_vdup[rt + 1], b_PT[si]], writes=[b_pg[2]], sig=False)
                            MM(pg[2][:, osl], vdup[rt][:, g, :], PT[si][:, 128:256], False, True,
                               reads=[b_vdup[rt], b_PT[si]], writes=[b_pg[2]], sig=False)
                            MM(pg[3][:, osl], onesb, PT[si][:, 0:128], True, False,
                               reads=[b_PT[si], b_cst], writes=[b_pg[3]], sig=False)
                            MM(pg[3][:, osl], onesb, PT[si][:, 128:256], False, True,
                               reads=[b_PT[si], b_cst], writes=[b_pg[3]], sig=True)
                            if rt == 3:
                                DV("tensor_scalar", rtmp[rows, :], pg[3][rows, :], esink[rows, h:h + 1], None, ALU.add,
                                   reads=[b_pg[3], b_esink], writes=[b_rtmp])
                                DV("reciprocal", rtmp[rows, :], rtmp[rows, :], reads=[b_rtmp], writes=[b_rtmp])
                                DV("tensor_tensor", yT[1][rows, qc, :], pg[2][rows, :], rtmp[rows, :], ALU.mult,
                                   reads=[b_pg[2], b_rtmp], writes=[b_yT[1][qc]])
                        stepA(0)
                        for j in range(8):
                            if j + 1 < 8:
                                stepA(j + 1)
                            stepB(j)
                    gemm_F(view, wb, 32, 256, hT_fn, [b_hT], cons_q)
                if s < n_slots - 1:
                    cfb = cfB_in[128:256, :].bitcast(BF16)
                    for g in range(4):
                        DMA("sp", cfb[:, g * 128:(g + 1) * 128], kTd[g][:, 512:640], [b_kTd[g]], [b_cfin])
                    DMA("sp", cfb[:, 512:1024].rearrange("p (g d) -> p g d", d=128), vdup[4][:, :, :], [b_vdup[4]], [b_cfin])
                for blk in range(8):
                    view, wb, _ = wload([wcols(OFF["b_z"] + blk * 256, 256)], D, 256)

                    def cons_zb(c, ps, bps, blk=blk):
                        cc = blk * 2 + c
                        ACT(szb[c], ps, AF.Silu, [bps], [b_szb[c]])
                        DV("tensor_tensor", yT[1][:, cc, :], yT[1][:, cc, :], szb[c], ALU.mult,
                           reads=[b_yT[1][cc], b_szb[c]], writes=[b_yT[1][cc]])
                    gemm_F(view, wb, 32, 256, hT_fn, [b_hT], cons_zb)
                P.barrier()
                if stop == "b":
                    dump_dbg(yT, b_yT, 2, mark_b)
                    raise _Stop()
                top[0] = mark_b

                yT[2] = alloc(4096, BF16).rearrange("p (k t) -> p k t", t=T)
                b_yT[2] = [Buf("yTc%d" % i) for i in range(16)]
                igc = alloc(512)
                lfb = alloc(512)
                bcs = alloc(512)
                Mg = alloc(512)
                b_gates = Buf("gates")
                gsm = alloc(1024)
                b_gsm = Buf("gsm")
                tokT = alloc(64)
                b_tokT = [Buf("tokT%d" % i) for i in range(4)]
                mng = small[:, 128:144]
                b_mng = smallbuf("mng")
                ibfb = small[:, 144:146]
                b_ibfb = smallbuf("ibfb")
                QT = [alloc(256, BF16) for _ in range(2)]
                KT = [alloc(256, BF16) for _ in range(2)]
                b_QT = [Buf("QT0"), Buf("QT1")]
                b_KT = [Buf("KT0"), Buf("KT1")]
                Ktok = [alloc(128, BF16) for _ in range(4)]
                b_Ktok = [Buf("Ktok%d" % i) for i in range(4)]
                Vtok = [alloc(256, BF16) for _ in range(4)]
                b_Vtok = [Buf("Vtok%d" % i) for i in range(4)]
                Dt = alloc(128)
                b_Dt = Buf("Dt")
                St = alloc(64, BF16)
                b_St = Buf("St")
                intra = alloc(512)
                b_intra = Buf("intra")
                num = alloc(512)
                b_num = Buf("num")
                hn2 = [alloc(256, BF16) for _ in range(2)]
                b_hn2 = [Buf("hn0"), Buf("hn1")]
                kw = alloc(128, BF16)
                b_kw = Buf("kw")
                junkC = alloc(256, BF16)
                sgc = [alloc(256, BF16) for _ in range(2)]
                b_sgc = [Buf("sgc0"), Buf("sgc1")]
                csm = small[:, 160:176]
                b_csm = smallbuf("csm")
                DMA("sp", mng, mng_d[l].rearrange("(c p) -> p c", p=128), [], [b_mng], slow=True)
                DMA("sp", ibfb[0:4, 0:1], ib_d[l].rearrange("(h o) -> h o", o=1), [], [b_ibfb], slow=True)
                DMA("sp", ibfb[0:4, 1:2], fb_d[l].rearrange("(h o) -> h o", o=1), [], [b_ibfb], slow=True)
                DV("tensor_scalar", ibfb[0:4, 0:2], ibfb[0:4, 0:2], 1.0 / 15.0, None, ALU.mult, reads=[b_ibfb], writes=[b_ibfb])
                view, wb, _ = wload([wcols(OFF["c_i"], 8)], D, 8)

                def cons_if(c, ps, bps):
                    dst = igc if c == 0 else lfb
                    ACT(dst[0:4, :], ps[0:4, :], AF.Tanh, [bps, b_ibfb], [b_gates], scale=1.0 / 15.0,
                        bias=ibfb[0:4, c:c + 1])
                gemm_F(view, wb, 32, 8, hT_fn, [b_hT], cons_if, cw=4)
                G4 = lambda a: a[0:4, :]
                DV("tensor_scalar", G4(igc), G4(igc), 15.0, None, ALU.mult, reads=[b_gates], writes=[b_gates])
                ACT(G4(lfb), G4(lfb), AF.Exp, [b_gates], [b_gates], scale=-15.0)
                ACT(G4(lfb), G4(lfb), AF.Ln, [b_gates], [b_gates], bias=1.0)
                DV("tensor_scalar", G4(lfb), G4(lfb), -1.0, None, ALU.mult, reads=[b_gates], writes=[b_gates])
                for ch in range(4):
                    csl = slice(ch * 128, (ch + 1) * 128)
                    DV("tensor_tensor_scan", bcs[0:4, csl], onesf[0:4, 0:128], lfb[0:4, csl], 0.0, ALU.mult, ALU.add,
                       reads=[b_gates, b_cst], writes=[b_gates])
                DV("tensor_tensor", G4(igc), G4(igc), G4(bcs), ALU.subtract, reads=[b_gates], writes=[b_gates])
                selrow = lambda h: cf[0:4, 385 + h * 128:385 + (h + 1) * 128]
                for ch in range(4):
                    csl = slice(ch * 128, (ch + 1) * 128)
                    last = ch * 128 + 127
                    gA = gsm[0:4, 0:128]
                    gE = gsm[0:4, 128:256]
                    gW = gsm[0:4, 256:384]
                    gNM = gsm[0:4, 384 + ch * 128:512 + ch * 128]
                    gml = gsm[0:4, 900:901]
                    gdiag = gsm[0:4, 904:908]
                    DV("tensor_tensor_scan", Mg[0:4, csl], igc[0:4, csl], igc[0:4, csl], mst[0:4, 0:1], ALU.max, ALU.max,
                       reads=[b_gates, b_m], writes=[b_gates])
                    ACT(gA, Mg[0:4, csl], AF.Exp, [b_gates, b_m], [b_gsm], scale=-1.0, bias=mst[0:4, 0:1])
                    DV("tensor_tensor", gE, bcs[0:4, csl], Mg[0:4, csl], ALU.add, reads=[b_gates], writes=[b_gsm])
                    ACT(gE, gE, AF.Exp, [b_gsm], [b_gsm], scale=-1.0)
                    DV("tensor_scalar", gml, Mg[0:4, last:last + 1], -1.0, None, ALU.mult, reads=[b_gates], writes=[b_gsm])
                    ACT(gW, igc[0:4, csl], AF.Exp, [b_gates, b_gsm], [b_gsm], bias=gml)
                    DV("tensor_scalar", gNM, Mg[0:4, csl], -1.0, None, ALU.mult, reads=[b_gates], writes=[b_gsm])
                    DV("tensor_scalar", gdiag, identf[0:4, 0:4], gA[:, 127:128], None, ALU.mult,
                       reads=[b_gsm, b_cst], writes=[b_gsm])
                    DV("tensor_tensor", mst[0:4, 0:1], bcs[0:4, last:last + 1], Mg[0:4, last:last + 1], ALU.add,
                       reads=[b_gates, b_gsm], writes=[b_m])
                    px = pg[0][:, 384:400]
                    TR(px[:, 0:4], gA, identf[0:4, 0:4], reads=[b_gsm, b_cst], writes=[b_pg[0]], sig=False)
                    TR(px[:, 4:8], gE, identf[0:4, 0:4], reads=[b_gsm, b_cst], writes=[b_pg[0]], sig=False)
                    TR(px[:, 8:12], gW, identf[0:4, 0:4], reads=[b_gsm, b_cst], writes=[b_pg[0]], sig=False)
                    MM(px[:, 12:16], onesf[0:4, 0:128], gdiag, True, True, reads=[b_gsm, b_cst], writes=[b_pg[0]])
                    tk = tokT[:, ch * 16:(ch + 1) * 16]
                    ACT(tk, px, AF.Copy, [b_pg[0]], [b_tokT[ch]])
                for hd in range(4):
                    view, wb, _ = wload([wcols(OFF["c_q"] + hd * 256, 256)], D, 256)

                    def cons_cq(c, ps, bps):
                        ACT(QT[c], ps, AF.Copy, [bps], [b_QT[c]], scale=1.0 / 16.0)
                    gemm_F(view, wb, 32, 256, hT_fn, [b_hT], cons_cq)
                    view, wb, _ = wload([wcols(OFF["c_k"] + hd * 256, 256)], D, 256)

                    def cons_ck(c, ps, bps):
                        ACT(KT[c], ps, AF.Copy, [bps], [b_KT[c]])
                    gemm_F(view, wb, 32, 256, hT_fn, [b_hT], cons_ck)
                    for ch in range(4):
                        transposes(lambda i, ch=ch: KT[i][:, ch * 128:(ch + 1) * 128], 2,
                                   lambda i0, cnt, ch=ch: Ktok[ch].rearrange("p (c d) -> p c d", d=128)[:, i0:i0 + cnt, :],
                                   b_KT, [b_Ktok[ch]])
                    for vb in range(2):
                        view, wb, _ = wload([wcols(OFF["c_v"] + hd * 512 + vb * 256, 256)], D, 256)

                        def cons_cv(rt, ps, bps, vb=vb):
                            ACT(Vtok[rt][:, vb * 256:(vb + 1) * 256], ps[:, 0:256], AF.Copy, [bps], [b_Vtok[rt]])
                        gemm_T(view, wb, 32, 256, hT_lhs, [b_hT], cons_cv)
                    bC = b_C[hd]
                    pending_tail = [None]
                    for ch in range(4):
                        csl = slice(ch * 128, (ch + 1) * 128)
                        hn, b_hn = hn2[ch % 2], b_hn2[ch % 2]
                        tk = tokT[:, ch * 16:(ch + 1) * 16]
                        a_col = tk[:, hd:hd + 1]
                        e_col = tk[:, 4 + hd:5 + hd]
                        w_col = tk[:, 8 + hd:9 + hd]
                        d_col = tk[:, 12 + hd:13 + hd]
                        gNM = gsm[0:4, 384 + ch * 128:512 + ch * 128]
                        pL = pg[0][:, 0:128]
                        pS = pg[0][:, 128:256]
                        psm = pg[0][:, 256:264]
                        MM(pL, igc[0:4, csl], selrow(hd), True, False, reads=[b_gates, b_cst], writes=[b_pg[0]], sig=False)
                        MM(pL, selrow(hd), gNM, False, False, reads=[b_gsm, b_cst], writes=[b_pg[0]], sig=False)
                        MM(pL, identf, mb128, False, True, reads=[b_cst], writes=[b_pg[0]], sig=True)
                        ACT(Dt, pL, AF.Exp, [b_pg[0]], [b_Dt])
                        for dc in range(2):
                            MM(pS, KT[dc][:, csl], QT[dc][:, csl], dc == 0, dc == 1, reads=[b_KT[dc], b_QT[dc]],
                               writes=[b_pg[0]])
                        DV("tensor_tensor", St, pS, Dt, ALU.mult, reads=[b_pg[0], b_Dt], writes=[b_St])
                        MM(pg[1], St, Vtok[ch], True, True, reads=[b_St, b_Vtok[ch]], writes=[b_pg[1]])
                        MM(psm[:, 0:1], St, onesb[:, 0:1], True, True, reads=[b_St, b_cst], writes=[b_pg[0]], sig=False)
                        for dc in range(2):
                            MM(pg[2], QT[dc][:, csl], Cbf[:, hd * 2 + dc, :], dc == 0, dc == 1,
                               reads=[b_QT[dc], bC], writes=[b_pg[2]])
                        for dc in range(2):
                            MM(psm[:, 1:2], QT[dc][:, csl], nbf[:, hd * 2 + dc:hd * 2 + dc + 1], dc == 0, dc == 1,
                               reads=[b_QT[dc], bC], writes=[b_pg[0]])
                        DV("tensor_scalar", kw, Ktok[ch], w_col, None, ALU.mult, reads=[b_Ktok[ch], b_tokT[ch]], writes=[b_kw])
                        for dc in range(2):
                            MM(pg[3], kw[:, dc * 128:(dc + 1) * 128], Vtok[ch], True, True,
                               reads=[b_kw, b_Vtok[ch]], writes=[b_pg[3]])
                            DV("scalar_tensor_tensor", Cst[:, hd * 2 + dc, :], Cst[:, hd * 2 + dc, :], d_col, pg[3],
                               ALU.mult, ALU.add, reads=[b_pg[3], b_tokT[ch], bC], writes=[bC])
                            P.op("act", lambda e, o=Cbf[:, hd * 2 + dc, :], i_=Cst[:, hd * 2 + dc, :]: e.copy(o, i_), [bC], [bC])
                        for dc in range(2):
                            MM(psm[:, 2 + dc:3 + dc], kw[:, dc * 128:(dc + 1) * 128], onesb[:, 0:1], True, True,
                               reads=[b_kw, b_cst], writes=[b_pg[0]])
                            DV("scalar_tensor_tensor", nst[:, hd * 2 + dc:hd * 2 + dc + 1], nst[:, hd * 2 + dc:hd * 2 + dc + 1],
                               d_col, psm[:, 2 + dc:3 + dc], ALU.mult, ALU.add, reads=[b_pg[0], b_tokT[ch], bC], writes=[bC])
                        DV("tensor_copy", nbf[:, hd * 2:hd * 2 + 2], nst[:, hd * 2:hd * 2 + 2], reads=[bC], writes=[bC])
                        ACT(intra, pg[1], AF.Copy, [b_pg[1]], [b_intra])
                        DV("scalar_tensor_tensor", num, pg[2], a_col, intra, ALU.mult, ALU.add,
                           reads=[b_pg[2], b_intra, b_tokT[ch]], writes=[b_num])
                        ACT(csm[:, 0:1], psm[:, 0:1], AF.Copy, [b_pg[0]], [b_csm])
                        DV("scalar_tensor_tensor", csm[:, 1:2], psm[:, 1:2], a_col, csm[:, 0:1], ALU.mult, ALU.add,
                           reads=[b_pg[0], b_csm, b_tokT[ch]], writes=[b_csm])
                        ACT(csm[:, 1:2], csm[:, 1:2], AF.Abs, [b_csm], [b_csm])
                        DV("tensor_tensor", csm[:, 1:2], csm[:, 1:2], e_col, ALU.max, reads=[b_csm, b_tokT[ch]], writes=[b_csm])
                        DV("reciprocal", csm[:, 2:3], csm[:, 1:2], reads=[b_csm], writes=[b_csm])
                        ACT(junkC, num, AF.Square, [b_num, b_csm], [b_junk, b_csm], scale=csm[:, 2:3], accum_out=csm[:, 3:4])
                        DV("tensor_scalar", csm[:, 6:7], csm[:, 3:4], 1.0 / 512, EPS, ALU.mult, ALU.add, reads=[b_csm], writes=[b_csm])
                        ACT(csm[:, 4:5], csm[:, 6:7], AF.Sqrt, [b_csm], [b_csm]); DV("reciprocal", csm[:, 4:5], csm[:, 4:5], reads=[b_csm], writes=[b_csm])
                        DV("tensor_tensor", csm[:, 5:6], csm[:, 2:3], csm[:, 4:5], ALU.mult, reads=[b_csm], writes=[b_csm])
                        DV("tensor_scalar", hn, num, csm[:, 5:6], None, ALU.mult, reads=[b_num, b_csm], writes=[b_hn])
                        def tail_fn(ch=ch, csl=csl, hn=hn, b_hn=b_hn):
                            for vc in range(4):
                                hb = tcount[0] % 2
                                tcount[0] += 1
                                pt = psT[hb][:, 0:128]
                                TR(pt, hn[:, vc * 128:(vc + 1) * 128], identb, reads=[b_hn, b_cst], writes=[b_psT[hb]])
                                DV("tensor_scalar", yT[2][:, hd * 4 + vc, csl], pt, mng[:, hd * 4 + vc:hd * 4 + vc + 1], None,
                                   ALU.mult, reads=[b_psT[hb], b_mng], writes=[b_yT[2][hd * 4 + vc]])
                        if pending_tail[0] is not None:
                            pending_tail[0]()
                        pending_tail[0] = tail_fn
                    if pending_tail[0] is not None:
                        pending_tail[0]()
                        pending_tail[0] = None
                    for (off, func) in ((OFF["c_o"], AF.Sigmoid), (OFF["c_z"], AF.Silu)):
                        for vb in range(2):
                            view, wb, _ = wload([wcols(off + hd * 512 + vb * 256, 256)], D, 256)

                            def cons_oz(c, ps, bps, vb=vb, func=func):
                                cc = hd * 4 + vb * 2 + c
                                ACT(sgc[c], ps, func, [bps], [b_sgc[c]])
                                DV("tensor_tensor", yT[2][:, cc, :], yT[2][:, cc, :], sgc[c], ALU.mult,
                                   reads=[b_yT[2][cc], b_sgc[c]], writes=[b_yT[2][cc]])
                            gemm_F(view, wb, 32, 256, hT_fn, [b_hT], cons_oz)
                if s < n_slots - 1:
                    DMA("sp", cfA_in[:, :].rearrange("(p j) n -> p j n", j=8), Cst, b_C, [b_cfin])
                    DMA("sp", cfB_in[0:128, 0:8], nst, b_C, [b_cfin])
                    DMA("sp", cfB_in[0:4, 8:9], mst[0:4, 0:1], [b_m], [b_cfin], slow=True)
                    if use_cc:
                        P.coll(lambda e: e.collective_compute("AllGather", ALU.bypass, replica_groups=GROUPS,
                                                              ins=[cfA_in_t.ap().opt()], outs=[cfA_out_t.ap().opt()]),
                               [b_cfin], [b_cfout])
                        P.coll(lambda e: e.collective_compute("AllGather", ALU.bypass, replica_groups=GROUPS,
                                                              ins=[cfB_in_t.ap().opt()], outs=[cfB_out_t.ap().opt()]),
                               [b_cfin], [b_cfout])
                    else:
                        DMA("sp", cfA_out[0:1024, :], cfA_in[:, :], [b_cfin], [b_cfout])
                        DMA("sp", cfA_out[1024:2048, :], cfA_in[:, :], [b_cfin], [b_cfout])
                        DMA("sp", cfB_out[0:256, :], cfB_in[:, :], [b_cfin], [b_cfout])
                        DMA("sp", cfB_out[256:512, :], cfB_in[:, :], [b_cfin], [b_cfout])
                P.barrier()
                if debug and s == 0:
                    top[0] = mark_b + 4096
                    dtmp = alloc(512)
                    b_dtmp = Buf("dtmp")
                    for j in range(3):
                        for cc in range(16):
                            DV("tensor_copy", dtmp, yT[j][:, cc, :], reads=[b_yT[j][cc]], writes=[b_dtmp])
                            DMA("sp", dbg_d[j, cc * 128:(cc + 1) * 128, :], dtmp, [b_dtmp], [b_dbg])
                    P.barrier()
                top[0] = mark_b + 4096

                sg = [[alloc(512) for _ in range(2)] for _ in range(2)]
                b_sg = [[Buf("sg%d%d" % (a, c)) for c in range(2)] for a in range(2)]
                acc = [alloc(512) for _ in range(2)]
                b_acc = [Buf("acc0"), Buf("acc1")]
                mtmp = [alloc(512) for _ in range(2)]
                b_mtmp = [Buf("mtmp0"), Buf("mtmp1")]
                mo = [alloc(256, BF16) for _ in range(2)]
                b_mo = [Buf("mo0"), Buf("mo1")]
                for db in range(16):
                    for j in range(3):
                        view, wb, _ = wload([wcols(OFF["g"] + j * D + db * 256, 256)], D, 256)

                        def cons_g(c, ps, bps, j=j):
                            ACT(sg[j % 2][c], ps, AF.Sigmoid, [bps], [b_sg[j % 2][c]])
                        gemm_F(view, wb, 32, 256, hT_fn, [b_hT], cons_g)
                        view, wb, _ = wload([w_branch[l, j][:, db * 256:(db + 1) * 256]], 2048, 256)

                        def cons_br(c, ps, bps, j=j, db=db):
                            s_ = sg[j % 2][c]
                            bs_ = b_sg[j % 2][c]
                            if j == 0:
                                DV("tensor_tensor", acc[c], ps, s_, ALU.mult, reads=[bps, bs_], writes=[b_acc[c]])
                            elif j == 1:
                                DV("tensor_tensor", mtmp[c], ps, s_, ALU.mult, reads=[bps, bs_], writes=[b_mtmp[c]])
                                DV("tensor_tensor", acc[c], acc[c], mtmp[c], ALU.add, reads=[b_acc[c], b_mtmp[c]],
                                   writes=[b_acc[c]])
                            else:
                                DV("tensor_tensor", mtmp[c], ps, s_, ALU.mult, reads=[bps, bs_], writes=[b_mtmp[c]])
                                DV("tensor_tensor", mo[c], acc[c], mtmp[c], ALU.add, reads=[b_acc[c], b_mtmp[c]],
                                   writes=[b_mo[c]])
                                r0 = (db * 2 + c) * 128
                                DMA("sp", mT_d[r0:r0 + 128, :], mo[c], [b_mo[c]], [b_mTd])
                        gemm_F(view, wb, 16, 256, lambda k, j=j: yT[j][:, k, :], b_yT[j], cons_br)
                P.barrier()
                if stop == "p2":
                    raise _Stop()
                top[0] = 0

                mT = alloc(8192, BF16).rearrange("p (k t) -> p k t", t=T)
                b_mT = Buf("mT")
                gbuf = alloc(4096)
                b_gbuf = Buf("gbuf2")
                orow = alloc(2048)
                xrow2 = alloc(2048)
                b_orow, b_xrow2 = Buf("orow"), Buf("xrow2")
                xmb = alloc(1024, BF16)
                b_xmb = Buf("xmb")
                otmp = [alloc(256) for _ in range(2)]
                b_otmp = [Buf("otmp0"), Buf("otmp1")]
                junk3 = alloc(128, BF16)
                so = small[:, 176:240]
                b_so = smallbuf("so")
                st3 = small[:, 240:248]
                b_st3 = smallbuf("st3")
                DMA("sp", mT, mT_d.rearrange("(k p) t -> p k t", p=128), [b_mTd], [b_mT])
                DMA("sp", gbuf, norm_post[l:l + 1, :].broadcast_to([128, D]), [], [b_gbuf])
                oc = [0]
                for blk in range(16):
                    view, wb, _ = wload([w_out[l][:, blk * 256:(blk + 1) * 256]], D, 256)

                    def cons_o(rt, ps, bps, blk=blk):
                        oi = oc[0] % 2
                        oc[0] += 1
                        ACT(junk3[:, 0:256], ps[:, 0:256], AF.Square, [bps], [b_junk, b_so],
                            accum_out=so[:, rt * 16 + blk:rt * 16 + blk + 1])
                        ACT(otmp[oi], ps[:, 0:256], AF.Copy, [bps], [b_otmp[oi]])
                        DMA("sp", o_d[rt * 128:(rt + 1) * 128, blk * 256:(blk + 1) * 256], otmp[oi], [b_otmp[oi]], [b_od])
                    gemm_T(view, wb, 32, 256, lambda k, rt: mT[:, k, rt * 128:(rt + 1) * 128], [b_mT], cons_o)
                DV("tensor_reduce", st3[:, 0:4], so.rearrange("p (r b) -> p r b", b=16), AX.X, ALU.add,
                   reads=[b_so], writes=[b_st3])
                DV("tensor_scalar", st3[:, 4:8], st3[:, 0:4], 1.0 / D, EPS, ALU.mult, ALU.add, reads=[b_st3], writes=[b_st3])
                ACT(st3[:, 0:4], st3[:, 4:8], AF.Sqrt, [b_st3], [b_st3]); DV("reciprocal", st3[:, 0:4], st3[:, 0:4], reads=[b_st3], writes=[b_st3])
                for rt in range(4):
                    r0 = tok0 + rt * 128
                    for half in range(2):
                        hs = slice(half * 2048, (half + 1) * 2048)
                        DMA("sp", orow, o_d[rt * 128:(rt + 1) * 128, hs], [b_od], [b_orow])
                        DMA("sp", xrow2, xc_d[rt * 128:(rt + 1) * 128, hs], [b_xc], [b_xrow2])
                        DV("scalar_tensor_tensor", orow, orow, st3[:, rt:rt + 1], gbuf[:, hs], ALU.mult, ALU.mult,
                           reads=[b_orow, b_st3, b_gbuf], writes=[b_orow])
                        DV("tensor_tensor", orow, orow, xrow2, ALU.add, reads=[b_orow, b_xrow2], writes=[b_orow])
                        DMA("sp", xm_d[rt * 128:(rt + 1) * 128, hs], orow, [b_orow], [b_xmd])
                        ACT(xmb, orow, AF.Copy, [b_orow], [b_xmb])
                        transposes(lambda i: xmb[:, i * 128:(i + 1) * 128], 16,
                                   lambda i0, cnt, rt=rt, half=half: mT[:, half * 16 + i0:half * 16 + i0 + cnt, rt * 128:(rt + 1) * 128],
                                   [b_xmb], [b_mT])
                prow = alloc(256)
                b_prow = Buf("prow")
                pbf = alloc(128, BF16)
                b_pbf = Buf("pbf")
                pT = alloc(512, BF16).rearrange("p (k t) -> p k t", t=T)
                b_pT = Buf("pT")
                sgt = [alloc(256) for _ in range(2)]
                b_sgt = [Buf("sgt0"), Buf("sgt1")]
                et = [alloc(256) for _ in range(2)]
                b_et = [Buf("et0"), Buf("et1")]
                xmt = [alloc(256) for _ in range(2)]
                b_xmt = [Buf("xmt0"), Buf("xmt1")]
                se = small[:, 176:240]
                b_se = b_so
                st4 = small[:, 248:256]
                b_st4 = smallbuf("st4")
                DMA("sp", gbuf, ple_norm[l:l + 1, :].broadcast_to([128, D]), [], [b_gbuf])
                for rt in range(4):
                    r0 = tok0 + rt * 128
                    DMA("sp", prow, p_d[r0:r0 + 128, :], [], [b_prow])
                    ACT(pbf, prow, AF.Copy, [b_prow], [b_pbf])
                    transposes(lambda i: pbf[:, i * 128:(i + 1) * 128], 2,
                               lambda i0, cnt, rt=rt: pT[:, i0:i0 + cnt, rt * 128:(rt + 1) * 128], [b_pbf], [b_pT])
                pview, pwb, pslot = wload([ple_proj[l][:, :]], 256, D, pin=True)
                for blk in range(16):
                    for rt in range(4):
                        pi = gcount[0] % 2
                        gcount[0] += 1
                        for k in range(2):
                            MM(psG[pi][:, 0:256], pT[:, k, rt * 128:(rt + 1) * 128], pview[:, k, blk * 256:(blk + 1) * 256],
                               k == 0, k == 1, reads=[b_pT, pwb], writes=[b_psG[pi]])
                        ACT(junk3[:, 0:256], psG[pi][:, 0:256], AF.Square, [b_psG[pi]], [b_junk, b_se],
                            accum_out=se[:, rt * 16 + blk:rt * 16 + blk + 1])
                DV("tensor_reduce", st4[:, 0:4], se.rearrange("p (r b) -> p r b", b=16), AX.X, ALU.add,
                   reads=[b_se], writes=[b_st4])
                DV("tensor_scalar", st4[:, 4:8], st4[:, 0:4], 1.0 / D, EPS, ALU.mult, ALU.add, reads=[b_st4], writes=[b_st4])
                ACT(st4[:, 0:4], st4[:, 4:8], AF.Sqrt, [b_st4], [b_st4]); DV("reciprocal", st4[:, 0:4], st4[:, 0:4], reads=[b_st4], writes=[b_st4])
                ec = [0]
                for blk in range(16):
                    view, wb, _ = wload([ple_gate[l][:, blk * 256:(blk + 1) * 256]], D, 256)

                    def cons_pg(rt, ps, bps, blk=blk):
                        ei = ec[0] % 2
                        ec[0] += 1
                        r0 = tok0 + rt * 128
                        bsl = slice(blk * 256, (blk + 1) * 256)
                        ACT(sgt[ei], ps[:, 0:256], AF.Sigmoid, [bps], [b_sgt[ei]])
                        DMA("sp", xmt[ei], xm_d[rt * 128:(rt + 1) * 128, bsl], [b_xmd], [b_xmt[ei]])
                        for k in range(2):
                            MM(pg[0][:, 0:256], pT[:, k, rt * 128:(rt + 1) * 128], pview[:, k, bsl], k == 0, k == 1,
                               reads=[b_pT, pwb], writes=[b_pg[0]])
                        DV("scalar_tensor_tensor", et[ei], pg[0][:, 0:256], st4[:, rt:rt + 1], gbuf[:, bsl], ALU.mult, ALU.mult,
                           reads=[b_pg[0], b_st4, b_gbuf], writes=[b_et[ei]])
                        DV("tensor_tensor", et[ei], et[ei], sgt[ei], ALU.mult, reads=[b_et[ei], b_sgt[ei]], writes=[b_et[ei]])
                        DV("tensor_tensor", et[ei], et[ei], xmt[ei], ALU.add, reads=[b_et[ei], b_xmt[ei]], writes=[b_et[ei]])
                        DMA("sp", x_out[r0:r0 + 128, bsl], et[ei], [b_et[ei]], [b_xout])
                    gemm_T(view, wb, 32, 256, lambda k, rt: mT[:, k, rt * 128:(rt + 1) * 128], [b_mT], cons_pg)
                ring["pinned"].discard(pslot)
        try:
            _layers()
        except _Stop:
            pass
        P.barrier()
        outs = [b_outs]
        if debug:
            outs.append(b_dbg)
        P.wait_all("sp", outs)
        P.build(st)
    return nc


_CACHE = {}


def _prep_weights(inputs):
    cbf, cff = make_consts()
    w = {}
    for k in ("norm_pre", "w_in", "gmlp_ln_g", "gmlp_ln_b", "attn_sinks", "mlstm_ib", "mlstm_fb",
              "mlstm_norm_g", "w_branch", "w_out", "norm_post", "ple_proj", "ple_norm", "ple_gate"):
        w[k] = np.ascontiguousarray(np.asarray(inputs[k], dtype=np.float32))
    w["gmlp_wsT"] = np.ascontiguousarray(np.transpose(np.asarray(inputs["gmlp_ws"], np.float32), (0, 1, 3, 2)))
    w["gmlp_bs"] = np.ascontiguousarray(np.asarray(inputs["gmlp_bs"], np.float32).reshape(2, 1024))
    w["cst_bf"] = cbf
    w["cst_f"] = cff
    return w


def _swap_layers(w):
    o = {}
    for k, v in w.items():
        if k in ("cst_bf", "cst_f"):
            o[k] = v
        else:
            o[k] = np.ascontiguousarray(v[::-1])
    return o


def _slot_task(s, r):
    k = s - r
    if 0 <= k <= 3:
        return k % 2, 2 * (k // 2) + r
    return None


def kernel(**inputs):
    x = np.asarray(inputs["x"], np.float32)
    p = np.asarray(inputs["p"], np.float32)
    pos = np.asarray(inputs["positions"], np.int32)
    B, S, _ = x.shape
    assert S == 4 * T and B == 4
    if "nc" not in _CACHE:
        _CACHE["nc"] = build_program()
    nc = _CACHE["nc"]
    w_even = _prep_weights(inputs)
    w_odd = _swap_layers(w_even)
    in_maps = []
    for c in range(2 * B):
        b, r = c // 2, c % 2
        m = dict(w_even if r == 0 else w_odd)
        xs = np.zeros((NS * T, D), np.float32)
        ps = np.zeros((NS * T, 256), np.float32)
        po = np.zeros((1, NS * T), np.int32)
        fl = np.zeros((1, 32), np.float32)
        for s in range(NS):
            task = _slot_task(s, r)
            if task is None:
                continue
            layer, t = task
            sl = slice(s * T, (s + 1) * T)
            tl = slice(t * T, (t + 1) * T)
            if layer == 0:
                xs[sl] = x[b, tl]
            else:
                fl[0, s * 4 + 0] = 1.0
            ps[sl] = p[layer, b, tl]
            po[0, sl] = pos[b, tl]
            if t >= 1:
                fl[0, s * 4 + 3] = 1.0
                fl[0, s * 4 + 1 + (1 - r)] = 1.0
        m["xs"], m["ps"], m["pos"], m["flags"] = xs, ps, po, fl
        in_maps.append(m)
    res = run_bass_kernel_spmd(nc, in_maps, core_ids=list(range(2 * B)))
    out = np.zeros((B, S, D), np.float32)
    for c in range(2 * B):
        b, r = c // 2, c % 2
        o = np.asarray(res.results[c]["outs"], np.float32)
        for s in range(NS):
            task = _slot_task(s, r)
            if task is not None and task[0] == 1:
                t = task[1]
                out[b, t * T:(t + 1) * T] = o[s * T:(s + 1) * T]
    return out
```

```python
import math
from contextlib import ExitStack
import numpy as np
import ml_dtypes
import concourse.bass as bass
import concourse.mybir as mybir
from concourse.bass_utils import run_bass_kernel_spmd

F32 = mybir.dt.float32
BF16 = mybir.dt.bfloat16
I32 = mybir.dt.int32
AF = mybir.ActivationFunctionType
ALU = mybir.AluOpType
AX = mybir.AxisListType
ENGS = ("pe", "act", "dve", "pool", "sp")

D = 4096
T = 512
NIN = 31240
OFF = dict(a_u=0, a_v=2048, a_z=4096, b_q=6144, b_k=8192, b_v=8448, b_z=8704, c_q=10752,
           c_k=11776, c_v=12800, c_i=14848, c_f=14852, c_o=14856, c_z=16904, g=18952)
EPS = 1e-6
NCB = 1152
NCF = 900
ARENA = 29000


class Buf:
    __slots__ = ("name", "w", "r", "dsem", "dcnt", "fenced")

    def __init__(self, name):
        self.name = name
        self.w = None
        self.r = []
        self.dsem = None
        self.dcnt = 0
        self.fenced = False


class Prog:
    def __init__(self, nc):
        self.nc = nc
        self.ops = {e: [] for e in ENGS}
        self.cnt = {e: 0 for e in ENGS}
        self.sems = {}
        self.sem_keys = list(ENGS)
        self.ndsem = 0
        self.seen = {e: {} for e in ENGS}
        self.dcounts = {}
        self.name2sem = {}
        self.fence = {}
        self.sem_keys.append("cc")

    def _dsem(self, buf):
        if buf.dsem is None:
            if buf.name not in self.name2sem:
                self.name2sem[buf.name] = "d%d" % self.ndsem
                self.ndsem += 1
                self.sem_keys.append(self.name2sem[buf.name])
            buf.dsem = self.name2sem[buf.name]
        return buf.dsem

    def _waits(self, eng, reads, writes):
        need = {}

        def add(tok):
            if tok is None:
                return
            k, v = tok
            if k == eng and eng == "pe":
                return
            if need.get(k, 0) < v:
                need[k] = v
        for b in reads:
            add(b.w)
        for b in writes:
            add(b.w)
            for t in b.r:
                add(t)
            if not b.fenced:
                b.fenced = True
                if b.w is None:
                    for k_, v_ in self.fence.items():
                        add((k_, v_))
        out = []
        seen = self.seen[eng]
        for k, v in need.items():
            if seen.get(k, 0) >= v:
                continue
            seen[k] = v
            out.append((k, v))
        return out

    def op(self, eng, fn, reads=(), writes=(), sig=True):
        waits = self._waits(eng, reads, writes)
        if sig:
            self.cnt[eng] += 1
            tok = (eng, self.cnt[eng])
        else:
            tok = (eng, self.cnt[eng] + 1)
        self.ops[eng].append((waits, fn, (eng, 1) if sig else None))
        for b in reads:
            if len(b.r) > 48:
                mx = {}
                for k_, v_ in b.r:
                    if mx.get(k_, 0) < v_:
                        mx[k_] = v_
                b.r = list(mx.items())
            b.r.append(tok)
        for b in writes:
            b.w = tok
            b.r = []
        return tok

    def dma(self, queue, fn, reads=(), writes=(), dbuf=None):
        if dbuf is None:
            dbuf = writes[0]
        waits = self._waits(queue, reads, writes)
        key = self._dsem(dbuf)
        self.dcounts[key] = self.dcounts.get(key, 0) + 16
        tok = (key, self.dcounts[key])
        self.ops[queue].append((waits, fn, (key, 16)))
        for b in reads:
            b.r.append(tok)
        for b in writes:
            b.w = tok
            b.r = []
        return tok

    def coll(self, fn, reads, writes):
        waits = self._waits("pool", reads, writes)
        self.dcounts["cc"] = self.dcounts.get("cc", 0) + 1
        tok = ("cc", self.dcounts["cc"])
        self.ops["pool"].append((waits, fn, ("cc", 1)))
        for b in reads:
            b.r.append(tok)
        for b in writes:
            b.w = tok
            b.r = []
        return tok

    def wait_all(self, eng, bufs):
        waits = self._waits(eng, bufs, ())
        self.ops[eng].append((waits, None, None))

    def barrier(self, hard=False):
        toks = {e: self.cnt[e] for e in ENGS if self.cnt[e] > 0}
        skip = set(self.name2sem.get(n) for n in ("wt0", "wt1", "wt2"))
        skip.add("cc")
        for k_, v_ in self.dcounts.items():
            if k_ not in skip:
                toks[k_] = v_
        if not hard:
            for k_, v_ in toks.items():
                if self.fence.get(k_, 0) < v_:
                    self.fence[k_] = v_
            return
        for e in ENGS:
            if e == "pool":
                continue
            waits = []
            for k, v in toks.items():
                if k == e or self.seen[e].get(k, 0) >= v:
                    continue
                self.seen[e][k] = v
                waits.append((k, v))
            if waits:
                self.ops[e].append((waits, None, None))

    def build(self, stack):
        nc = self.nc
        for k in self.sem_keys:
            self.sems[k] = stack.enter_context(nc.semaphore("s_" + k))
        block = stack.enter_context(nc.Block())
        sems = self.sems

        def run(engname):
            def body(e):
                for waits, fn, inc in self.ops[engname]:
                    for k, v in waits:
                        e.wait_ge(sems[k], v)
                    if fn is not None:
                        ins = fn(e)
                        if inc is not None:
                            ins.then_inc(sems[inc[0]], inc[1])
            return body
        block.tensor(run("pe"))
        block.scalar(run("act"))
        block.vector(run("dve"))
        block.gpsimd(run("pool"))
        block.sync(run("sp"))


def make_consts():
    cb = np.zeros((128, NCB), np.float32)
    cf = np.zeros((128, NCF), np.float32)
    idx = np.arange(128)
    cb[:, 0:128] = np.eye(128)
    cb[:, 128:256] = 1.0
    cb[:, 256:384] = (idx[:, None] <= idx[None, :])
    cb[:, 384:512] = (idx[:, None] > idx[None, :])
    R = np.zeros((128, 128), np.float32)
    for p_ in range(128):
        d = p_ % 64
        if d < 8:
            R[p_ + 8, p_] = -1.0
        elif d < 16:
            R[p_ - 8, p_] = 1.0
    cb[:, 512:640] = R
    for half in range(2):
        Dm = np.zeros((128, 128), np.float32)
        for p_ in range(128):
            Dm[half * 64 + p_ % 64, p_] = 1.0
        cb[:, 640 + half * 128:768 + half * 128] = Dm
        cb[:, 896 + half * 128:1024 + half * 128] = R @ Dm
    cf[:, 0:128] = np.eye(128)
    cf[:, 128:256] = 1.0
    cf[:, 256:384] = np.where(idx[:, None] <= idx[None, :], 0.0, -30000.0)
    inv = (500000.0 ** (-(np.arange(0, 16, 2, dtype=np.float32)) / 16.0)).astype(np.float32)
    for p_ in range(128):
        d = p_ % 64
        cf[p_, 384] = inv[d % 8] if d < 16 else 0.0
    for h in range(4):
        cf[h, 385 + h * 128:385 + (h + 1) * 128] = 1.0
    return cb.astype(ml_dtypes.bfloat16), cf


class _Stop(Exception):
    pass


NS = 5
GROUPS = [[0, 1], [2, 3], [4, 5], [6, 7]]


def build_program(n_slots=NS, debug=False, stop=None, use_cc=True):
    nc = bass.Bass("TRN2", target_bir_lowering=False)

    def din(name, shape, dt=F32):
        return nc.dram_tensor(name, list(shape), dt, kind="ExternalInput").ap()
    x_d = din("xs", [n_slots * T, D])
    p_d = din("ps", [n_slots * T, 256])
    pos_d = din("pos", [1, n_slots * T], I32)
    flags_d = din("flags", [1, 32])
    norm_pre = din("norm_pre", [2, D])
    w_in = din("w_in", [2, D, NIN])
    ln_g = din("gmlp_ln_g", [2, 2048])
    ln_b = din("gmlp_ln_b", [2, 2048])
    wsT_d = din("gmlp_wsT", [2, 8, 128, 128])
    bs_d = din("gmlp_bs", [2, 1024])
    sinks_d = din("attn_sinks", [2, 32])
    ib_d = din("mlstm_ib", [2, 4])
    fb_d = din("mlstm_fb", [2, 4])
    mng_d = din("mlstm_norm_g", [2, 2048])
    w_branch = din("w_branch", [2, 3, 2048, D])
    w_out = din("w_out", [2, D, D])
    norm_post = din("norm_post", [2, D])
    ple_proj = din("ple_proj", [2, 256, D])
    ple_norm = din("ple_norm", [2, D])
    ple_gate = din("ple_gate", [2, D, D])
    cb_d = din("cst_bf", [128, NCB], BF16)
    cf_d = din("cst_f", [128, NCF])
    out_d = nc.dram_tensor("outs", [n_slots * T, D], F32, kind="ExternalOutput").ap()
    xc_d = nc.dram_tensor("xc_scr", [T, D], F32).ap()
    cfA_in_t = nc.dram_tensor("cfA_in", [1024, 512], F32)
    cfA_out_t = nc.dram_tensor("cfA_out", [2048, 512], F32)
    cfB_in_t = nc.dram_tensor("cfB_in", [256, 512], F32)
    cfB_out_t = nc.dram_tensor("cfB_out", [512, 512], F32)
    cfA_in, cfA_out, cfB_in, cfB_out = cfA_in_t.ap(), cfA_out_t.ap(), cfB_in_t.ap(), cfB_out_t.ap()
    mT_d = nc.dram_tensor("mT_scr", [4096, T], BF16).ap()
    o_d = nc.dram_tensor("o_scr", [T, D], F32).ap()
    xm_d = nc.dram_tensor("xm_scr", [T, D], F32).ap()
    if debug:
        dbg_d = nc.dram_tensor("dbg", [3, 2048, T], F32, kind="ExternalOutput").ap()

    P = Prog(nc)
    st = ExitStack()
    with st:
        def _full(t, shape):
            return t[:, :] if len(shape) == 2 else t[:, :, :]

        def sb(name, shape, dt=F32):
            return _full(st.enter_context(nc.sbuf_tensor(name, list(shape), dt)), shape)

        def psum(name, shape, dt=F32):
            return _full(st.enter_context(nc.psum_tensor(name, list(shape), dt)), shape)

        cb = sb("cb", [128, NCB], BF16)
        cf = sb("cf", [128, NCF])
        identb, onesb = cb[:, 0:128], cb[:, 128:256]
        mask2 = cb[:, 256:512]
        RrotT = cb[:, 512:640]
        DupT = [cb[:, 640:768], cb[:, 768:896]]
        DupRT = [cb[:, 896:1024], cb[:, 1024:1152]]
        identf, onesf, mb128 = cf[:, 0:128], cf[:, 128:256], cf[:, 256:384]
        invf = cf[:, 384:385]
        b_cst = Buf("cst")
        wt = [sb("wt%d" % i, [128, 8192], BF16) for i in range(3)]
        b_wt = [Buf("wt%d" % i) for i in range(3)]
        Cst = sb("Cst", [128, 8, 512])
        Cbf = sb("Cbf", [128, 8, 512], BF16)
        nst = sb("nst", [128, 8])
        nbf = sb("nbf", [128, 8], BF16)
        mst = sb("mst", [4, 1])
        b_C = [Buf("C%d" % i) for i in range(4)]
        b_m = Buf("mst")
        kTd = [sb("kTd%d" % g, [128, 640], BF16) for g in range(4)]
        b_kTd = [Buf("kTd%d" % g) for g in range(4)]
        vdup = [sb("vdup%d" % i, [128, 4, 128], BF16) for i in range(5)]
        b_vdup = [Buf("vdup%d" % i) for i in range(5)]
        cosF = sb("cosF", [128, T])
        sinF = sb("sinF", [128, T])
        b_cs = Buf("cossin")
        small = sb("small", [128, 256])
        b_small = {}
        arena = sb("arena", [128, ARENA])

        def smallbuf(name):
            if name not in b_small:
                b_small[name] = Buf("sm_" + name)
            return b_small[name]

        psG = [psum("psG%d" % i, [128, 512]) for i in range(2)]
        b_psG = [Buf("psG%d" % i) for i in range(2)]
        psT = [psum("psT%d" % i, [128, 1024], BF16) for i in range(2)]
        b_psT = [Buf("psT0"), Buf("psT1")]
        pg = [psum("pg%d" % i, [128, 512]) for i in range(4)]
        b_pg = [Buf("pg%d" % i) for i in range(4)]

        top = [0]

        def alloc(n32, dt=F32):
            a = top[0]
            top[0] += n32
            assert top[0] <= ARENA, ("arena overflow", top[0])
            v = arena[:, a:a + n32]
            return v if dt == F32 else v.bitcast(dt)

        def ACT(out, in_, func, reads, writes, **kw):
            P.op("act", lambda e: e.activation(out, in_, func, **kw), reads, writes)

        def DV(name, *args, reads, writes, **kw):
            P.op("dve", lambda e: getattr(e, name)(*args, **kw), reads, writes)

        def PL(name, *args, reads, writes, **kw):
            P.op("pool", lambda e: getattr(e, name)(*args, **kw), reads, writes)

        def MM(out, lhsT, rhs, start, stop, reads, writes, sig=None):
            if sig is None:
                sig = stop
            P.op("pe", lambda e: e.matmul(out, lhsT, rhs, start=start, stop=stop), reads, writes, sig)

        def TR(out, in_, ident, reads, writes, sig=True):
            P.op("pe", lambda e: e.transpose(out, in_, ident), reads, writes, sig)

        def DMA(q, out, in_, reads, writes, dbuf=None, slow=False):
            if slow:
                P.dma(q, lambda e: e.dma_start(out=out, in_=in_, allow_slow_non_contiguous=True), reads, writes, dbuf)
            else:
                P.dma(q, lambda e: e.dma_start(out=out, in_=in_), reads, writes, dbuf)

        ring = {"i": 0, "pinned": set()}

        def wload(srcs, K, ncols, pin=False):
            while ring["i"] % 3 in ring["pinned"]:
                ring["i"] += 1
            s = ring["i"] % 3
            ring["i"] += 1
            if pin:
                ring["pinned"].add(s)
            kc = K // 128
            view = wt[s][:, 0:kc * ncols].rearrange("p (k n) -> p k n", n=ncols)
            col = 0
            for ap in srcs:
                n_i = ap.shape[1]
                DMA("pool", view[:, :, col:col + n_i], ap.rearrange("(k p) n -> p k n", p=128),
                    reads=[], writes=[b_wt[s]])
                col += n_i
            return view, b_wt[s], s

        gcount = [0]

        def gemm_F(view, wb, kc, ncols, rhs_fn, rhs_bufs, consume, cw=128):
            for c in range((ncols + cw - 1) // cw):
                m = min(cw, ncols - c * cw)
                pi = gcount[0] % 2
                gcount[0] += 1
                for k in range(kc):
                    MM(psG[pi][0:m, :], view[:, k, c * cw:c * cw + m], rhs_fn(k), k == 0, k == kc - 1,
                       reads=[wb] + rhs_bufs, writes=[b_psG[pi]])
                consume(c, psG[pi], b_psG[pi])

        def gemm_T(view, wb, kc, ncols, lhs_fn, lhs_bufs, consume):
            for rt in range(4):
                pi = gcount[0] % 2
                gcount[0] += 1
                for k in range(kc):
                    MM(psG[pi][:, 0:ncols], lhs_fn(k, rt), view[:, k, 0:ncols], k == 0, k == kc - 1,
                       reads=[wb] + lhs_bufs, writes=[b_psG[pi]])
                consume(rt, psG[pi], b_psG[pi])

        tcount = [0]

        def transposes(src_fn, n, dst_fn, src_bufs, dst_bufs, scale_fn=None):
            i = 0
            while i < n:
                cnt = min(4, n - i)
                hb = tcount[0] % 2
                tcount[0] += 1
                base = 0
                psTb = psT[hb]
                for j in range(cnt):
                    TR(psTb[:, base + j * 128:base + (j + 1) * 128], src_fn(i + j), identb,
                       reads=src_bufs + [b_cst], writes=[b_psT[hb]], sig=(j == cnt - 1))
                src = psTb[:, base:base + cnt * 128].rearrange("p (c t) -> p c t", t=128)
                if tcount[0] % 2 == 0:
                    P.op("act", lambda e, o=dst_fn(i, cnt), s_=src: e.copy(o, s_), [b_psT[hb]], dst_bufs)
                else:
                    DV("tensor_copy", dst_fn(i, cnt), src, reads=[b_psT[hb]], writes=dst_bufs)
                i += cnt

        DMA("sp", cb[:, :], cb_d[:, :], [], [b_cst])
        DMA("sp", cf[:, :], cf_d[:, :], [], [b_cst])

        flg = sb("flg", [128, 32])
        b_flg = Buf("flg")
        DMA("sp", flg, flags_d[0:1, :].broadcast_to([128, 32]), [], [b_flg])
        b_outs, b_xc, b_cfin, b_cfout = Buf("outs"), Buf("xc"), Buf("cfin"), Buf("cfout")
        zt = arena[:, 0:512]
        b_zt = Buf("zt")
        DV("memset", zt, 0.0, reads=[], writes=[b_zt])
        DMA("sp", cfB_in[0:128, :], zt, [b_zt], [b_cfin])
        P.barrier()
        b_x = [Buf("x_l0"), Buf("x_l1"), Buf("x_l2")]
        b_mTd, b_od, b_xmd = Buf("mTd"), Buf("od"), Buf("xmd")
        b_dbg = Buf("dbg")

        def dump_dbg(yT, b_yT, nj, mark):
            top[0] = mark
            dtmp = alloc(512)
            b_dtmp = Buf("dtmp")
            for j in range(nj):
                for cc in range(16):
                    DV("tensor_copy", dtmp, yT[j][:, cc, :], reads=[b_yT[j][cc]], writes=[b_dtmp])
                    DMA("sp", dbg_d[j, cc * 128:(cc + 1) * 128, :], dtmp, [b_dtmp], [b_dbg])
            P.barrier()

        def _layers():
          for s in range(n_slots):
            for l in [s % 2]:
                tok0 = s * T
                x_out = out_d
                b_xout = b_outs
                f_x = flg[:, s * 4:s * 4 + 1]
                f_0 = flg[:, s * 4 + 1:s * 4 + 2]
                f_1 = flg[:, s * 4 + 2:s * 4 + 3]
                f_c = flg[:, s * 4 + 3:s * 4 + 4]
                P.barrier()
                top[0] = 0
                if s == 0:
                    DV("memset", Cst[:, :, :], 0.0, reads=[], writes=b_C)
                    DV("memset", Cbf[:, :, :], 0.0, reads=[], writes=b_C)
                    DV("memset", nst[:, :], 0.0, reads=[], writes=b_C)
                    DV("memset", nbf[:, :], 0.0, reads=[], writes=b_C)
                    DV("memset", mst[:, :], 0.0, reads=[], writes=[b_m])
                    for g in range(4):
                        DV("memset", kTd[g][:, 0:128], 0.0, reads=[], writes=[b_kTd[g]])
                    DV("memset", vdup[0][:, :, :], 0.0, reads=[], writes=[b_vdup[0]])
                hT = alloc(8192, BF16).rearrange("p (k t) -> p k t", t=T)
                b_hT = Buf("hT")
                mark_p0 = top[0]
                if s > 0:
                    rC = alloc(4096)
                    b_rC = Buf("rC")
                    rM = alloc(32).rearrange("p (a n) -> p a n", a=2)
                    b_rM = Buf("rM")
                    rB = [alloc(512, BF16) for _ in range(2)]
                    b_rB = Buf("rB")
                    C2 = Cst.rearrange("p j n -> p (j n)")
                    DMA("sp", Cst, cfA_out[0:1024, :].rearrange("(p j) n -> p j n", j=8), [b_cfout], b_C)
                    DMA("sp", rC.rearrange("p (j n) -> p j n", j=8), cfA_out[1024:2048, :].rearrange("(p j) n -> p j n", j=8),
                        [b_cfout], [b_rC])
                    DV("tensor_scalar", C2, C2, f_0, None, ALU.mult, reads=b_C + [b_flg], writes=b_C)
                    DV("scalar_tensor_tensor", C2, rC, f_1, C2, ALU.mult, ALU.add, reads=[b_rC, b_flg] + b_C, writes=b_C)
                    P.op("act", lambda e, o=Cbf.rearrange("p j n -> p (j n)"), i_=C2: e.copy(o, i_), b_C, b_C)
                    DMA("sp", rM[:, 0, :], cfB_out[0:128, 0:16], [b_cfout], [b_rM])
                    DMA("sp", rM[:, 1, :], cfB_out[256:384, 0:16], [b_cfout], [b_rM])
                    DV("tensor_scalar", rM[:, 0, :], rM[:, 0, :], f_0, None, ALU.mult, reads=[b_rM, b_flg], writes=[b_rM])
                    DV("scalar_tensor_tensor", rM[:, 0, :], rM[:, 1, :], f_1, rM[:, 0, :], ALU.mult, ALU.add,
                       reads=[b_rM, b_flg], writes=[b_rM])
                    DV("tensor_copy", nst[:, :], rM[:, 0, 0:8], reads=[b_rM], writes=b_C)
                    DV("tensor_copy", nbf[:, :], rM[:, 0, 0:8], reads=[b_rM], writes=b_C)
                    DV("tensor_copy", mst[0:4, 0:1], rM[0:4, 0, 8:9], reads=[b_rM], writes=[b_m])
                    DMA("sp", rB[0], cfB_out[128:256, :].bitcast(BF16), [b_cfout], [b_rB])
                    DMA("sp", rB[1], cfB_out[384:512, :].bitcast(BF16), [b_cfout], [b_rB])
                    DV("tensor_scalar", rB[0], rB[0], f_0, None, ALU.mult, reads=[b_rB, b_flg], writes=[b_rB])
                    DV("scalar_tensor_tensor", rB[0], rB[1], f_1, rB[0], ALU.mult, ALU.add, reads=[b_rB, b_flg], writes=[b_rB])
                    for g in range(4):
                        DV("tensor_copy", kTd[g][:, 0:128], rB[0][:, g * 128:(g + 1) * 128], reads=[b_rB], writes=[b_kTd[g]])
                    DV("tensor_copy", vdup[0][:, :, :], rB[0][:, 512:1024].rearrange("p (g d) -> p g d", d=128),
                       reads=[b_rB], writes=[b_vdup[0]])
                gbuf = alloc(4096)
                b_gbuf = Buf("gbuf")
                xrow = alloc(4096)
                b_xrow = Buf("xrow")
                xsb = alloc(2048, BF16)
                b_xsb = Buf("xsb")
                xch = alloc(4096)
                b_xch = Buf("xch")
                sm_ss = small[:, 0:1]
                sm_rstd = small[:, 1:2]
                b_ss = smallbuf("ss")
                DMA("sp", gbuf, norm_pre[l:l + 1, :].broadcast_to([128, D]), [], [b_gbuf])
                for rt in range(4):
                    r0 = tok0 + rt * 128
                    DMA("sp", xrow, x_d[r0:r0 + 128, :], [], [b_xrow])
                    if s > 0:
                        DMA("sp", xch, out_d[r0 - T:r0 - T + 128, :], [b_outs], [b_xch])
                        DV("scalar_tensor_tensor", xrow, xch, f_x, xrow, ALU.mult, ALU.add,
                           reads=[b_xch, b_flg, b_xrow], writes=[b_xrow])
                    DMA("sp", xc_d[rt * 128:(rt + 1) * 128, :], xrow, [b_xrow], [b_xc])
                    ACT(xsb, xrow, AF.Square, [b_xrow], [b_xsb, b_ss], accum_out=sm_ss)
                    DV("tensor_scalar", small[:, 2:3], sm_ss, 1.0 / D, EPS, ALU.mult, ALU.add, reads=[b_ss], writes=[b_ss])
                    ACT(sm_rstd, small[:, 2:3], AF.Sqrt, [b_ss], [b_ss]); DV("reciprocal", sm_rstd, sm_rstd, reads=[b_ss], writes=[b_ss])
                    DV("scalar_tensor_tensor", xsb, xrow, sm_rstd, gbuf, ALU.mult, ALU.mult,
                       reads=[b_xrow, b_ss, b_gbuf], writes=[b_xsb])
                    transposes(lambda i: xsb[:, i * 128:(i + 1) * 128], 32,
                               lambda i0, cnt, rt=rt: hT[:, i0:i0 + cnt, rt * 128:(rt + 1) * 128],
                               [b_xsb], [b_hT])
                P.barrier()
                if stop == "p0":
                    raise _Stop()
                top[0] = mark_p0
                hT_fn = lambda k: hT[:, k, :]
                hT_lhs = lambda k, rt: hT[:, k, rt * 128:(rt + 1) * 128]
                W = w_in[l]

                def wcols(c0, n):
                    return W[:, c0:c0 + n]

                yT = [None, None, None]
                b_yT = [None, None, None]
                yT[0] = alloc(4096, BF16).rearrange("p (k t) -> p k t", t=T)
                b_yT[0] = [Buf("yTa%d" % i) for i in range(16)]
                mark_a = top[0]
                GB = alloc(4096).rearrange("p (a n) -> p a n", a=2)
                b_GB = Buf("GB")
                gv = [alloc(1024, BF16) for _ in range(4)]
                b_gv = [Buf("gv%d" % i) for i in range(4)]
                wsraw = alloc(1024).rearrange("p (g t) -> p g t", t=128)
                wsTm = alloc(512, BF16).rearrange("p (g t) -> p g t", t=128)
                b_ws = Buf("ws")
                bsrow = alloc(1024)
                b_bs = Buf("bsrow")
                gu = [alloc(256, BF16) for _ in range(2)]
                sz = [alloc(256, BF16) for _ in range(2)]
                b_gu = [Buf("gu0"), Buf("gu1")]
                b_sz = [Buf("sz0"), Buf("sz1")]
                atmp = alloc(512)
                b_atmp = Buf("atmp")
                junkA = alloc(128, BF16)
                b_junk = Buf("junk")
                s1 = small[:, 8:40]
                s2 = small[:, 40:72]
                b_s12 = smallbuf("s12")
                stA = small[:, 72:96]
                b_stA = smallbuf("stA")
                DMA("sp", GB[:, 0, :], ln_g[l:l + 1, :].broadcast_to([128, 2048]), [], [b_GB])
                DMA("sp", GB[:, 1, :], ln_b[l:l + 1, :].broadcast_to([128, 2048]), [], [b_GB])
                DMA("sp", wsraw, wsT_d[l].rearrange("g s t -> s g t"), [], [b_ws])
                DMA("sp", bsrow[0:1, :], bs_d[l:l + 1, :], [], [b_bs])
                for g in range(8):
                    DV("tensor_tensor", wsTm[:, g, :], wsraw[:, g, :], mask2[:, 0:128], ALU.mult,
                       reads=[b_ws, b_cst], writes=[b_ws])
                for blk in range(8):
                    view, wb, _ = wload([wcols(OFF["a_v"] + blk * 256, 256)], D, 256)

                    def cons_v(rt, ps, bps, blk=blk):
                        sl = gv[rt][:, blk * 256:(blk + 1) * 256]
                        ACT(sl, ps[:, 0:256], AF.Gelu, [bps], [b_gv[rt]])
                        DV("tensor_reduce", s1[:, rt * 8 + blk:rt * 8 + blk + 1], sl, AX.X, ALU.add,
                           reads=[b_gv[rt]], writes=[b_s12])
                        ACT(junkA[:, 0:256], sl, AF.Square, [b_gv[rt]], [b_junk, b_s12],
                            accum_out=s2[:, rt * 8 + blk:rt * 8 + blk + 1])
                    gemm_T(view, wb, 32, 256, hT_lhs, [b_hT], cons_v)
                DV("tensor_reduce", stA[:, 0:4], s1.rearrange("p (r b) -> p r b", b=8), AX.X, ALU.add,
                   reads=[b_s12], writes=[b_stA])
                DV("tensor_reduce", stA[:, 4:8], s2.rearrange("p (r b) -> p r b", b=8), AX.X, ALU.add,
                   reads=[b_s12], writes=[b_stA])
                DV("tensor_scalar", stA[:, 8:12], stA[:, 0:4], 1.0 / 2048, None, ALU.mult, reads=[b_stA], writes=[b_stA])
                DV("tensor_tensor", stA[:, 12:16], stA[:, 8:12], stA[:, 8:12], ALU.mult, reads=[b_stA], writes=[b_stA])
                DV("scalar_tensor_tensor", stA[:, 12:16], stA[:, 4:8], 1.0 / 2048, stA[:, 12:16], ALU.mult, ALU.subtract,
                   reads=[b_stA], writes=[b_stA])
                DV("tensor_scalar", stA[:, 12:16], stA[:, 12:16], 0.0, EPS, ALU.max, ALU.add, reads=[b_stA], writes=[b_stA]); ACT(stA[:, 16:20], stA[:, 12:16], AF.Sqrt, [b_stA], [b_stA]); DV("reciprocal", stA[:, 16:20], stA[:, 16:20], reads=[b_stA], writes=[b_stA])
                DV("scalar_tensor_tensor", stA[:, 20:24], stA[:, 8:12], -1.0, stA[:, 16:20], ALU.mult, ALU.mult,
                   reads=[b_stA], writes=[b_stA])
                for rt in range(4):
                    DV("tensor_scalar", gv[rt], gv[rt], stA[:, 16 + rt:17 + rt], stA[:, 20 + rt:21 + rt],
                       ALU.mult, ALU.add, reads=[b_gv[rt], b_stA], writes=[b_gv[rt]])
                    DV("tensor_tensor", gv[rt], gv[rt], GB[:, 0, :], ALU.mult, reads=[b_gv[rt], b_GB], writes=[b_gv[rt]])
                    DV("tensor_tensor", gv[rt], gv[rt], GB[:, 1, :], ALU.add, reads=[b_gv[rt], b_GB], writes=[b_gv[rt]])
                for blk in range(8):
                    view, wb, _ = wload([wcols(OFF["a_u"] + blk * 256, 256)], D, 256)

                    def cons_u(c, ps, bps):
                        ACT(gu[c], ps, AF.Gelu, [bps], [b_gu[c]])
                    gemm_F(view, wb, 32, 256, hT_fn, [b_hT], cons_u)
                    view, wb, _ = wload([wcols(OFF["a_z"] + blk * 256, 256)], D, 256)

                    def cons_z(c, ps, bps):
                        ACT(sz[c], ps, AF.Silu, [bps], [b_sz[c]])
                    gemm_F(view, wb, 32, 256, hT_fn, [b_hT], cons_z)
                    for c in range(2):
                        cc = blk * 2 + c
                        for rt in range(4):
                            MM(pg[0][:, rt * 128:(rt + 1) * 128], gv[rt][:, cc * 128:(cc + 1) * 128], wsTm[:, blk, :],
                               True, False, reads=[b_gv[rt], b_ws], writes=[b_pg[0]], sig=False)
                            MM(pg[0][:, rt * 128:(rt + 1) * 128], onesf[0:1, 0:128], bsrow[0:1, blk * 128:(blk + 1) * 128],
                               False, True, reads=[b_bs, b_cst], writes=[b_pg[0]], sig=(rt == 3))
                        DV("tensor_tensor", atmp, pg[0], gu[c], ALU.mult, reads=[b_pg[0], b_gu[c]], writes=[b_atmp])
                        DV("tensor_tensor", yT[0][:, cc, :], atmp, sz[c], ALU.mult, reads=[b_atmp, b_sz[c]],
                           writes=[b_yT[0][cc]])
                P.barrier()
                if stop == "a":
                    dump_dbg(yT, b_yT, 1, mark_a)
                    raise _Stop()
                top[0] = mark_a

                yT[1] = alloc(4096, BF16).rearrange("p (k t) -> p k t", t=T)
                b_yT[1] = [Buf("yTb%d" % i) for i in range(16)]
                mark_b = top[0]
                posb = alloc(512).bitcast(I32)
                angf = alloc(512)
                b_pos = Buf("pos")
                esink = small[:, 96:128]
                b_esink = smallbuf("esink")
                kraw = [alloc(256, BF16) for _ in range(2)]
                b_kraw = [Buf("kraw0"), Buf("kraw1")]
                qraw = [alloc(256, BF16) for _ in range(2)]
                b_qraw = [Buf("qraw0"), Buf("qraw1")]
                qT = [alloc(256, BF16) for _ in range(2)]
                b_qT = [Buf("qT0"), Buf("qT1")]
                t1 = alloc(512)
                t2 = alloc(512)
                b_t1, b_t2 = Buf("t1"), Buf("t2")
                vtmp = alloc(128, BF16)
                b_vtmp = Buf("vtmp")
                PT = [alloc(128, BF16) for _ in range(2)]
                b_PT = [Buf("PT0"), Buf("PT1")]
                rtmp = alloc(512)
                b_rtmp = Buf("rtmp")
                szb = [alloc(256, BF16) for _ in range(2)]
                b_szb = [Buf("szb0"), Buf("szb1")]
                DMA("sp", posb, pos_d[0:1, tok0:tok0 + T].broadcast_to([128, T]), [], [b_pos])
                DV("tensor_copy", angf, posb, reads=[b_pos], writes=[b_pos])
                DV("tensor_scalar", angf, angf, invf, None, ALU.mult, reads=[b_pos, b_cst], writes=[b_pos])
                C1 = 6.28125
                C2 = 2 * math.pi - 6.28125
                kint = posb
                DV("tensor_scalar", t1, angf, 1.0 / (2 * math.pi), None, ALU.mult, reads=[b_pos], writes=[b_t1])
                DV("tensor_copy", kint, t1, reads=[b_t1], writes=[b_pos])
                DV("tensor_copy", t2, kint, reads=[b_pos], writes=[b_t2])
                DV("scalar_tensor_tensor", t1, t2, -C1, angf, ALU.mult, ALU.add, reads=[b_t2, b_pos], writes=[b_t1])
                DV("scalar_tensor_tensor", t1, t2, -C2, t1, ALU.mult, ALU.add, reads=[b_t2, b_t1], writes=[b_t1])
                DV("tensor_scalar", t2, t1, math.pi, 2 * math.pi, ALU.is_gt, ALU.mult, reads=[b_t1], writes=[b_t2])
                DV("tensor_tensor", t1, t1, t2, ALU.subtract, reads=[b_t1, b_t2], writes=[b_t1])
                DV("tensor_scalar", t2, t1, -math.pi, 2 * math.pi, ALU.is_lt, ALU.mult, reads=[b_t1], writes=[b_t2])
                DV("tensor_tensor", t1, t1, t2, ALU.add, reads=[b_t1, b_t2], writes=[b_t1])
                DV("tensor_scalar", angf, t1, 0.5 * math.pi, None, ALU.add, reads=[b_t1], writes=[b_pos])
                DV("tensor_scalar", t2, angf, math.pi, 2 * math.pi, ALU.is_gt, ALU.mult, reads=[b_pos], writes=[b_t2])
                DV("tensor_tensor", t2, angf, t2, ALU.subtract, reads=[b_pos, b_t2], writes=[b_t2])
                ACT(sinF, t1, AF.Sin, [b_t1], [b_cs])
                ACT(cosF, t2, AF.Sin, [b_t2], [b_cs])
                DMA("sp", esink, sinks_d[l:l + 1, :].broadcast_to([128, 32]), [], [b_esink])
                ACT(esink, esink, AF.Exp, [b_esink], [b_esink])
                view, wb, _ = wload([wcols(OFF["b_k"], 256)], D, 256)

                def cons_k(c, ps, bps):
                    ACT(kraw[c], ps, AF.Copy, [bps], [b_kraw[c]])
                gemm_F(view, wb, 32, 256, hT_fn, [b_hT], cons_k)
                for g in range(4):
                    MM(pg[0], DupT[g % 2], kraw[g // 2], True, True, reads=[b_kraw[g // 2], b_cst], writes=[b_pg[0]])
                    MM(pg[1], DupRT[g % 2], kraw[g // 2], True, True, reads=[b_kraw[g // 2], b_cst], writes=[b_pg[1]])
                    DV("tensor_tensor", t1, pg[0], cosF, ALU.mult, reads=[b_pg[0], b_cs], writes=[b_t1])
                    DV("tensor_tensor", t2, pg[1], sinF, ALU.mult, reads=[b_pg[1], b_cs], writes=[b_t2])
                    DV("tensor_tensor", kTd[g][:, 128:640], t1, t2, ALU.add, reads=[b_t1, b_t2], writes=[b_kTd[g]])
                view, wb, _ = wload([wcols(OFF["b_v"], 256)], D, 256)

                def cons_vb(rt, ps, bps):
                    ACT(vtmp, ps[:, 0:256], AF.Copy, [bps], [b_vtmp])
                    v3 = vtmp.rearrange("p (g d) -> p g d", d=64)
                    DV("tensor_copy", vdup[rt + 1][:, :, 0:64], v3, reads=[b_vtmp], writes=[b_vdup[rt + 1]])
                    DV("tensor_copy", vdup[rt + 1][:, :, 64:128], v3, reads=[b_vtmp], writes=[b_vdup[rt + 1]])
                gemm_T(view, wb, 32, 256, hT_lhs, [b_hT], cons_vb)
                sc_cnt = [0]
                for qb in range(8):
                    view, wb, _ = wload([wcols(OFF["b_q"] + qb * 256, 256)], D, 256)

                    def cons_q(c, ps, bps, qb=qb):
                        qc = qb * 2 + c
                        g = qc // 4
                        ACT(qraw[c], ps, AF.Copy, [bps], [b_qraw[c]], scale=0.125)
                        MM(pg[0], RrotT, qraw[c], True, True, reads=[b_qraw[c], b_cst], writes=[b_pg[0]])
                        DV("tensor_tensor", t1, qraw[c], cosF, ALU.mult, reads=[b_qraw[c], b_cs], writes=[b_t1])
                        DV("tensor_tensor", t2, pg[0], sinF, ALU.mult, reads=[b_pg[0], b_cs], writes=[b_t2])
                        DV("tensor_tensor", qT[c], t1, t2, ALU.add, reads=[b_t1, b_t2], writes=[b_qT[c]])
                        steps = [(half, rt) for half in range(2) for rt in range(4)]
                        S2bufs = [(pg[1][:, 0:256], b_pg[1]), (psT[0].bitcast(F32)[:, 0:256], b_psT[0])]

                        def stepA(j):
                            half, rt = steps[j]
                            rows = slice(half * 64, half * 64 + 64)
                            S2, bS2 = S2bufs[j % 2]
                            qsl = qT[c][rows, rt * 128:(rt + 1) * 128]
                            MM(S2[:, 0:128], kTd[g][rows, 128 + rt * 128:256 + rt * 128], qsl, True, True,
                               reads=[b_kTd[g], b_qT[c]], writes=[bS2], sig=False)
                            MM(S2[:, 128:256], kTd[g][rows, rt * 128:rt * 128 + 128], qsl, True, True,
                               reads=[b_kTd[g], b_qT[c]], writes=[bS2], sig=True)

                        def stepB(j):
                            half, rt = steps[j]
                            h = 2 * qc + half
                            rows = slice(half * 64, half * 64 + 64)
                            S2, bS2 = S2bufs[j % 2]
                            si = j % 2
                            ACT(PT[si][:, 0:256], S2[:, 0:256], AF.Exp, [bS2], [b_PT[si]])
                            DV("tensor_tensor", PT[si][:, 0:256], PT[si][:, 0:256], mask2[:, 0:256], ALU.mult,
                               reads=[b_PT[si], b_cst], writes=[b_PT[si]])
                            if rt == 0:
                                DV("tensor_scalar", PT[si][:, 128:256], PT[si][:, 128:256], f_c, None, ALU.mult,
                                   reads=[b_PT[si], b_flg], writes=[b_PT[si]])
                            osl = slice(rt * 128, (rt + 1) * 128)
                            MM(pg[2][:, osl], vdup[rt + 1][:, g, :], PT[si][:, 0:128], True, False,
                               reads=[b_vdup[rt + 1], b_PT[si]], writes=[b_pg[2]], sig=False)
                            MM(pg[2][:, osl], vdup[rt][:, g, :], PT[si][:, 128:256], False, True,
                               reads=[b_vdup[rt], b_PT[si]], writes=[b_pg[2]], sig=False)
                            MM(pg[3][:, osl], onesb, PT[si][:, 0:128], True, False,
                               reads=[b_PT[si], b_cst], writes=[b_pg[3]], sig=False)
                            MM(pg[3][:, osl], onesb, PT[si][:, 128:256], False, True,
                               reads=[b_PT[si], b_cst], writes=[b_pg[3]], sig=True)
                            if rt == 3:
                                DV("tensor_scalar", rtmp[rows, :], pg[3][rows, :], esink[rows, h:h + 1], None, ALU.add,
                                   reads=[b_pg[3], b_esink], writes=[b_rtmp])
                                DV("reciprocal", rtmp[rows, :], rtmp[rows, :], reads=[b_rtmp], writes=[b_rtmp])
                                DV("tensor_tensor", yT[1][rows, qc, :], pg[2][rows, :], rtmp[rows, :], ALU.mult,
                                   reads=[b_pg[2], b_rtmp], writes=[b_yT[1][qc]])
                        stepA(0)
                        for j in range(8):
                            if j + 1 < 8:
                                stepA(j + 1)
                            stepB(j)
                    gemm_F(view, wb, 32, 256, hT_fn, [b_hT], cons_q)
                if s < n_slots - 1:
                    cfb = cfB_in[128:256, :].bitcast(BF16)
                    for g in range(4):
                        DMA("sp", cfb[:, g * 128:(g + 1) * 128], kTd[g][:, 512:640], [b_kTd[g]], [b_cfin])
                    DMA("sp", cfb[:, 512:1024].rearrange("p (g d) -> p g d", d=128), vdup[4][:, :, :], [b_vdup[4]], [b_cfin])
                for blk in range(8):
                    view, wb, _ = wload([wcols(OFF["b_z"] + blk * 256, 256)], D, 256)

                    def cons_zb(c, ps, bps, blk=blk):
                        cc = blk * 2 + c
                        ACT(szb[c], ps, AF.Silu, [bps], [b_szb[c]])
                        DV("tensor_tensor", yT[1][:, cc, :], yT[1][:, cc, :], szb[c], ALU.mult,
                           reads=[b_yT[1][cc], b_szb[c]], writes=[b_yT[1][cc]])
                    gemm_F(view, wb, 32, 256, hT_fn, [b_hT], cons_zb)
                P.barrier()
                if stop == "b":
                    dump_dbg(yT, b_yT, 2, mark_b)
                    raise _Stop()
                top[0] = mark_b

                yT[2] = alloc(4096, BF16).rearrange("p (k t) -> p k t", t=T)
                b_yT[2] = [Buf("yTc%d" % i) for i in range(16)]
                igc = alloc(512)
                lfb = alloc(512)
                bcs = alloc(512)
                Mg = alloc(512)
                b_gates = Buf("gates")
                gsm = alloc(1024)
                b_gsm = Buf("gsm")
                tokT = alloc(64)
                b_tokT = [Buf("tokT%d" % i) for i in range(4)]
                mng = small[:, 128:144]
                b_mng = smallbuf("mng")
                ibfb = small[:, 144:146]
                b_ibfb = smallbuf("ibfb")
                QT = [alloc(256, BF16) for _ in range(2)]
                KT = [alloc(256, BF16) for _ in range(2)]
                b_QT = [Buf("QT0"), Buf("QT1")]
                b_KT = [Buf("KT0"), Buf("KT1")]
                Ktok = [alloc(128, BF16) for _ in range(4)]
                b_Ktok = [Buf("Ktok%d" % i) for i in range(4)]
                Vtok = [alloc(256, BF16) for _ in range(4)]
                b_Vtok = [Buf("Vtok%d" % i) for i in range(4)]
                Dt = alloc(128)
                b_Dt = Buf("Dt")
                St = alloc(64, BF16)
                b_St = Buf("St")
                intra = alloc(512)
                b_intra = Buf("intra")
                num = alloc(512)
                b_num = Buf("num")
                hn2 = [alloc(256, BF16) for _ in range(2)]
                b_hn2 = [Buf("hn0"), Buf("hn1")]
                kw = alloc(128, BF16)
                b_kw = Buf("kw")
                junkC = alloc(256, BF16)
                sgc = [alloc(256, BF16) for _ in range(2)]
                b_sgc = [Buf("sgc0"), Buf("sgc1")]
                csm = small[:, 160:176]
                b_csm = smallbuf("csm")
                DMA("sp", mng, mng_d[l].rearrange("(c p) -> p c", p=128), [], [b_mng], slow=True)
                DMA("sp", ibfb[0:4, 0:1], ib_d[l].rearrange("(h o) -> h o", o=1), [], [b_ibfb], slow=True)
                DMA("sp", ibfb[0:4, 1:2], fb_d[l].rearrange("(h o) -> h o", o=1), [], [b_ibfb], slow=True)
                DV("tensor_scalar", ibfb[0:4, 0:2], ibfb[0:4, 0:2], 1.0 / 15.0, None, ALU.mult, reads=[b_ibfb], writes=[b_ibfb])
                view, wb, _ = wload([wcols(OFF["c_i"], 8)], D, 8)

                def cons_if(c, ps, bps):
                    dst = igc if c == 0 else lfb
                    ACT(dst[0:4, :], ps[0:4, :], AF.Tanh, [bps, b_ibfb], [b_gates], scale=1.0 / 15.0,
                        bias=ibfb[0:4, c:c + 1])
                gemm_F(view, wb, 32, 8, hT_fn, [b_hT], cons_if, cw=4)
                G4 = lambda a: a[0:4, :]
                DV("tensor_scalar", G4(igc), G4(igc), 15.0, None, ALU.mult, reads=[b_gates], writes=[b_gates])
                ACT(G4(lfb), G4(lfb), AF.Exp, [b_gates], [b_gates], scale=-15.0)
                ACT(G4(lfb), G4(lfb), AF.Ln, [b_gates], [b_gates], bias=1.0)
                DV("tensor_scalar", G4(lfb), G4(lfb), -1.0, None, ALU.mult, reads=[b_gates], writes=[b_gates])
                for ch in range(4):
                    csl = slice(ch * 128, (ch + 1) * 128)
                    DV("tensor_tensor_scan", bcs[0:4, csl], onesf[0:4, 0:128], lfb[0:4, csl], 0.0, ALU.mult, ALU.add,
                       reads=[b_gates, b_cst], writes=[b_gates])
                DV("tensor_tensor", G4(igc), G4(igc), G4(bcs), ALU.subtract, reads=[b_gates], writes=[b_gates])
                selrow = lambda h: cf[0:4, 385 + h * 128:385 + (h + 1) * 128]
                for ch in range(4):
                    csl = slice(ch * 128, (ch + 1) * 128)
                    last = ch * 128 + 127
                    gA = gsm[0:4, 0:128]
                    gE = gsm[0:4, 128:256]
                    gW = gsm[0:4, 256:384]
                    gNM = gsm[0:4, 384 + ch * 128:512 + ch * 128]
                    gml = gsm[0:4, 900:901]
                    gdiag = gsm[0:4, 904:908]
                    DV("tensor_tensor_scan", Mg[0:4, csl], igc[0:4, csl], igc[0:4, csl], mst[0:4, 0:1], ALU.max, ALU.max,
                       reads=[b_gates, b_m], writes=[b_gates])
                    ACT(gA, Mg[0:4, csl], AF.Exp, [b_gates, b_m], [b_gsm], scale=-1.0, bias=mst[0:4, 0:1])
                    DV("tensor_tensor", gE, bcs[0:4, csl], Mg[0:4, csl], ALU.add, reads=[b_gates], writes=[b_gsm])
                    ACT(gE, gE, AF.Exp, [b_gsm], [b_gsm], scale=-1.0)
                    DV("tensor_scalar", gml, Mg[0:4, last:last + 1], -1.0, None, ALU.mult, reads=[b_gates], writes=[b_gsm])
                    ACT(gW, igc[0:4, csl], AF.Exp, [b_gates, b_gsm], [b_gsm], bias=gml)
                    DV("tensor_scalar", gNM, Mg[0:4, csl], -1.0, None, ALU.mult, reads=[b_gates], writes=[b_gsm])
                    DV("tensor_scalar", gdiag, identf[0:4, 0:4], gA[:, 127:128], None, ALU.mult,
                       reads=[b_gsm, b_cst], writes=[b_gsm])
                    DV("tensor_tensor", mst[0:4, 0:1], bcs[0:4, last:last + 1], Mg[0:4, last:last + 1], ALU.add,
                       reads=[b_gates, b_gsm], writes=[b_m])
                    px = pg[0][:, 384:400]
                    TR(px[:, 0:4], gA, identf[0:4, 0:4], reads=[b_gsm, b_cst], writes=[b_pg[0]], sig=False)
                    TR(px[:, 4:8], gE, identf[0:4, 0:4], reads=[b_gsm, b_cst], writes=[b_pg[0]], sig=False)
                    TR(px[:, 8:12], gW, identf[0:4, 0:4], reads=[b_gsm, b_cst], writes=[b_pg[0]], sig=False)
                    MM(px[:, 12:16], onesf[0:4, 0:128], gdiag, True, True, reads=[b_gsm, b_cst], writes=[b_pg[0]])
                    tk = tokT[:, ch * 16:(ch + 1) * 16]
                    ACT(tk, px, AF.Copy, [b_pg[0]], [b_tokT[ch]])
                for hd in range(4):
                    view, wb, _ = wload([wcols(OFF["c_q"] + hd * 256, 256)], D, 256)

                    def cons_cq(c, ps, bps):
                        ACT(QT[c], ps, AF.Copy, [bps], [b_QT[c]], scale=1.0 / 16.0)
                    gemm_F(view, wb, 32, 256, hT_fn, [b_hT], cons_cq)
                    view, wb, _ = wload([wcols(OFF["c_k"] + hd * 256, 256)], D, 256)

                    def cons_ck(c, ps, bps):
                        ACT(KT[c], ps, AF.Copy, [bps], [b_KT[c]])
                    gemm_F(view, wb, 32, 256, hT_fn, [b_hT], cons_ck)
                    for ch in range(4):
                        transposes(lambda i, ch=ch: KT[i][:, ch * 128:(ch + 1) * 128], 2,
                                   lambda i0, cnt, ch=ch: Ktok[ch].rearrange("p (c d) -> p c d", d=128)[:, i0:i0 + cnt, :],
                                   b_KT, [b_Ktok[ch]])
                    for vb in range(2):
                        view, wb, _ = wload([wcols(OFF["c_v"] + hd * 512 + vb * 256, 256)], D, 256)

                        def cons_cv(rt, ps, bps, vb=vb):
                            ACT(Vtok[rt][:, vb * 256:(vb + 1) * 256], ps[:, 0:256], AF.Copy, [bps], [b_Vtok[rt]])
                        gemm_T(view, wb, 32, 256, hT_lhs, [b_hT], cons_cv)
                    bC = b_C[hd]
                    pending_tail = [None]
                    for ch in range(4):
                        csl = slice(ch * 128, (ch + 1) * 128)
                        hn, b_hn = hn2[ch % 2], b_hn2[ch % 2]
                        tk = tokT[:, ch * 16:(ch + 1) * 16]
                        a_col = tk[:, hd:hd + 1]
                        e_col = tk[:, 4 + hd:5 + hd]
                        w_col = tk[:, 8 + hd:9 + hd]
                        d_col = tk[:, 12 + hd:13 + hd]
                        gNM = gsm[0:4, 384 + ch * 128:512 + ch * 128]
                        pL = pg[0][:, 0:128]
                        pS = pg[0][:, 128:256]
                        psm = pg[0][:, 256:264]
                        MM(pL, igc[0:4, csl], selrow(hd), True, False, reads=[b_gates, b_cst], writes=[b_pg[0]], sig=False)
                        MM(pL, selrow(hd), gNM, False, False, reads=[b_gsm, b_cst], writes=[b_pg[0]], sig=False)
                        MM(pL, identf, mb128, False, True, reads=[b_cst], writes=[b_pg[0]], sig=True)
                        ACT(Dt, pL, AF.Exp, [b_pg[0]], [b_Dt])
                        for dc in range(2):
                            MM(pS, KT[dc][:, csl], QT[dc][:, csl], dc == 0, dc == 1, reads=[b_KT[dc], b_QT[dc]],
                               writes=[b_pg[0]])
                        DV("tensor_tensor", St, pS, Dt, ALU.mult, reads=[b_pg[0], b_Dt], writes=[b_St])
                        MM(pg[1], St, Vtok[ch], True, True, reads=[b_St, b_Vtok[ch]], writes=[b_pg[1]])
                        MM(psm[:, 0:1], St, onesb[:, 0:1], True, True, reads=[b_St, b_cst], writes=[b_pg[0]], sig=False)
                        for dc in range(2):
                            MM(pg[2], QT[dc][:, csl], Cbf[:, hd * 2 + dc, :], dc == 0, dc == 1,
                               reads=[b_QT[dc], bC], writes=[b_pg[2]])
                        for dc in range(2):
                            MM(psm[:, 1:2], QT[dc][:, csl], nbf[:, hd * 2 + dc:hd * 2 + dc + 1], dc == 0, dc == 1,
                               reads=[b_QT[dc], bC], writes=[b_pg[0]])
                        DV("tensor_scalar", kw, Ktok[ch], w_col, None, ALU.mult, reads=[b_Ktok[ch], b_tokT[ch]], writes=[b_kw])
                        for dc in range(2):
                            MM(pg[3], kw[:, dc * 128:(dc + 1) * 128], Vtok[ch], True, True,
                               reads=[b_kw, b_Vtok[ch]], writes=[b_pg[3]])
                            DV("scalar_tensor_tensor", Cst[:, hd * 2 + dc, :], Cst[:, hd * 2 + dc, :], d_col, pg[3],
                               ALU.mult, ALU.add, reads=[b_pg[3], b_tokT[ch], bC], writes=[bC])
                            P.op("act", lambda e, o=Cbf[:, hd * 2 + dc, :], i_=Cst[:, hd * 2 + dc, :]: e.copy(o, i_), [bC], [bC])
                        for dc in range(2):
                            MM(psm[:, 2 + dc:3 + dc], kw[:, dc * 128:(dc + 1) * 128], onesb[:, 0:1], True, True,
                               reads=[b_kw, b_cst], writes=[b_pg[0]])
                            DV("scalar_tensor_tensor", nst[:, hd * 2 + dc:hd * 2 + dc + 1], nst[:, hd * 2 + dc:hd * 2 + dc + 1],
                               d_col, psm[:, 2 + dc:3 + dc], ALU.mult, ALU.add, reads=[b_pg[0], b_tokT[ch], bC], writes=[bC])
                        DV("tensor_copy", nbf[:, hd * 2:hd * 2 + 2], nst[:, hd * 2:hd * 2 + 2], reads=[bC], writes=[bC])
                        ACT(intra, pg[1], AF.Copy, [b_pg[1]], [b_intra])
                        DV("scalar_tensor_tensor", num, pg[2], a_col, intra, ALU.mult, ALU.add,
                           reads=[b_pg[2], b_intra, b_tokT[ch]], writes=[b_num])
                        ACT(csm[:, 0:1], psm[:, 0:1], AF.Copy, [b_pg[0]], [b_csm])
                        DV("scalar_tensor_tensor", csm[:, 1:2], psm[:, 1:2], a_col, csm[:, 0:1], ALU.mult, ALU.add,
                           reads=[b_pg[0], b_csm, b_tokT[ch]], writes=[b_csm])
                        ACT(csm[:, 1:2], csm[:, 1:2], AF.Abs, [b_csm], [b_csm])
                        DV("tensor_tensor", csm[:, 1:2], csm[:, 1:2], e_col, ALU.max, reads=[b_csm, b_tokT[ch]], writes=[b_csm])
                        DV("reciprocal", csm[:, 2:3], csm[:, 1:2], reads=[b_csm], writes=[b_csm])
                        ACT(junkC, num, AF.Square, [b_num, b_csm], [b_junk, b_csm], scale=csm[:, 2:3], accum_out=csm[:, 3:4])
                        DV("tensor_scalar", csm[:, 6:7], csm[:, 3:4], 1.0 / 512, EPS, ALU.mult, ALU.add, reads=[b_csm], writes=[b_csm])
                        ACT(csm[:, 4:5], csm[:, 6:7], AF.Sqrt, [b_csm], [b_csm]); DV("reciprocal", csm[:, 4:5], csm[:, 4:5], reads=[b_csm], writes=[b_csm])
                        DV("tensor_tensor", csm[:, 5:6], csm[:, 2:3], csm[:, 4:5], ALU.mult, reads=[b_csm], writes=[b_csm])
                        DV("tensor_scalar", hn, num, csm[:, 5:6], None, ALU.mult, reads=[b_num, b_csm], writes=[b_hn])
                        def tail_fn(ch=ch, csl=csl, hn=hn, b_hn=b_hn):
                            for vc in range(4):
                                hb = tcount[0] % 2
                                tcount[0] += 1
                                pt = psT[hb][:, 0:128]
                                TR(pt, hn[:, vc * 128:(vc + 1) * 128], identb, reads=[b_hn, b_cst], writes=[b_psT[hb]])
                                DV("tensor_scalar", yT[2][:, hd * 4 + vc, csl], pt, mng[:, hd * 4 + vc:hd * 4 + vc + 1], None,
                                   ALU.mult, reads=[b_psT[hb], b_mng], writes=[b_yT[2][hd * 4 + vc]])
                        if pending_tail[0] is not None:
                            pending_tail[0]()
                        pending_tail[0] = tail_fn
                    if pending_tail[0] is not None:
                        pending_tail[0]()
                        pending_tail[0] = None
                    for (off, func) in ((OFF["c_o"], AF.Sigmoid), (OFF["c_z"], AF.Silu)):
                        for vb in range(2):
                            view, wb, _ = wload([wcols(off + hd * 512 + vb * 256, 256)], D, 256)

                            def cons_oz(c, ps, bps, vb=vb, func=func):
                                cc = hd * 4 + vb * 2 + c
                                ACT(sgc[c], ps, func, [bps], [b_sgc[c]])
                                DV("tensor_tensor", yT[2][:, cc, :], yT[2][:, cc, :], sgc[c], ALU.mult,
                                   reads=[b_yT[2][cc], b_sgc[c]], writes=[b_yT[2][cc]])
                            gemm_F(view, wb, 32, 256, hT_fn, [b_hT], cons_oz)
                if s < n_slots - 1:
                    DMA("sp", cfA_in[:, :].rearrange("(p j) n -> p j n", j=8), Cst, b_C, [b_cfin])
                    DMA("sp", cfB_in[0:128, 0:8], nst, b_C, [b_cfin])
                    DMA("sp", cfB_in[0:4, 8:9], mst[0:4, 0:1], [b_m], [b_cfin], slow=True)
                    if use_cc:
                        P.coll(lambda e: e.collective_compute("AllGather", ALU.bypass, replica_groups=GROUPS,
                                                              ins=[cfA_in_t.ap().opt()], outs=[cfA_out_t.ap().opt()]),
                               [b_cfin], [b_cfout])
                        P.coll(lambda e: e.collective_compute("AllGather", ALU.bypass, replica_groups=GROUPS,
                                                              ins=[cfB_in_t.ap().opt()], outs=[cfB_out_t.ap().opt()]),
                               [b_cfin], [b_cfout])
                    else:
                        DMA("sp", cfA_out[0:1024, :], cfA_in[:, :], [b_cfin], [b_cfout])
                        DMA("sp", cfA_out[1024:2048, :], cfA_in[:, :], [b_cfin], [b_cfout])
                        DMA("sp", cfB_out[0:256, :], cfB_in[:, :], [b_cfin], [b_cfout])
                        DMA("sp", cfB_out[256:512, :], cfB_in[:, :], [b_cfin], [b_cfout])
                P.barrier()
                if debug and s == 0:
                    top[0] = mark_b + 4096
                    dtmp = alloc(512)
                    b_dtmp = Buf("dtmp")
                    for j in range(3):
                        for cc in range(16):
                            DV("tensor_copy", dtmp, yT[j][:, cc, :], reads=[b_yT[j][cc]], writes=[b_dtmp])
                            DMA("sp", dbg_d[j, cc * 128:(cc + 1) * 128, :], dtmp, [b_dtmp], [b_dbg])
                    P.barrier()
                top[0] = mark_b + 4096

                sg = [[alloc(512) for _ in range(2)] for _ in range(2)]
                b_sg = [[Buf("sg%d%d" % (a, c)) for c in range(2)] for a in range(2)]
                acc = [alloc(512) for _ in range(2)]
                b_acc = [Buf("acc0"), Buf("acc1")]
                mtmp = [alloc(512) for _ in range(2)]
                b_mtmp = [Buf("mtmp0"), Buf("mtmp1")]
                mo = [alloc(256, BF16) for _ in range(2)]
                b_mo = [Buf("mo0"), Buf("mo1")]
                for db in range(16):
                    for j in range(3):
                        view, wb, _ = wload([wcols(OFF["g"] + j * D + db * 256, 256)], D, 256)

                        def cons_g(c, ps, bps, j=j):
                            ACT(sg[j % 2][c], ps, AF.Sigmoid, [bps], [b_sg[j % 2][c]])
                        gemm_F(view, wb, 32, 256, hT_fn, [b_hT], cons_g)
                        view, wb, _ = wload([w_branch[l, j][:, db * 256:(db + 1) * 256]], 2048, 256)

                        def cons_br(c, ps, bps, j=j, db=db):
                            s_ = sg[j % 2][c]
                            bs_ = b_sg[j % 2][c]
                            if j == 0:
                                DV("tensor_tensor", acc[c], ps, s_, ALU.mult, reads=[bps, bs_], writes=[b_acc[c]])
                            elif j == 1:
                                DV("tensor_tensor", mtmp[c], ps, s_, ALU.mult, reads=[bps, bs_], writes=[b_mtmp[c]])
                                DV("tensor_tensor", acc[c], acc[c], mtmp[c], ALU.add, reads=[b_acc[c], b_mtmp[c]],
                                   writes=[b_acc[c]])
                            else:
                                DV("tensor_tensor", mtmp[c], ps, s_, ALU.mult, reads=[bps, bs_], writes=[b_mtmp[c]])
                                DV("tensor_tensor", mo[c], acc[c], mtmp[c], ALU.add, reads=[b_acc[c], b_mtmp[c]],
                                   writes=[b_mo[c]])
                                r0 = (db * 2 + c) * 128
                                DMA("sp", mT_d[r0:r0 + 128, :], mo[c], [b_mo[c]], [b_mTd])
                        gemm_F(view, wb, 16, 256, lambda k, j=j: yT[j][:, k, :], b_yT[j], cons_br)
                P.barrier()
                if stop == "p2":
                    raise _Stop()
                top[0] = 0

                mT = alloc(8192, BF16).rearrange("p (k t) -> p k t", t=T)
                b_mT = Buf("mT")
                gbuf = alloc(4096)
                b_gbuf = Buf("gbuf2")
                orow = alloc(2048)
                xrow2 = alloc(2048)
                b_orow, b_xrow2 = Buf("orow"), Buf("xrow2")
                xmb = alloc(1024, BF16)
                b_xmb = Buf("xmb")
                otmp = [alloc(256) for _ in range(2)]
                b_otmp = [Buf("otmp0"), Buf("otmp1")]
                junk3 = alloc(128, BF16)
                so = small[:, 176:240]
                b_so = smallbuf("so")
                st3 = small[:, 240:248]
                b_st3 = smallbuf("st3")
                DMA("sp", mT, mT_d.rearrange("(k p) t -> p k t", p=128), [b_mTd], [b_mT])
                DMA("sp", gbuf, norm_post[l:l + 1, :].broadcast_to([128, D]), [], [b_gbuf])
                oc = [0]
                for blk in range(16):
                    view, wb, _ = wload([w_out[l][:, blk * 256:(blk + 1) * 256]], D, 256)

                    def cons_o(rt, ps, bps, blk=blk):
                        oi = oc[0] % 2
                        oc[0] += 1
                        ACT(junk3[:, 0:256], ps[:, 0:256], AF.Square, [bps], [b_junk, b_so],
                            accum_out=so[:, rt * 16 + blk:rt * 16 + blk + 1])
                        ACT(otmp[oi], ps[:, 0:256], AF.Copy, [bps], [b_otmp[oi]])
                        DMA("sp", o_d[rt * 128:(rt + 1) * 128, blk * 256:(blk + 1) * 256], otmp[oi], [b_otmp[oi]], [b_od])
                    gemm_T(view, wb, 32, 256, lambda k, rt: mT[:, k, rt * 128:(rt + 1) * 128], [b_mT], cons_o)
                DV("tensor_reduce", st3[:, 0:4], so.rearrange("p (r b) -> p r b", b=16), AX.X, ALU.add,
                   reads=[b_so], writes=[b_st3])
                DV("tensor_scalar", st3[:, 4:8], st3[:, 0:4], 1.0 / D, EPS, ALU.mult, ALU.add, reads=[b_st3], writes=[b_st3])
                ACT(st3[:, 0:4], st3[:, 4:8], AF.Sqrt, [b_st3], [b_st3]); DV("reciprocal", st3[:, 0:4], st3[:, 0:4], reads=[b_st3], writes=[b_st3])
                for rt in range(4):
                    r0 = tok0 + rt * 128
                    for half in range(2):
                        hs = slice(half * 2048, (half + 1) * 2048)
                        DMA("sp", orow, o_d[rt * 128:(rt + 1) * 128, hs], [b_od], [b_orow])
                        DMA("sp", xrow2, xc_d[rt * 128:(rt + 1) * 128, hs], [b_xc], [b_xrow2])
                        DV("scalar_tensor_tensor", orow, orow, st3[:, rt:rt + 1], gbuf[:, hs], ALU.mult, ALU.mult,
                           reads=[b_orow, b_st3, b_gbuf], writes=[b_orow])
                        DV("tensor_tensor", orow, orow, xrow2, ALU.add, reads=[b_orow, b_xrow2], writes=[b_orow])
                        DMA("sp", xm_d[rt * 128:(rt + 1) * 128, hs], orow, [b_orow], [b_xmd])
                        ACT(xmb, orow, AF.Copy, [b_orow], [b_xmb])
                        transposes(lambda i: xmb[:, i * 128:(i + 1) * 128], 16,
                                   lambda i0, cnt, rt=rt, half=half: mT[:, half * 16 + i0:half * 16 + i0 + cnt, rt * 128:(rt + 1) * 128],
                                   [b_xmb], [b_mT])
                prow = alloc(256)
                b_prow = Buf("prow")
                pbf = alloc(128, BF16)
                b_pbf = Buf("pbf")
                pT = alloc(512, BF16).rearrange("p (k t) -> p k t", t=T)
                b_pT = Buf("pT")
                sgt = [alloc(256) for _ in range(2)]
                b_sgt = [Buf("sgt0"), Buf("sgt1")]
                et = [alloc(256) for _ in range(2)]
                b_et = [Buf("et0"), Buf("et1")]
                xmt = [alloc(256) for _ in range(2)]
                b_xmt = [Buf("xmt0"), Buf("xmt1")]
                se = small[:, 176:240]
                b_se = b_so
                st4 = small[:, 248:256]
                b_st4 = smallbuf("st4")
                DMA("sp", gbuf, ple_norm[l:l + 1, :].broadcast_to([128, D]), [], [b_gbuf])
                for rt in range(4):
                    r0 = tok0 + rt * 128
                    DMA("sp", prow, p_d[r0:r0 + 128, :], [], [b_prow])
                    ACT(pbf, prow, AF.Copy, [b_prow], [b_pbf])
                    transposes(lambda i: pbf[:, i * 128:(i + 1) * 128], 2,
                               lambda i0, cnt, rt=rt: pT[:, i0:i0 + cnt, rt * 128:(rt + 1) * 128], [b_pbf], [b_pT])
                pview, pwb, pslot = wload([ple_proj[l][:, :]], 256, D, pin=True)
                for blk in range(16):
                    for rt in range(4):
                        pi = gcount[0] % 2
                        gcount[0] += 1
                        for k in range(2):
                            MM(psG[pi][:, 0:256], pT[:, k, rt * 128:(rt + 1) * 128], pview[:, k, blk * 256:(blk + 1) * 256],
                               k == 0, k == 1, reads=[b_pT, pwb], writes=[b_psG[pi]])
                        ACT(junk3[:, 0:256], psG[pi][:, 0:256], AF.Square, [b_psG[pi]], [b_junk, b_se],
                            accum_out=se[:, rt * 16 + blk:rt * 16 + blk + 1])
                DV("tensor_reduce", st4[:, 0:4], se.rearrange("p (r b) -> p r b", b=16), AX.X, ALU.add,
                   reads=[b_se], writes=[b_st4])
                DV("tensor_scalar", st4[:, 4:8], st4[:, 0:4], 1.0 / D, EPS, ALU.mult, ALU.add, reads=[b_st4], writes=[b_st4])
                ACT(st4[:, 0:4], st4[:, 4:8], AF.Sqrt, [b_st4], [b_st4]); DV("reciprocal", st4[:, 0:4], st4[:, 0:4], reads=[b_st4], writes=[b_st4])
                ec = [0]
                for blk in range(16):
                    view, wb, _ = wload([ple_gate[l][:, blk * 256:(blk + 1) * 256]], D, 256)

                    def cons_pg(rt, ps, bps, blk=blk):
                        ei = ec[0] % 2
                        ec[0] += 1
                        r0 = tok0 + rt * 128
                        bsl = slice(blk * 256, (blk + 1) * 256)
                        ACT(sgt[ei], ps[:, 0:256], AF.Sigmoid, [bps], [b_sgt[ei]])
                        DMA("sp", xmt[ei], xm_d[rt * 128:(rt + 1) * 128, bsl], [b_xmd], [b_xmt[ei]])
                        for k in range(2):
                            MM(pg[0][:, 0:256], pT[:, k, rt * 128:(rt + 1) * 128], pview[:, k, bsl], k == 0, k == 1,
                               reads=[b_pT, pwb], writes=[b_pg[0]])
                        DV("scalar_tensor_tensor", et[ei], pg[0][:, 0:256], st4[:, rt:rt + 1], gbuf[:, bsl], ALU.mult, ALU.mult,
                           reads=[b_pg[0], b_st4, b_gbuf], writes=[b_et[ei]])
                        DV("tensor_tensor", et[ei], et[ei], sgt[ei], ALU.mult, reads=[b_et[ei], b_sgt[ei]], writes=[b_et[ei]])
                        DV("tensor_tensor", et[ei], et[ei], xmt[ei], ALU.add, reads=[b_et[ei], b_xmt[ei]], writes=[b_et[ei]])
                        DMA("sp", x_out[r0:r0 + 128, bsl], et[ei], [b_et[ei]], [b_xout])
                    gemm_T(view, wb, 32, 256, lambda k, rt: mT[:, k, rt * 128:(rt + 1) * 128], [b_mT], cons_pg)
                ring["pinned"].discard(pslot)
        try:
            _layers()
        except _Stop:
            pass
        P.barrier()
        outs = [b_outs]
        if debug:
            outs.append(b_dbg)
        P.wait_all("sp", outs)
        P.build(st)
    return nc


_CACHE = {}


def _prep_weights(inputs):
    cbf, cff = make_consts()
    w = {}
    for k in ("norm_pre", "w_in", "gmlp_ln_g", "gmlp_ln_b", "attn_sinks", "mlstm_ib", "mlstm_fb",
              "mlstm_norm_g", "w_branch", "w_out", "norm_post", "ple_proj", "ple_norm", "ple_gate"):
        w[k] = np.ascontiguousarray(np.asarray(inputs[k], dtype=np.float32))
    w["gmlp_wsT"] = np.ascontiguousarray(np.transpose(np.asarray(inputs["gmlp_ws"], np.float32), (0, 1, 3, 2)))
    w["gmlp_bs"] = np.ascontiguousarray(np.asarray(inputs["gmlp_bs"], np.float32).reshape(2, 1024))
    w["cst_bf"] = cbf
    w["cst_f"] = cff
    return w


def _swap_layers(w):
    o = {}
    for k, v in w.items():
        if k in ("cst_bf", "cst_f"):
            o[k] = v
        else:
            o[k] = np.ascontiguousarray(v[::-1])
    return o


def _slot_task(s, r):
    k = s - r
    if 0 <= k <= 3:
        return k % 2, 2 * (k // 2) + r
    return None


def kernel(**inputs):
    x = np.asarray(inputs["x"], np.float32)
    p = np.asarray(inputs["p"], np.float32)
    pos = np.asarray(inputs["positions"], np.int32)
    B, S, _ = x.shape
    assert S == 4 * T and B == 4
    if "nc" not in _CACHE:
        _CACHE["nc"] = build_program()
    nc = _CACHE["nc"]
    w_even = _prep_weights(inputs)
    w_odd = _swap_layers(w_even)
    in_maps = []
    for c in range(2 * B):
        b, r = c // 2, c % 2
        m = dict(w_even if r == 0 else w_odd)
        xs = np.zeros((NS * T, D), np.float32)
        ps = np.zeros((NS * T, 256), np.float32)
        po = np.zeros((1, NS * T), np.int32)
        fl = np.zeros((1, 32), np.float32)
        for s in range(NS):
            task = _slot_task(s, r)
            if task is None:
                continue
            layer, t = task
            sl = slice(s * T, (s + 1) * T)
            tl = slice(t * T, (t + 1) * T)
            if layer == 0:
                xs[sl] = x[b, tl]
            else:
                fl[0, s * 4 + 0] = 1.0
            ps[sl] = p[layer, b, tl]
            po[0, sl] = pos[b, tl]
            if t >= 1:
                fl[0, s * 4 + 3] = 1.0
                fl[0, s * 4 + 1 + (1 - r)] = 1.0
        m["xs"], m["ps"], m["pos"], m["flags"] = xs, ps, po, fl
        in_maps.append(m)
    res = run_bass_kernel_spmd(nc, in_maps, core_ids=list(range(2 * B)))
    out = np.zeros((B, S, D), np.float32)
    for c in range(2 * B):
        b, r = c // 2, c % 2
        o = np.asarray(res.results[c]["outs"], np.float32)
        for s in range(NS):
            task = _slot_task(s, r)
            if task is not None and task[0] == 1:
                t = task[1]
                out[b, t * T:(t + 1) * T] = o[s * T:(s + 1) * T]
    return out
```
